# Optimizing a Trainium2 kernel written in Bass

```python
import math
import jax
import jax.numpy as jnp
from jax import lax
import numpy as np

D_MODEL = 1024
BATCH = 32
SEQ = 2048
DEPTH = 2

CTX_LEN = 256
GRID_W = 64

GLA_HEADS = 6
GLA_DK = 32
GLA_DV = 64
GLA_GATE_RANK = 16
GLA_TAU = 16.0
GLA_CHUNK = 64
NAT_HEADS = 4
NAT_DH = 64
NAT_WIN_R = 8
NAT_WIN_C = 16
RW_HEADS = 6
RW_DH = 64
RW_DECAY_RANK = 64
RW_A_RANK = 64
RW_GATE_RANK = 128
RW_GN_EPS = 64e-5
RW_DECAY_SCALE = math.exp(-0.5)

ROPE_BASE = 10000.0
LN_EPS = 1e-5

GLA_QK = GLA_HEADS * GLA_DK
GLA_V = GLA_HEADS * GLA_DV
NAT_W = NAT_HEADS * NAT_DH
RW_W = RW_HEADS * RW_DH
MIX_WIDTH = GLA_V + NAT_W + RW_W
FFN_HIDDEN = ((8 * D_MODEL + 3 * 256 - 1) // (3 * 256)) * 256

GLA_SIZES = (GLA_QK, GLA_QK, GLA_V, GLA_V, 2 * GLA_GATE_RANK)
NAT_SIZES = (NAT_W, NAT_W, NAT_W)
RW_SIZES = (RW_W, RW_W, RW_W, 2 * RW_DECAY_RANK, 2 * RW_A_RANK, RW_GATE_RANK)
GLA_COLS = 2 * GLA_QK + 2 * GLA_V + 2 * GLA_GATE_RANK
NAT_COLS = 3 * NAT_W
RW_COLS = 3 * RW_W + 2 * RW_DECAY_RANK + 2 * RW_A_RANK + RW_GATE_RANK
IN_COLS = GLA_COLS + NAT_COLS + RW_COLS

kernel_name = 'hybrid_gla_natten_rwkv7_dit_block'


def split_cols(t, sizes):
    return jnp.split(t, [int(s) for s in np.cumsum(sizes)[:-1]], axis=-1)


def layer_norm(x, w, b):
    xf = x.astype(jnp.float32)
    mu = jnp.mean(xf, axis=-1, keepdims=True)
    var = jnp.mean(jnp.square(xf - mu), axis=-1, keepdims=True)
    return (xf - mu) * lax.rsqrt(var + LN_EPS) * w.astype(jnp.float32) + b.astype(jnp.float32)


def rms_norm(x, w):
    xf = x.astype(jnp.float32)
    return xf * lax.rsqrt(jnp.mean(jnp.square(xf), axis=-1, keepdims=True) + LN_EPS) * w.astype(jnp.float32)


def group_norm_heads(x, w, b):
    xf = x.astype(jnp.float32)
    mu = jnp.mean(xf, axis=-1, keepdims=True)
    var = jnp.mean(jnp.square(xf - mu), axis=-1, keepdims=True)
    y = (xf - mu) * lax.rsqrt(var + RW_GN_EPS)
    return y * w.reshape(RW_HEADS, RW_DH).astype(jnp.float32) + b.reshape(RW_HEADS, RW_DH).astype(jnp.float32)


def ada_modulation(cvec, w_mod, b_mod):
    m = jax.nn.silu(cvec) @ w_mod + b_mod
    return jnp.split(m, 6, axis=-1)


def axial_rope_tables(n_tok, dim):
    t = jnp.arange(n_tok, dtype=jnp.int32)
    row = (t // GRID_W).astype(jnp.float32)
    col = (t % GRID_W).astype(jnp.float32)
    n_freq = dim // 4
    inv = ROPE_BASE ** (-jnp.arange(n_freq, dtype=jnp.float32) / n_freq)
    ang = jnp.concatenate([row[:, None] * inv, col[:, None] * inv], axis=-1)
    return jnp.cos(ang), jnp.sin(ang)


def apply_rope(x, cos, sin):
    xf = x.astype(jnp.float32)
    x1, x2 = xf[..., 0::2], xf[..., 1::2]
    c, s = cos[:, None, :], sin[:, None, :]
    return jnp.stack([x1 * c - x2 * s, x1 * s + x2 * c], axis=-1).reshape(x.shape)


def gla_chunked(q, k, v, logd, s0, with_out=True):
    B, H, L, DK = q.shape
    DV = v.shape[-1]
    C = GLA_CHUNK
    N = L // C
    q, k, v = (t.astype(jnp.float32).reshape(B, H, N, C, t.shape[-1]) for t in (q, k, v))
    b = jnp.cumsum(logd.astype(jnp.float32).reshape(B, H, N, C, DK), axis=3)
    b_last = b[:, :, :, -1]
    chunk_kv = jnp.einsum('bhnck,bhncv->bhnkv', k * jnp.exp(b_last[:, :, :, None] - b), v)

    def step(S, inp):
        dec, kv = inp
        return dec[..., None] * S + kv, S

    s_final, s_start = lax.scan(step, s0, (jnp.moveaxis(jnp.exp(b_last), 2, 0), jnp.moveaxis(chunk_kv, 2, 0)))
    if not with_out:
        return None, s_final
    q_e = q * jnp.exp(b)
    k_e = k * jnp.exp(-b)
    lower = jnp.tril(jnp.ones((C, C), dtype=bool))
    att = jnp.where(lower, jnp.einsum('bhnik,bhnjk->bhnij', q_e, k_e), 0.0)
    o = (jnp.einsum('bhnij,bhnjv->bhniv', att, v)
         + jnp.einsum('bhnik,bhnkv->bhniv', q_e, jnp.moveaxis(s_start, 0, 2)))
    return o.reshape(B, H, L, DV), s_final


def gla_two_way(q, k, v, logd2, s0f, s0b, with_out=True):
    of, sf = gla_chunked(q, k, v, logd2[0], s0f, with_out)
    rev = lambda t: jnp.flip(t, axis=2)
    ob, sb = gla_chunked(rev(q), rev(k), rev(v), rev(logd2[1]), s0b, with_out)
    o = of + rev(ob) if with_out else None
    return o, sf, sb


def gla_group(parts_l, parts_c, gate_up, gate_b, norm_w, rope, ctx_out):
    def prep(parts, rope_tab):
        q, k, v, g, dn = parts
        B, L, _ = q.shape
        q = q.reshape(B, L, GLA_HEADS, GLA_DK)
        k = k.reshape(B, L, GLA_HEADS, GLA_DK)
        if rope_tab is not None:
            q = apply_rope(q, *rope_tab)
            k = apply_rope(k, *rope_tab)
        z = jnp.einsum('blgr,grk->gblk', dn.reshape(B, L, 2, GLA_GATE_RANK), gate_up) + gate_b[:, None, None, :]
        logd = jax.nn.log_sigmoid(z.astype(jnp.float32)) / GLA_TAU
        logd = logd.reshape(2, B, L, GLA_HEADS, GLA_DK).transpose(0, 1, 3, 2, 4)
        bhld = lambda t: jnp.swapaxes(t, 1, 2)
        return (bhld(q) * GLA_DK ** -0.5, bhld(k), bhld(v.reshape(B, L, GLA_HEADS, GLA_DV)), logd, g)

    def readout(o, g):
        o = jnp.swapaxes(o, 1, 2)
        B, L = o.shape[:2]
        gate = jax.nn.silu(g.reshape(B, L, GLA_HEADS, GLA_DV).astype(jnp.float32))
        return (rms_norm(o, norm_w) * gate).reshape(B, L, GLA_V)

    qc, kc, vc, dc, gc = prep(parts_c, None)
    s0 = jnp.zeros((qc.shape[0], GLA_HEADS, GLA_DK, GLA_DV), jnp.float32)
    oc, sf, sb = gla_two_way(qc, kc, vc, dc, s0, s0, ctx_out)
    ql, kl, vl, dl, gl = prep(parts_l, rope)
    ol, _, _ = gla_two_way(ql, kl, vl, dl, sf, sb)
    return readout(ol, gl), (readout(oc, gc) if ctx_out else None)


def nat_group(parts_l, parts_c, rpb, ctx_out):
    ql, kl, vl = (t.reshape(t.shape[0], t.shape[1], NAT_HEADS, NAT_DH) for t in parts_l)
    qc, kc, vc = (t.reshape(t.shape[0], t.shape[1], NAT_HEADS, NAT_DH) for t in parts_c)
    B, L = ql.shape[:2]
    R = L // GRID_W
    KR = min(NAT_WIN_R, R)
    KC = NAT_WIN_C
    scale = NAT_DH ** -0.5
    r_idx = jnp.arange(R)
    rows = jnp.clip(r_idx - KR // 2, 0, R - KR)[:, None] + jnp.arange(KR)[None, :]
    cq = jnp.arange(GRID_W)
    col0 = jnp.clip(cq - KC // 2, 0, GRID_W - KC)
    in_win = (cq[None, :] >= col0[:, None]) & (cq[None, :] < col0[:, None] + KC)
    dr = rows - r_idx[:, None] + NAT_WIN_R - 1
    dc = jnp.clip(cq[None, :] - cq[:, None], -(KC - 1), KC - 1) + KC - 1
    bias = rpb[:, dr][:, :, :, dc].transpose(0, 1, 3, 2, 4)

    qg = ql.reshape(B, R, GRID_W, NAT_HEADS, NAT_DH)
    kg = kl.reshape(B, R, GRID_W, NAT_HEADS, NAT_DH)[:, rows]
    vg = vl.reshape(B, R, GRID_W, NAT_HEADS, NAT_DH)[:, rows]
    s_loc = jnp.einsum('brqhd,brkwhd->bhrqkw', qg, kg).astype(jnp.float32) * scale + bias
    s_loc = jnp.where(in_win[:, None, :], s_loc, -jnp.inf)
    s_ctx = jnp.einsum('brqhd,bchd->bhrqc', qg, kc).astype(jnp.float32) * scale
    n_loc = KR * GRID_W
    s = jnp.concatenate([s_loc.reshape(B, NAT_HEADS, R, GRID_W, n_loc), s_ctx], axis=-1)
    p = jax.nn.softmax(s, axis=-1)
    p_loc = p[..., :n_loc].reshape(B, NAT_HEADS, R, GRID_W, KR, GRID_W)
    o = (jnp.einsum('bhrqkw,brkwhd->brqhd', p_loc, vg.astype(jnp.float32))
         + jnp.einsum('bhrqc,bchd->brqhd', p[..., n_loc:], vc.astype(jnp.float32)))
    out_l = o.reshape(B, L, NAT_W)
    if not ctx_out:
        return out_l, None
    sc = jnp.einsum('bihd,bjhd->bhij', qc, kc).astype(jnp.float32) * scale
    oc = jnp.einsum('bhij,bjhd->bihd', jax.nn.softmax(sc, axis=-1), vc.astype(jnp.float32))
    return out_l, oc.reshape(B, qc.shape[1], NAT_W)


def centred_shift(y, mu):
    pad = jnp.pad(y, ((0, 0), (1, 1), (0, 0)))
    return y + (0.5 * (pad[:, :-2] + pad[:, 2:]) - y) * mu


def rwkv7_scan(r, w, k, v, kk, a, s0):
    def step(S, inp):
        r_t, w_t, k_t, v_t, kk_t, a_t = inp
        sa = jnp.einsum('bhvk,bhk->bhv', S, -kk_t)
        S = (S * w_t[:, :, None, :] + sa[..., None] * (kk_t * a_t)[:, :, None, :]
             + v_t[..., None] * k_t[:, :, None, :])
        return S, jnp.einsum('bhvk,bhk->bhv', S, r_t)

    s_final, o = lax.scan(step, s0, tuple(jnp.moveaxis(t, 1, 0) for t in (r, w, k, v, kk, a)))
    return jnp.moveaxis(o, 0, 1), s_final


def rwkv_group(P_l, P_c, mu, w0, wd2, a0, wa2, wg2, k_k, k_a, r_k, gn_w, gn_b, ctx_out):
    def heads(t):
        return t.reshape(t.shape[:-1] + (RW_HEADS, RW_DH)).astype(jnp.float32)

    def prep(P):
        y = centred_shift(P, mu)
        r, k, v, dd, ad, gd = split_cols(y, RW_SIZES)
        B, L, _ = r.shape
        d = w0[:, None, None] + jnp.einsum('blgr,grc->gblc', jnp.tanh(dd.reshape(B, L, 2, RW_DECAY_RANK)), wd2)
        w = jnp.exp(-RW_DECAY_SCALE * jax.nn.sigmoid(d.astype(jnp.float32)))
        a = jax.nn.sigmoid((a0[:, None, None] + jnp.einsum('blgr,grc->gblc', ad.reshape(B, L, 2, RW_A_RANK), wa2)).astype(jnp.float32))
        g = jax.nn.sigmoid(gd) @ wg2
        kk = heads(k * k_k)
        kk = kk * lax.rsqrt(jnp.sum(kk * kk, axis=-1, keepdims=True) + 1e-12)
        kmod = k[None].astype(jnp.float32) * (1.0 + (a - 1.0) * k_a)
        return heads(r), heads(w), heads(kmod), heads(v), kk, heads(a), g

    def two_way(st, s0f, s0b):
        r, w, k, v, kk, a, _ = st
        of, sf = rwkv7_scan(r, w[0], k[0], v, kk, a[0], s0f)
        rev = lambda t: jnp.flip(t, axis=1)
        ob, sb = rwkv7_scan(rev(r), rev(w[1]), rev(k[1]), rev(v), rev(kk), rev(a[1]), s0b)
        return of + rev(ob), sf, sb

    def readout(o, st):
        r, _, k, v, _, _, g = st
        B, L = o.shape[:2]
        rk = r_k.reshape(RW_HEADS, RW_DH).astype(jnp.float32)
        bonus = jnp.sum(r[None] * k * rk, axis=-1, keepdims=True).sum(0) * v
        return (group_norm_heads(o, gn_w, gn_b) + bonus).reshape(B, L, RW_W) * g

    st_c = prep(P_c)
    s0 = jnp.zeros((P_c.shape[0], RW_HEADS, RW_DH, RW_DH), jnp.float32)
    oc, sf, sb = two_way(st_c, s0, s0)
    st_l = prep(P_l)
    ol, _, _ = two_way(st_l, sf, sb)
    return readout(ol, st_l), (readout(oc, st_c) if ctx_out else None)


def swiglu(h, w13, w2):
    gte, up = jnp.split(h @ w13, 2, axis=-1)
    return (jax.nn.silu(gte) * up) @ w2


def token_mixers(P, Pc, p, rope, ctx_out):
    gl_, nl_, rl_ = split_cols(P, (GLA_COLS, NAT_COLS, RW_COLS))
    gc_, nc_, rc_ = split_cols(Pc, (GLA_COLS, NAT_COLS, RW_COLS))
    a_l, a_c = gla_group(split_cols(gl_, GLA_SIZES), split_cols(gc_, GLA_SIZES),
                         p['gla_gate_up'], p['gla_gate_b'], p['gla_norm_w'], rope, ctx_out)
    b_l, b_c = nat_group(split_cols(nl_, NAT_SIZES), split_cols(nc_, NAT_SIZES), p['nat_rpb'], ctx_out)
    c_l, c_c = rwkv_group(rl_, rc_, p['rw_mu'], p['rw_w0'], p['rw_wd2'], p['rw_a0'], p['rw_wa2'], p['rw_wg2'],
                          p['rw_k_k'], p['rw_k_a'], p['rw_r_k'], p['rw_gn_w'], p['rw_gn_b'], ctx_out)
    y_l = jnp.concatenate([a_l, b_l, c_l], axis=-1)
    y_c = jnp.concatenate([a_c, b_c, c_c], axis=-1) if ctx_out else None
    return y_l, y_c


def hybrid_layer(xl, xc, c, c_ctx, p, rope, last):
    dt = xl.dtype
    alpha = (2.0 * DEPTH) ** 0.25
    ctx_out = not last
    ml = [m[:, None, :] for m in ada_modulation(c, p['w_mod'], p['b_mod'])]
    mc = ada_modulation(c_ctx, p['w_mod'], p['b_mod'])
    P = (xl * (1.0 + ml[1]) + ml[0]) @ p['w_in']
    Pc = (xc * (1.0 + mc[1]) + mc[0]) @ p['w_in']
    y_l, y_c = token_mixers(P, Pc, p, rope, ctx_out)
    xl = layer_norm(alpha * xl + ml[2] * (y_l.astype(dt) @ p['w_out']), p['ln1_w'], p['ln1_b']).astype(dt)
    f_l = swiglu(xl * (1.0 + ml[4]) + ml[3], p['ffn_w13'], p['ffn_w2'])
    xl = layer_norm(alpha * xl + ml[5] * f_l, p['ln2_w'], p['ln2_b']).astype(dt)
    if ctx_out:
        xc = layer_norm(alpha * xc + mc[2] * (y_c.astype(dt) @ p['w_out']), p['ln1_w'], p['ln1_b']).astype(dt)
        f_c = swiglu(xc * (1.0 + mc[4]) + mc[3], p['ffn_w13'], p['ffn_w2'])
        xc = layer_norm(alpha * xc + mc[5] * f_c, p['ln2_w'], p['ln2_b']).astype(dt)
    return xl, xc


def setup_inputs(seed: int = 0) -> dict:
    key = jax.random.key(seed)
    ks = jax.random.split(key, 40)
    f32 = jnp.float32
    D = D_MODEL
    beta = (8.0 * DEPTH) ** -0.25

    def nrm(k, shape, s):
        return jax.random.normal(k, shape, f32) * s

    return {
        'x': nrm(ks[0], (BATCH, SEQ, D), 1.0),
        'c': nrm(ks[1], (BATCH, D), 1.0),
        'ctx': nrm(ks[2], (BATCH, CTX_LEN, D), 1.0),
        'c_ctx': nrm(ks[3], (D,), 1.0),
        'w_mod': nrm(ks[4], (DEPTH, D, 6 * D), 0.5 * D ** -0.5),
        'b_mod': nrm(ks[5], (DEPTH, 6 * D), 0.02),
        'w_in': nrm(ks[6], (DEPTH, D, IN_COLS), D ** -0.5),
        'gla_gate_up': nrm(ks[7], (DEPTH, 2, GLA_GATE_RANK, GLA_QK), GLA_GATE_RANK ** -0.5),
        'gla_gate_b': nrm(ks[8], (DEPTH, 2, GLA_QK), 0.5),
        'gla_norm_w': 1.0 + nrm(ks[9], (DEPTH, GLA_DV), 0.02),
        'nat_rpb': nrm(ks[10], (DEPTH, NAT_HEADS, 2 * NAT_WIN_R - 1, 2 * NAT_WIN_C - 1), 0.02),
        'rw_mu': jax.random.uniform(ks[11], (DEPTH, RW_COLS), f32),
        'rw_w0': jax.random.uniform(ks[12], (DEPTH, 2, RW_W), f32, -3.0, 1.0),
        'rw_wd2': nrm(ks[13], (DEPTH, 2, RW_DECAY_RANK, RW_W), 0.5 * RW_DECAY_RANK ** -0.5),
        'rw_a0': nrm(ks[14], (DEPTH, 2, RW_W), 0.1),
        'rw_wa2': nrm(ks[15], (DEPTH, 2, RW_A_RANK, RW_W), 0.5 * RW_A_RANK ** -0.5),
        'rw_wg2': nrm(ks[16], (DEPTH, RW_GATE_RANK, RW_W), RW_GATE_RANK ** -0.5),
        'rw_k_k': 0.85 + nrm(ks[17], (DEPTH, RW_W), 0.02),
        'rw_k_a': 1.0 + nrm(ks[18], (DEPTH, RW_W), 0.02),
        'rw_r_k': nrm(ks[19], (DEPTH, RW_W), 0.1),
        'rw_gn_w': 1.0 + nrm(ks[20], (DEPTH, RW_W), 0.02),
        'rw_gn_b': nrm(ks[21], (DEPTH, RW_W), 0.02),
        'w_out': nrm(ks[22], (DEPTH, MIX_WIDTH, D), beta * MIX_WIDTH ** -0.5),
        'ln1_w': 1.0 + nrm(ks[23], (DEPTH, D), 0.02),
        'ln1_b': nrm(ks[24], (DEPTH, D), 0.02),
        'ffn_w13': nrm(ks[25], (DEPTH, D, 2 * FFN_HIDDEN), D ** -0.5),
        'ffn_w2': nrm(ks[26], (DEPTH, FFN_HIDDEN, D), beta * FFN_HIDDEN ** -0.5),
        'ln2_w': 1.0 + nrm(ks[27], (DEPTH, D), 0.02),
        'ln2_b': nrm(ks[28], (DEPTH, D), 0.02),
    }


def reference(x, c, ctx, c_ctx, w_mod, b_mod, w_in, gla_gate_up, gla_gate_b, gla_norm_w, nat_rpb,
              rw_mu, rw_w0, rw_wd2, rw_a0, rw_wa2, rw_wg2, rw_k_k, rw_k_a, rw_r_k, rw_gn_w, rw_gn_b,
              w_out, ln1_w, ln1_b, ffn_w13, ffn_w2, ln2_w, ln2_b):
    rope = axial_rope_tables(x.shape[1], GLA_DK)
    xl, xc = x, ctx
    for i in range(DEPTH):
        p = dict(w_mod=w_mod[i], b_mod=b_mod[i], w_in=w_in[i], gla_gate_up=gla_gate_up[i], gla_gate_b=gla_gate_b[i],
                 gla_norm_w=gla_norm_w[i], nat_rpb=nat_rpb[i], rw_mu=rw_mu[i], rw_w0=rw_w0[i], rw_wd2=rw_wd2[i],
                 rw_a0=rw_a0[i], rw_wa2=rw_wa2[i], rw_wg2=rw_wg2[i], rw_k_k=rw_k_k[i], rw_k_a=rw_k_a[i],
                 rw_r_k=rw_r_k[i], rw_gn_w=rw_gn_w[i], rw_gn_b=rw_gn_b[i], w_out=w_out[i], ln1_w=ln1_w[i],
                 ln1_b=ln1_b[i], ffn_w13=ffn_w13[i], ffn_w2=ffn_w2[i], ln2_w=ln2_w[i], ln2_b=ln2_b[i])
        xl, xc = hybrid_layer(xl, xc, c, c_ctx, p, rope, i == DEPTH - 1)
    return xl
```

```python
import contextlib
import numpy as np
import concourse.bass as bass
import concourse.mybir as mybir
from concourse.bass_utils import run_bass_kernel_spmd

F32 = mybir.dt.float32
BF16 = mybir.dt.bfloat16
AF = mybir.ActivationFunctionType
ALU = mybir.AluOpType
AX = mybir.AxisListType

ENGS = ('pe', 'dve', 'act', 'pool', 'sp')


class Buf:
    def __init__(self, t, name):
        self.t = t
        self.name = name
        self.last_write = None
        self.reads = {}


class Prog:
    SAME_ENGINE_SYNC = True

    def __init__(self, nc, n_dma_sems=12):
        self.nc = nc
        self.stack = contextlib.ExitStack()
        self.ops = {e: [] for e in ENGS}
        self.sem = {}
        for e in ('pe', 'dve', 'act', 'pool'):
            self.sem[e] = self.stack.enter_context(nc.semaphore("s_" + e))
        self.cnt = {e: 0 for e in ('pe', 'dve', 'act', 'pool')}
        self.seen = {e: {} for e in ENGS}
        self.dsems = {}
        self.dcur = {}
        for q in ('sp', 'pool', 'act'):
            self.dsems[q] = []
            for i in range(n_dma_sems):
                key = "d_%s_%d" % (q, i)
                self.sem[key] = self.stack.enter_context(nc.semaphore(key))
                self.dsems[q].append([key, 0])
            self.dcur[q] = 0
        self.n_ops = 0

    _uid = 0

    def sbuf(self, name, shape, dtype):
        Prog._uid += 1
        name = "%s_u%d" % (name, Prog._uid)
        t = self.stack.enter_context(self.nc.sbuf_tensor(name, list(shape), dtype))
        return Buf(t, name)

    def psum(self, name, shape, dtype):
        t = self.stack.enter_context(self.nc.psum_tensor(name, list(shape), dtype))
        return Buf(t, name)

    def view(self, buf, name=None):
        return Buf(buf.t, name or buf.name)

    def _needs(self, reads, writes):
        need = {}

        def add(tok):
            if tok is None:
                return
            k, v = tok
            if need.get(k, 0) < v:
                need[k] = v
        for b in reads:
            add(b.last_write)
        for b in writes:
            add(b.last_write)
            for k, v in b.reads.items():
                add((k, v))
        return need

    def _waits(self, eng, need):
        waits = []
        for k, v in need.items():
            if k == eng and not (self.SAME_ENGINE_SYNC and eng != 'pe'):
                continue
            if self.seen[eng].get(k, 0) >= v:
                continue
            self.seen[eng][k] = v
            waits.append((k, v))
        return waits

    limit = None
    trace_range = None

    def op(self, eng, fn, reads=(), writes=()):
        if self.limit is not None and self.n_ops >= self.limit:
            return
        if self.trace_range and self.trace_range[0] <= self.n_ops < self.trace_range[1]:
            import inspect
            fr = inspect.stack()
            print("OP", self.n_ops, eng, [f.lineno for f in fr[1:4]])
        need = self._needs(reads, writes)
        waits = self._waits(eng, need)
        self.cnt[eng] += 1
        c = self.cnt[eng]
        self.ops[eng].append((waits, fn, (eng, 1)))
        tok = (eng, c)
        for b in reads:
            b.reads[eng] = c
        for b in writes:
            b.last_write = tok
            b.reads = {}
        self.n_ops += 1

    def dma(self, q, out_ap, in_ap, reads=(), writes=(), **kw):
        if self.limit is not None and self.n_ops >= self.limit and not kw.pop("force", False):
            return
        kw.pop("force", None)
        need = self._needs(reads, writes)
        slot = self.dsems[q][self.dcur[q]]
        self.dcur[q] = (self.dcur[q] + 1) % len(self.dsems[q])
        key, val = slot
        if val > 0:
            if need.get(key, 0) < val:
                need[key] = val
        waits = self._waits(q, need)
        slot[1] = val + 16
        tok = (key, val + 16)
        self.ops[q].append((waits, lambda e: e.dma_start(out=out_ap, in_=in_ap, **kw), (key, 16)))
        for b in reads:
            b.reads[key] = val + 16
        for b in writes:
            b.last_write = tok
            b.reads = {}
        self.n_ops += 1

    def barrier(self):
        need = {e: c for e, c in self.cnt.items() if c > 0}
        for q in self.dsems:
            for key, val in self.dsems[q]:
                if val > 0:
                    need[key] = val
        for eng in ENGS:
            waits = self._waits(eng, dict(need))
            if waits:
                self.ops[eng].append((waits, None, None))

    def finish(self):
        self.barrier()
        nc = self.nc
        sem = self.sem
        ops = self.ops

        def replay(eng_name):
            def run(e):
                for waits, fn, inc in ops[eng_name]:
                    for k, v in waits:
                        e.wait_ge(sem[k], v)
                    if fn is not None:
                        ins = fn(e)
                        ins.then_inc(sem[inc[0]], inc[1])
            return run

        with nc.Block() as block:
            block.tensor(replay('pe'))
            block.vector(replay('dve'))
            block.scalar(replay('act'))
            block.gpsimd(replay('pool'))
            block.sync(replay('sp'))
        self.stack.close()


D = 1024
NCTX = 256
LAT = 2048
S = NCTX + LAT
NT = S // 128
NB = 4
DEPTH = 2
HID = 2816
NHC = HID // 128
IN_COLS = 3488
GRID_W = 64
ALPHA = (2.0 * DEPTH) ** 0.25
LN_EPS = 1e-5
RW_GN_EPS = 64e-5
DECAY = float(np.exp(-0.5))
CB = 256
NBLK = S // CB
GQ, GK, GV, GG, GDN = 0, 192, 384, 768, 1152
NQ, NK, NV = 1184, 1440, 1696
RR, RK, RV, RDD, RAD, RGD = 1952, 2336, 2720, 3104, 3232, 3360
NEG = -30000.0

PC_W0 = 0
PC_A0 = 6
PC_KK = 12
PC_KA = 15
PC_RK = 18
PC_GB = 21
PC_BMOD = 27
NPC = 75
PR_LN1W, PR_LN1B, PR_LN2W, PR_LN2B = 0, 1024, 2048, 3072
PR_GNORM = 4096
PR_GNW = 4480
PR_GNB = 4864
PR_MU = 5248
NPR = 6784
C_IDENT = 0
C_MASKF = 128
C_MASKB = 384
C_LMF = 640
C_LMB = 768
C_SCF = 896
C_SCB = 1152
C_BONES = 1408
C_BD = 1536
C_HSEL = 1664
C_ONE = 1666
C_E12 = 1667
NCST = 1668


def _host_constants():
    c = np.zeros((128, NCST), np.float32)
    i = np.arange(128)
    c[:, C_IDENT:C_IDENT + 128] = np.eye(128, dtype=np.float32)
    j, t = i[:, None], i[None, :]
    c[:, C_MASKF:C_MASKF + 128] = (t > j)
    c[:, C_MASKF + 128:C_MASKF + 256] = (t >= j)
    c[:, C_MASKB:C_MASKB + 128] = (t < j)
    c[:, C_MASKB + 128:C_MASKB + 256] = (t <= j)
    c[:, C_LMF:C_LMF + 128] = (i[None, :] < i[:, None])
    c[:, C_LMB:C_LMB + 128] = (i[None, :] > i[:, None])
    sc = np.ones((256,), np.float32); sc[0] = 0; sc[128] = 0
    c[:, C_SCF:C_SCF + 256] = sc[None]
    sc = np.ones((256,), np.float32); sc[127] = 0; sc[255] = 0
    c[:, C_SCB:C_SCB + 256] = sc[None]
    blk = (i[:, None] // 64 == i[None, :] // 64).astype(np.float32)
    c[:, C_BONES:C_BONES + 128] = blk
    c[:, C_BD:C_BD + 128] = blk
    c[:, C_HSEL] = (i < 64)
    c[:, C_HSEL + 1] = (i >= 64)
    c[:, C_ONE] = 1.0
    c[:, C_E12] = 1e-12
    return c


def _rope_tables():
    tt = np.arange(LAT)
    row = (tt // GRID_W).astype(np.float32)
    col = (tt % GRID_W).astype(np.float32)
    n_freq = 8
    inv = (10000.0 ** (-np.arange(n_freq, dtype=np.float32) / n_freq)).astype(np.float32)
    ang = np.concatenate([row[:, None] * inv, col[:, None] * inv], axis=-1).astype(np.float32)
    cos, sin = np.cos(ang).astype(np.float32), np.sin(ang).astype(np.float32)
    C = np.zeros((128, LAT), np.float32)
    Sn = np.zeros((128, LAT), np.float32)
    for e in range(2):
        for kp in range(32):
            C[e * 64 + kp] = cos[:, kp // 2]
            Sn[e * 64 + kp] = sin[:, kp // 2]
    return C, Sn


def _nat_tables(rpb):
    cq = np.arange(GRID_W)
    col0 = np.clip(cq - 8, 0, GRID_W - 16)
    in_win = (cq[None, :] >= col0[:, None]) & (cq[None, :] < col0[:, None] + 16)
    dc = np.clip(cq[None, :] - cq[:, None], -15, 15) + 15
    out = np.full((DEPTH, 4, 128, 14, 64), NEG, np.float32)
    for e in range(2):
        for idx in range(14):
            g = rpb[:, :, idx + e, :][:, :, dc]
            g = np.where(in_win[None, None], g, np.float32(NEG))
            out[:, :, e * 64:(e + 1) * 64, idx, :] = np.transpose(g, (0, 1, 3, 2))
    return out.reshape(DEPTH, 4, 128, 14 * 64)


def _vec3(v):
    return np.ascontiguousarray(v.reshape(3, 128).T)


def _pad_gla(v):
    o = np.zeros((3, 2, 64), np.float32)
    o[:, :, :32] = v.reshape(3, 2, 32)
    return np.ascontiguousarray(o.reshape(3, 128).T)


class Kern:
    def __init__(self, nb=NB, depth=DEPTH, test=None):
        self.nb = nb
        self.depth = depth
        self.test = test
        nc = self.nc = bass.Bass("TRN2", target_bir_lowering=False)
        P = self.P = Prog(nc)

        def din(name, shape, dt=F32):
            return nc.dram_tensor(name, list(shape), dt, kind="ExternalInput").ap()
        self.x = din("x", [nb, LAT, D])
        self.ctx = din("ctx", [nb, NCTX, D])
        self.cc = din("cc", [NB + 1, D])
        self.w_mod = din("w_mod", [DEPTH, D, 6 * D])
        self.w_in = din("w_in", [DEPTH, D, IN_COLS])
        self.w_out = din("w_out", [DEPTH, D, D])
        self.w13 = din("ffn_w13", [DEPTH, D, 2 * HID])
        self.w2 = din("ffn_w2", [DEPTH, HID, D])
        self.pcol = din("pcol", [DEPTH, 128, NPC])
        self.prow = din("prow", [DEPTH, 1, NPR])
        self.cst = din("cst", [128, NCST])
        self.ropeC = din("ropeC", [128, LAT])
        self.ropeS = din("ropeS", [128, LAT])
        self.nattab = din("nattab", [DEPTH, 4, 128, 14 * 64])
        self.gup = din("gup", [DEPTH, 2, 16, 384])
        self.wd2 = din("wd2", [DEPTH, 128, 384])
        self.wa2 = din("wa2", [DEPTH, 128, 384])
        self.wg2 = din("wg2", [DEPTH, 128, 384])
        self.y = nc.dram_tensor("y", [nb, LAT, D], F32, kind="ExternalOutput").ap()
        self.XA = Buf(nc.dram_tensor("XA", [S, D], F32), "XA")
        self.XB = Buf(nc.dram_tensor("XB", [S, D], F32), "XB")
        if test == 'dense':
            self.YT = Buf(nc.dram_tensor("YT", [D, S], BF16, kind="ExternalInput"), "YT")
        else:
            self.YT = Buf(nc.dram_tensor("YT", [D, S], BF16), "YT")
        self.OF = Buf(nc.dram_tensor("OFs", [NT, 128, 384], F32), "OF")
        self.dbg = {}
        self.cst_sb = P.sbuf("cst_sb", [128, NCST], F32)
        P.dma('sp', self.cst_sb.t[:], self.cst, writes=[self.cst_sb])
        self.identb = P.sbuf("identb", [128, 128], BF16)
        P.op('dve', lambda e: e.tensor_copy(self.identb.t[:], self.cst_sb.t[:, C_IDENT:C_IDENT + 128]),
             reads=[self.cst_sb], writes=[self.identb])
        self.mTs = [P.sbuf("mT%d" % l, [128, 48, 8], F32) for l in range(DEPTH)]
        self.pcols = [P.sbuf("pcol_sb%d" % l, [128, NPC + 12], F32) for l in range(DEPTH)]
        self.set_layer(0)
        self.PS = [P.psum("psb%d" % i, [128, 512], F32) for i in range(8)]

    def set_layer(self, l):
        self.mT = self.mTs[l]
        self.pcol_sb = self.pcols[l]

    def ident(self):
        return self.cst_sb.t[:, C_IDENT:C_IDENT + 128]

    def dbg_out(self, name, shape, dt=F32):
        ap = self.nc.dram_tensor(name, list(shape), dt, kind="ExternalOutput").ap()
        self.dbg[name] = ap
        return ap

    def ph_mod(self, l):
        P, nc = self.P, self.nc
        P.barrier()
        pcol_sb, mT = self.pcol_sb, self.mT
        with contextlib.ExitStack() as st:
            P.stack, old = st, P.stack
            ccT = P.sbuf("ccT", [128, 8, 8], F32)
            sc = P.sbuf("scT", [128, 8, 8], F32)
            P.dma('sp', pcol_sb.t[:, 0:NPC], self.pcol[l], writes=[pcol_sb])
            P.op('dve', lambda e: e.memset(ccT.t[:], 0.0), writes=[ccT])
            for j in range(NB + 1):
                P.dma('sp', ccT.t[:, :, j:j + 1], self.cc[j].rearrange("(k p o) -> p k o", p=128, o=1), writes=[ccT],
                      allow_slow_non_contiguous=True)
            P.op('act', lambda e: e.activation(sc.t[:], ccT.t[:], AF.Silu), reads=[ccT], writes=[sc])
            P.op('dve', lambda e: e.tensor_scalar(pcol_sb.t[:, NPC:NPC + 3], pcol_sb.t[:, PC_KA:PC_KA + 3], -1.0, 1.0,
                                                  ALU.mult, ALU.add), reads=[pcol_sb], writes=[pcol_sb])
            P.op('dve', lambda e: e.tensor_scalar(pcol_sb.t[:, NPC + 3:NPC + 9], pcol_sb.t[:, PC_GB:PC_GB + 6], -1.0, None,
                                                  ALU.mult), reads=[pcol_sb], writes=[pcol_sb])
            wm = [P.sbuf("wm%d" % i, [128, 8, 512], F32) for i in range(2)]
            ps = self.PS[0]
            for cg in range(12):
                w = wm[cg % 2]
                P.dma('sp', w.t[:], self.w_mod[l][:, cg * 512:(cg + 1) * 512].rearrange("(k p) c -> p k c", p=128), writes=[w])
                first = True
                for c in range(4):
                    for k in range(8):
                        P.op('pe', lambda e, w=w, c=c, k=k, first=first: e.matmul(
                            ps.t[:, c * 8:c * 8 + 8], w.t[:, k, c * 128:(c + 1) * 128], sc.t[:, k, :], start=first, stop=(k == 7),
                            skip_group_check=True), reads=[w, sc], writes=[ps])
                        first = False
                for c in range(4):
                    ch = cg * 4 + c
                    isscale = (8 <= ch < 16) or (32 <= ch < 40)
                    P.op('dve', lambda e, c=c, ch=ch, isscale=isscale: e.tensor_scalar(
                        mT.t[:, ch, :], ps.t[:, c * 8:c * 8 + 8], pcol_sb.t[:, PC_BMOD + ch:PC_BMOD + ch + 1],
                        1.0 if isscale else 0.0, ALU.add, ALU.add), reads=[ps, pcol_sb], writes=[mT])
            P.barrier()
            P.stack = old

    def build_xT(self, src_rows, xT, ntok, shift_ch, scale_ch, jb, segs, xt_bufs):
        P = self.P
        gi = 0
        for (tok0, n, jcol) in segs:
            nt = n // 128
            xt = xt_bufs[gi % 2]
            gi += 1
            P.dma('sp', xt.t[:, 0:nt, :], src_rows(tok0, n).rearrange("(j p) d -> p j d", p=128), reads=self._src_reads, writes=[xt])
            for dc in range(8):
                ps = self.PS[dc % 2]
                for j in range(nt):
                    P.op('pe', lambda e, ps=ps, xt=xt, j=j, dc=dc: e.transpose(
                        ps.t[:, j * 128:(j + 1) * 128], xt.t[:, j, dc * 128:(dc + 1) * 128], self.ident()),
                        reads=[xt, self.cst_sb], writes=[ps])
                eng = 'dve' if dc % 2 == 0 else 'act'
                o = xT.t[:, dc, tok0:tok0 + n]
                sc_ap = self.mT.t[:, scale_ch + dc, jcol:jcol + 1]
                sh_ap = self.mT.t[:, shift_ch + dc, jcol:jcol + 1]
                if eng == 'dve':
                    P.op('dve', lambda e, o=o, ps=ps, n=n, sc_ap=sc_ap, sh_ap=sh_ap: e.tensor_scalar(
                        o, ps.t[:, 0:n], sc_ap, sh_ap, ALU.mult, ALU.add), reads=[ps, self.mT], writes=[xT])
                else:
                    P.op('act', lambda e, o=o, ps=ps, n=n, sc_ap=sc_ap, sh_ap=sh_ap: e.activation(
                        o, ps.t[:, 0:n], AF.Identity, bias=sh_ap, scale=sc_ap), reads=[ps, self.mT], writes=[xT])

    def seq_segs(self, b, with_ctx=True):
        segs = []
        if with_ctx:
            segs.append((0, NCTX, NB))
        for g in range(4):
            segs.append((NCTX + g * 512, 512, b))
        return segs

    def src_rows_fn(self, b, l):
        if l == 0:
            def f(tok0, n):
                if tok0 < NCTX:
                    return self.ctx[b][tok0:tok0 + n, :]
                return self.x[b][tok0 - NCTX:tok0 - NCTX + n, :]
            return f, []
        XB = self.XB

        def f2(tok0, n):
            return XB.t.ap()[tok0:tok0 + n, :]
        return f2, [XB]

    def ln_tail(self, pre, rows_w, rows_b, out_tile, tmp):
        P = self.P
        st = self.ln_st
        mv = self.ln_mv
        P.op('dve', lambda e: e.tensor_reduce(mv.t[:, 5:6], pre.t[:], AX.X, ALU.add), reads=[pre], writes=[mv])
        P.op('act', lambda e: e.activation(tmp.t[:], pre.t[:], AF.Square, accum_out=st.t[:, 0:1]), reads=[pre], writes=[tmp, st])
        P.op('dve', lambda e: e.tensor_scalar(mv.t[:, 0:1], mv.t[:, 5:6], 1.0 / D, None, ALU.mult), reads=[mv], writes=[mv])
        P.op('dve', lambda e: e.tensor_tensor(mv.t[:, 1:2], mv.t[:, 0:1], mv.t[:, 0:1], op=ALU.mult), reads=[mv], writes=[mv])
        P.op('dve', lambda e: e.scalar_tensor_tensor(mv.t[:, 2:3], st.t[:, 0:1], 1.0 / D, mv.t[:, 1:2], ALU.mult, ALU.subtract),
             reads=[mv, st], writes=[mv])
        P.op('dve', lambda e: e.tensor_scalar(mv.t[:, 2:3], mv.t[:, 2:3], LN_EPS, None, ALU.add), reads=[mv], writes=[mv])
        P.op('act', lambda e: e.activation(mv.t[:, 3:4], mv.t[:, 2:3], AF.Sqrt), reads=[mv], writes=[mv])
        P.op('dve', lambda e: e.reciprocal(mv.t[:, 4:5], mv.t[:, 3:4]), reads=[mv], writes=[mv])
        P.op('dve', lambda e: e.tensor_scalar(tmp.t[:], pre.t[:], mv.t[:, 0:1], mv.t[:, 4:5], ALU.subtract, ALU.mult),
             reads=[pre, mv], writes=[tmp])
        P.op('pool', lambda e: e.tensor_tensor(tmp.t[:], tmp.t[:], rows_w, op=ALU.mult), reads=[tmp, self.prow_sb], writes=[tmp])
        P.op('pool', lambda e: e.tensor_tensor(out_tile.t[:], tmp.t[:], rows_b, op=ALU.add), reads=[tmp, self.prow_sb], writes=[out_tile])

    def gate_bcast(self, gate_ch, jcol, gb):
        P = self.P
        dg = self.diag_tmp
        mT = self.mT
        ones_f = self.ones_f
        for c in range(8):
            ps = self.PS[2 + (c // 4)]
            P.op('dve', lambda e, c=c: e.tensor_scalar(dg.t[:], self.ident(), mT.t[:, gate_ch + c, jcol:jcol + 1], None, ALU.mult),
                 reads=[mT, self.cst_sb], writes=[dg])
            P.op('pe', lambda e, c=c, ps=ps: e.matmul(ps.t[:, (c % 4) * 128:(c % 4 + 1) * 128], ones_f.t[:], dg.t[:],
                                                      start=True, stop=True), reads=[dg, ones_f], writes=[ps])
            if c % 4 == 3:
                h = c // 4
                P.op('act', lambda e, ps=ps, h=h: e.activation(gb.t[:, h * 512:(h + 1) * 512], ps.t[:], AF.Identity),
                     reads=[ps], writes=[gb])

    def ph_wout_ln1(self, b, l, ntiles_from=0):
        P, nc = self.P, self.nc
        P.barrier()
        last = (l == self.depth - 1)
        src, src_reads = self.src_rows_fn(b, l)
        with contextlib.ExitStack() as st:
            P.stack, old = st, P.stack
            wo = P.sbuf("wo", [128, 8, D], BF16)
            for k in range(8):
                P.dma('pool', wo.t[:, k, :], self.w_out[l][k * 128:(k + 1) * 128, :], writes=[wo])
            self.prow_sb = P.sbuf("prow_sb", [128, 2048], F32)
            P.dma('sp', self.prow_sb.t[:], self.prow[l][:, PR_LN1W:PR_LN1W + 2048].partition_broadcast(128), writes=[self.prow_sb])
            self.ln_st = P.sbuf("ln_st", [128, 12], F32)
            self.ln_mv = P.sbuf("ln_mv", [128, 8], F32)
            self.diag_tmp = P.sbuf("diag_tmp", [128, 128], F32)
            self.ones_f = ones_f_ = P.sbuf("ones_f", [128, 128], F32)
            P.op('pool', lambda e: e.memset(ones_f_.t[:], 1.0), writes=[ones_f_])
            gbl = P.sbuf("gbl", [128, D], F32)
            gbc = P.sbuf("gbc", [128, D], F32)
            self.gate_bcast(16, b, gbl)
            self.gate_bcast(16, NB, gbc)
            yt = [P.sbuf("yt%d" % i, [128, 8, 128], BF16) for i in range(2)]
            xr = [P.sbuf("xr%d" % i, [128, D], F32) for i in range(2)]
            pre = [P.sbuf("pre%d" % i, [128, D], F32) for i in range(2)]
            tmp = P.sbuf("lntmp", [128, D], F32)
            ot = [P.sbuf("ot%d" % i, [128, D], F32) for i in range(2)]
            t0 = 2 if last else 0
            for ti in range(t0, NT):
                i2 = ti % 2
                gb = gbc if ti < 2 else gbl
                P.dma('sp', yt[i2].t[:], self.YT.t.ap()[:, ti * 128:(ti + 1) * 128].rearrange("(k p) t -> p k t", p=128),
                      reads=[self.YT], writes=[yt[i2]])
                P.dma('sp', xr[i2].t[:], src(ti * 128, 128), reads=src_reads, writes=[xr[i2]])
                for h in range(2):
                    ps = self.PS[4 + h]
                    for k in range(8):
                        P.op('pe', lambda e, ps=ps, k=k, h=h, i2=i2: e.matmul(ps.t[:], yt[i2].t[:, k, :], wo.t[:, k, h * 512:(h + 1) * 512],
                                                                           start=(k == 0), stop=(k == 7)), reads=[yt[i2], wo], writes=[ps])
                    P.op('dve', lambda e, ps=ps, h=h, i2=i2, gb=gb: e.tensor_tensor(pre[i2].t[:, h * 512:(h + 1) * 512], ps.t[:],
                                                                             gb.t[:, h * 512:(h + 1) * 512], op=ALU.mult),
                         reads=[ps, gb], writes=[pre[i2]])
                P.op('dve', lambda e, i2=i2: e.scalar_tensor_tensor(pre[i2].t[:], xr[i2].t[:], ALPHA, pre[i2].t[:], ALU.mult, ALU.add),
                     reads=[xr[i2], pre[i2]], writes=[pre[i2]])
                self.ln_tail(pre[i2], self.prow_sb.t[:, 0:1024], self.prow_sb.t[:, 1024:2048], ot[i2], tmp)
                P.dma('pool', self.XA.t.ap()[ti * 128:(ti + 1) * 128, :], ot[i2].t[:], reads=[ot[i2]], writes=[self.XA])
            P.barrier()
            P.stack = old

    def ph_ffn_ln2(self, b, l):
        P, nc = self.P, self.nc
        P.barrier()
        last = (l == self.depth - 1)
        tok_lo = NCTX if last else 0
        XA = self.XA
        with contextlib.ExitStack() as st:
            P.stack, old = st, P.stack
            hT = P.sbuf("hT", [128, NHC, S], BF16)
            with contextlib.ExitStack() as st2:
                P.stack = st2
                xT = P.sbuf("x2T", [128, 8, S], BF16)
                xtb = [P.sbuf("xtb%d" % i, [128, 4, D], F32) for i in range(2)]
                self._src_reads = [XA]
                segs = self.seq_segs(b, with_ctx=not last)
                self.build_xT(lambda tok0, n: XA.t.ap()[tok0:tok0 + n, :], xT, S, 24, 32, b, segs, xtb)
                wg = [P.sbuf("wg%d" % i, [128, 8, 128], BF16) for i in range(2)]
                wu = [P.sbuf("wu%d" % i, [128, 8, 128], BF16) for i in range(2)]
                sg = [P.sbuf("sg%d" % i, [128, 512], F32) for i in range(2)]
                blocks = [(t, min(512, S - t)) for t in range(tok_lo, S, 512)]
                for hc in range(NHC):
                    i2 = hc % 2
                    P.dma('pool', wg[i2].t[:], self.w13[l][:, hc * 128:(hc + 1) * 128].rearrange("(k p) c -> p k c", p=128), writes=[wg[i2]])
                    P.dma('pool', wu[i2].t[:], self.w13[l][:, HID + hc * 128:HID + (hc + 1) * 128].rearrange("(k p) c -> p k c", p=128),
                          writes=[wu[i2]])
                    for bi, (t0, n) in enumerate(blocks):
                        pg = self.PS[(bi % 2) * 2]
                        pu = self.PS[(bi % 2) * 2 + 1]
                        for k in range(8):
                            P.op('pe', lambda e, pg=pg, k=k, i2=i2, t0=t0, n=n: e.matmul(pg.t[:, 0:n], wg[i2].t[:, k, :], xT.t[:, k, t0:t0 + n],
                                                                                     start=(k == 0), stop=(k == 7)), reads=[wg[i2], xT], writes=[pg])
                        for k in range(8):
                            P.op('pe', lambda e, pu=pu, k=k, i2=i2, t0=t0, n=n: e.matmul(pu.t[:, 0:n], wu[i2].t[:, k, :], xT.t[:, k, t0:t0 + n],
                                                                                     start=(k == 0), stop=(k == 7)), reads=[wu[i2], xT], writes=[pu])
                        s = sg[bi % 2]
                        P.op('act', lambda e, s=s, pg=pg, n=n: e.activation(s.t[:, 0:n], pg.t[:, 0:n], AF.Silu), reads=[pg], writes=[s])
                        P.op('dve', lambda e, s=s, pu=pu, n=n, hc=hc, t0=t0: e.tensor_tensor(hT.t[:, hc, t0:t0 + n], s.t[:, 0:n], pu.t[:, 0:n],
                                                                                       op=ALU.mult), reads=[s, pu], writes=[hT])
                P.barrier()
            P.stack = st
            w2 = P.sbuf("w2", [128, NHC, D], BF16)
            for hc in range(NHC):
                P.dma('pool', w2.t[:, hc, :], self.w2[l][hc * 128:(hc + 1) * 128, :], writes=[w2])
            self.prow_sb = P.sbuf("prow_sb2", [128, 2048], F32)
            P.dma('sp', self.prow_sb.t[:], self.prow[l][:, PR_LN2W:PR_LN2W + 2048].partition_broadcast(128), writes=[self.prow_sb])
            self.ln_st = P.sbuf("ln_st2", [128, 12], F32)
            self.ln_mv = P.sbuf("ln_mv2", [128, 8], F32)
            self.diag_tmp = P.sbuf("diag_tmp2", [128, 128], F32)
            self.ones_f = ones_f_ = P.sbuf("ones_f2", [128, 128], F32)
            P.op('pool', lambda e: e.memset(ones_f_.t[:], 1.0), writes=[ones_f_])
            gbl = P.sbuf("gbl2", [128, D], F32)
            gbc = P.sbuf("gbc2", [128, D], F32)
            self.gate_bcast(40, b, gbl)
            self.gate_bcast(40, NB, gbc)
            xr = [P.sbuf("xr2%d" % i, [128, D], F32) for i in range(2)]
            pre = [P.sbuf("pre2%d" % i, [128, D], F32) for i in range(2)]
            tmp = P.sbuf("lntmp2", [128, D], F32)
            ot = [P.sbuf("ot2%d" % i, [128, D], F32) for i in range(2)]
            for ti in range(2 if last else 0, NT):
                i2 = ti % 2
                gb = gbc if ti < 2 else gbl
                P.dma('sp', xr[i2].t[:], XA.t.ap()[ti * 128:(ti + 1) * 128, :], reads=[XA], writes=[xr[i2]])
                for h in range(2):
                    ps = self.PS[4 + h]
                    for hc in range(NHC):
                        P.op('pe', lambda e, ps=ps, hc=hc, h=h, ti=ti: e.matmul(ps.t[:], hT.t[:, hc, ti * 128:(ti + 1) * 128],
                                                                             w2.t[:, hc, h * 512:(h + 1) * 512], start=(hc == 0), stop=(hc == NHC - 1)),
                             reads=[hT, w2], writes=[ps])
                    P.op('dve', lambda e, ps=ps, h=h, i2=i2, gb=gb: e.tensor_tensor(pre[i2].t[:, h * 512:(h + 1) * 512], ps.t[:],
                                                                             gb.t[:, h * 512:(h + 1) * 512], op=ALU.mult),
                         reads=[ps, gb], writes=[pre[i2]])
                P.op('dve', lambda e, i2=i2: e.scalar_tensor_tensor(pre[i2].t[:], xr[i2].t[:], ALPHA, pre[i2].t[:], ALU.mult, ALU.add),
                     reads=[xr[i2], pre[i2]], writes=[pre[i2]])
                self.ln_tail(pre[i2], self.prow_sb.t[:, 0:1024], self.prow_sb.t[:, 1024:2048], ot[i2], tmp)
                if last:
                    P.dma('pool', self.y[b][(ti - 2) * 128:(ti - 1) * 128, :], ot[i2].t[:], reads=[ot[i2]])
                else:
                    P.dma('pool', self.XB.t.ap()[ti * 128:(ti + 1) * 128, :], ot[i2].t[:], reads=[ot[i2]], writes=[self.XB])
            P.barrier()
            P.stack = old

    def dump(self, buf, name, shape, dt=F32):
        ap = self.dbg_out(name, shape, dt)
        self.P.dma('sp', ap, buf.t.ap(), reads=[buf])


def host_inputs(inputs, core, nb=NB):
    f = lambda a: np.ascontiguousarray(np.asarray(a, dtype=np.float32))
    b0 = core * nb
    m = {}
    m["x"] = f(inputs["x"][b0:b0 + nb])
    m["ctx"] = f(inputs["ctx"][b0:b0 + nb])
    cc = np.zeros((NB + 1, D), np.float32)
    cc[:nb] = inputs["c"][b0:b0 + nb]
    cc[NB] = inputs["c_ctx"]
    m["cc"] = cc
    for k in ("w_mod", "w_in", "w_out", "ffn_w13", "ffn_w2"):
        m[k] = f(inputs[k])
    pcol = np.zeros((DEPTH, 128, NPC), np.float32)
    prow = np.zeros((DEPTH, 1, NPR), np.float32)
    gup = np.zeros((DEPTH, 2, 16, 3, 2, 64), np.float32)
    for l in range(DEPTH):
        for d in range(2):
            pcol[l, :, PC_W0 + d * 3:PC_W0 + d * 3 + 3] = _vec3(inputs["rw_w0"][l, d])
            pcol[l, :, PC_A0 + d * 3:PC_A0 + d * 3 + 3] = _vec3(inputs["rw_a0"][l, d])
            pcol[l, :, PC_GB + d * 3:PC_GB + d * 3 + 3] = _pad_gla(inputs["gla_gate_b"][l, d])
            gup[l, d, :, :, :, :32] = inputs["gla_gate_up"][l, d].reshape(16, 3, 2, 32)
        pcol[l, :, PC_KK:PC_KK + 3] = _vec3(inputs["rw_k_k"][l])
        pcol[l, :, PC_KA:PC_KA + 3] = _vec3(inputs["rw_k_a"][l])
        pcol[l, :, PC_RK:PC_RK + 3] = _vec3(inputs["rw_r_k"][l])
        pcol[l, :, PC_BMOD:PC_BMOD + 48] = inputs["b_mod"][l].reshape(48, 128).T
        prow[l, 0, PR_LN1W:PR_LN1W + 1024] = inputs["ln1_w"][l]
        prow[l, 0, PR_LN1B:PR_LN1B + 1024] = inputs["ln1_b"][l]
        prow[l, 0, PR_LN2W:PR_LN2W + 1024] = inputs["ln2_w"][l]
        prow[l, 0, PR_LN2B:PR_LN2B + 1024] = inputs["ln2_b"][l]
        prow[l, 0, PR_GNORM:PR_GNORM + 384] = np.tile(inputs["gla_norm_w"][l], 6)
        prow[l, 0, PR_GNW:PR_GNW + 384] = inputs["rw_gn_w"][l]
        prow[l, 0, PR_GNB:PR_GNB + 384] = inputs["rw_gn_b"][l]
        prow[l, 0, PR_MU:PR_MU + 1536] = inputs["rw_mu"][l]
    m["pcol"] = pcol
    m["prow"] = prow
    m["gup"] = np.ascontiguousarray(gup.reshape(DEPTH, 2, 16, 384))
    m["cst"] = _host_constants()
    C, Sn = _rope_tables()
    m["ropeC"], m["ropeS"] = C, Sn
    m["nattab"] = _nat_tables(np.asarray(inputs["nat_rpb"], np.float32))
    m["wd2"] = f(inputs["rw_wd2"]).reshape(DEPTH, 128, 384)
    m["wa2"] = f(inputs["rw_wa2"]).reshape(DEPTH, 128, 384)
    m["wg2"] = f(inputs["rw_wg2"])
    return m


def _ph_x(self, b, l, need_dx=True):
    P = self.P
    xT = P.sbuf("xmodT", [128, 8, S], BF16)
    dxT = P.sbuf("dxT", [128, 8, S], BF16) if need_dx else None
    outer = P.stack
    with contextlib.ExitStack() as st:
        P.stack = st
        xtb = [P.sbuf("xtb%d" % i, [128, 4, D], F32) for i in range(2)]
        src, src_reads = self.src_rows_fn(b, l)
        self._src_reads = src_reads
        self.build_xT(src, xT, S, 0, 8, b, self.seq_segs(b), xtb)
        if need_dx:
            tmp = P.sbuf("dxtmp", [128, 2, LAT], F32)
            for (t0, n) in ((0, NCTX), (NCTX, LAT)):
                for c in range(0, 8, 2):
                    P.op('dve', lambda e, t0=t0, n=n, c=c: e.tensor_tensor(tmp.t[:, :, 0:n - 2], xT.t[:, c:c + 2, t0:t0 + n - 2],
                                                                        xT.t[:, c:c + 2, t0 + 2:t0 + n], op=ALU.add), reads=[xT], writes=[tmp])
                    P.op('dve', lambda e, t0=t0, n=n, c=c: e.scalar_tensor_tensor(dxT.t[:, c:c + 2, t0 + 1:t0 + n - 1], tmp.t[:, :, 0:n - 2], 0.5,
                                                                               xT.t[:, c:c + 2, t0 + 1:t0 + n - 1], ALU.mult, ALU.subtract),
                         reads=[tmp, xT], writes=[dxT])
                P.op('dve', lambda e, t0=t0: e.scalar_tensor_tensor(dxT.t[:, :, t0:t0 + 1], xT.t[:, :, t0 + 1:t0 + 2], 0.5, xT.t[:, :, t0:t0 + 1],
                                                                 ALU.mult, ALU.subtract), reads=[xT], writes=[dxT])
                P.op('dve', lambda e, t0=t0, n=n: e.scalar_tensor_tensor(dxT.t[:, :, t0 + n - 1:t0 + n], xT.t[:, :, t0 + n - 2:t0 + n - 1], 0.5,
                                                                      xT.t[:, :, t0 + n - 1:t0 + n], ALU.mult, ALU.subtract), reads=[xT], writes=[dxT])
        P.barrier()
    P.stack = outer
    return xT, dxT


Kern.ph_x = _ph_x


def _ph_nat(self, b, l, xT):
    P = self.P
    last = (l == self.depth - 1)
    with contextlib.ExitStack() as st:
        P.stack, old = st, P.stack
        qT = P.sbuf("n_qT", [128, 2, S], BF16)
        kT = P.sbuf("n_kT", [128, 2, S], BF16)
        V = P.sbuf("n_V", [128, NT, 256], BF16)
        Vs = P.sbuf("n_Vs", [128, 15, 256], BF16)
        yTn = P.sbuf("n_yT", [128, 2, S], BF16)
        tab = P.sbuf("n_tab", [128, 4, 14 * 64], F32)
        onesb = P.sbuf("n_ones", [128, 128], BF16)
        P.op('pool', lambda e: e.memset(onesb.t[:], 1.0), writes=[onesb])
        for h in range(4):
            P.dma('sp', tab.t[:, h, :], self.nattab[l, h], writes=[tab])
        wq = P.sbuf("n_wq", [128, 8, 256], BF16)
        wk = P.sbuf("n_wk", [128, 8, 256], BF16)
        wv = P.sbuf("n_wv", [128, 8, 256], BF16)
        for (w, c0) in ((wq, NQ), (wk, NK), (wv, NV)):
            P.dma('pool', w.t[:], self.w_in[l][:, c0:c0 + 256].rearrange("(k p) c -> p k c", p=128), writes=[w])
        blocks = [(t, min(512, S - t)) for t in range(0, S, 512)]
        n = 0
        for (w, dst) in ((wq, qT), (wk, kT)):
            for tl in range(2):
                for (t0, nn) in blocks:
                    ps = self.PS[n % 2]
                    for k in range(8):
                        P.op('pe', lambda e, ps=ps, w=w, tl=tl, k=k, t0=t0, nn=nn: e.matmul(
                            ps.t[:, 0:nn], w.t[:, k, tl * 128:(tl + 1) * 128], xT.t[:, k, t0:t0 + nn], start=(k == 0), stop=(k == 7)),
                            reads=[w, xT], writes=[ps])
                    if n % 2 == 0:
                        P.op('dve', lambda e, ps=ps, dst=dst, tl=tl, t0=t0, nn=nn: e.tensor_copy(dst.t[:, tl, t0:t0 + nn], ps.t[:, 0:nn]),
                             reads=[ps], writes=[dst])
                    else:
                        P.op('act', lambda e, ps=ps, dst=dst, tl=tl, t0=t0, nn=nn: e.activation(dst.t[:, tl, t0:t0 + nn], ps.t[:, 0:nn], AF.Identity),
                             reads=[ps], writes=[dst])
                    n += 1
        for (dst, nt, base) in ((V, NT, 0), (Vs, 15, NCTX + 64)):
            for ti in range(nt):
                ps = self.PS[2 + ti % 2]
                t0 = base + ti * 128
                for k in range(8):
                    P.op('pe', lambda e, ps=ps, k=k, t0=t0: e.matmul(ps.t[:, 0:256], xT.t[:, k, t0:t0 + 128], wv.t[:, k, :], start=(k == 0), stop=(k == 7)),
                         reads=[wv, xT], writes=[ps])
                if ti % 2 == 0:
                    P.op('dve', lambda e, ps=ps, dst=dst, ti=ti: e.tensor_copy(dst.t[:, ti, :], ps.t[:, 0:256]), reads=[ps], writes=[dst])
                else:
                    P.op('act', lambda e, ps=ps, dst=dst, ti=ti: e.activation(dst.t[:, ti, :], ps.t[:, 0:256], AF.Identity), reads=[ps], writes=[dst])
        stt = [P.sbuf("n_stt%d" % i, [128, 4, 64], F32) for i in range(2)]
        E = [P.sbuf("n_E%d" % i, [128, 6, 64], BF16) for i in range(2)]
        rB = P.sbuf("n_rB", [128, 2, 64], F32)
        it = 0
        for r in range(32):
            r0 = min(max(r - 4, 0), 24)
            off = r0 - r + 7
            q0 = NCTX + r * 64
            for hp in range(2):
                pso = self.PS[4 + (it % 2)]
                for e2 in range(2):
                    h = hp * 2 + e2
                    pb = e2 * 64
                    pss = self.PS[(it % 2) * 2 + e2]
                    Eh = E[e2]
                    for j in range(6):
                        k0 = (NCTX + r0 * 64 + j * 128) if j < 4 else (j - 4) * 128
                        P.op('pe', lambda e, pss=pss, j=j, pb=pb, hp=hp, k0=k0, q0=q0: e.matmul(
                            pss.t[:, j * 64:(j + 1) * 64], kT.t[pb:pb + 64, hp, k0:k0 + 128], qT.t[pb:pb + 64, hp, q0:q0 + 64], start=True, stop=True),
                            reads=[kT, qT], writes=[pss])
                    P.op('dve', lambda e, pss=pss, e2=e2, h=h, off=off: e.scalar_tensor_tensor(
                        stt[e2].t[:], pss.t[:, 0:256].rearrange("p (a b) -> p a b", a=4), 0.125,
                        tab.t[:, h, :].rearrange("p (a b) -> p a b", a=14)[:, off:off + 7:2, :], ALU.mult, ALU.add),
                        reads=[pss, tab], writes=[stt[e2]])
                    P.op('act', lambda e, Eh=Eh, e2=e2: e.activation(Eh.t[:, 0:4, :], stt[e2].t[:], AF.Exp), reads=[stt[e2]], writes=[Eh])
                    P.op('act', lambda e, Eh=Eh, pss=pss: e.activation(Eh.t[:, 4:6, :], pss.t[:, 256:384].rearrange("p (a b) -> p a b", a=2), AF.Exp, scale=0.125),
                         reads=[pss], writes=[Eh])
                first = True
                for e2 in range(2):
                    Eh = E[e2]
                    for j in range(6):
                        if j < 4:
                            if r0 % 2 == 0:
                                vt = V.t[:, 2 + r0 // 2 + j, hp * 128:(hp + 1) * 128]
                            else:
                                vt = Vs.t[:, (r0 - 1) // 2 + j, hp * 128:(hp + 1) * 128]
                        else:
                            vt = V.t[:, j - 4, hp * 128:(hp + 1) * 128]
                        P.op('pe', lambda e, pso=pso, vt=vt, Eh=Eh, j=j, e2=e2, first=first: e.matmul(
                            pso.t[:, e2 * 64:(e2 + 1) * 64], vt, Eh.t[:, j, :], start=first, stop=(j == 5), skip_group_check=True),
                            reads=[V, Vs, Eh], writes=[pso])
                        first = False
                        P.op('pe', lambda e, pso=pso, Eh=Eh, j=j, e2=e2: e.matmul(
                            pso.t[:, 128 + e2 * 64:128 + (e2 + 1) * 64], onesb.t[:], Eh.t[:, j, :], start=False, stop=(j == 5), skip_group_check=True),
                            reads=[onesb, Eh], writes=[pso])
                P.op('dve', lambda e, pso=pso: e.reciprocal(rB.t[:], pso.t[:, 128:256].rearrange("p (a b) -> p a b", a=2)), reads=[pso], writes=[rB])
                for e2 in range(2):
                    pb = e2 * 64
                    P.op('dve', lambda e, pso=pso, e2=e2, pb=pb, hp=hp, q0=q0: e.tensor_tensor(
                        yTn.t[pb:pb + 64, hp, q0:q0 + 64], pso.t[pb:pb + 64, e2 * 64:(e2 + 1) * 64], rB.t[pb:pb + 64, e2, :], op=ALU.mult),
                        reads=[pso, rB], writes=[yTn])
                it += 1
        if not last:
            Ec = [P.sbuf("n_Ec%d" % i, [128, 2, 256], BF16) for i in range(2)]
            rBc = P.sbuf("n_rBc", [128, 256], F32)
            for hp in range(2):
                for e2 in range(2):
                    pb = e2 * 64
                    pss = self.PS[e2]
                    pso = self.PS[2 + e2]
                    for j in range(2):
                        P.op('pe', lambda e, pss=pss, j=j, pb=pb, hp=hp: e.matmul(
                            pss.t[:, j * 256:(j + 1) * 256], kT.t[pb:pb + 64, hp, j * 128:(j + 1) * 128], qT.t[pb:pb + 64, hp, 0:256], start=True, stop=True),
                            reads=[kT, qT], writes=[pss])
                    P.op('act', lambda e, pss=pss, e2=e2: e.activation(Ec[e2].t[:], pss.t[:].rearrange("p (a b) -> p a b", a=2), AF.Exp, scale=0.125),
                         reads=[pss], writes=[Ec[e2]])
                    for j in range(2):
                        P.op('pe', lambda e, pso=pso, j=j, hp=hp, e2=e2: e.matmul(
                            pso.t[:, 0:256], V.t[:, j, hp * 128:(hp + 1) * 128], Ec[e2].t[:, j, :], start=(j == 0), stop=(j == 1), skip_group_check=True),
                            reads=[V, Ec[e2]], writes=[pso])
                        P.op('pe', lambda e, pso=pso, j=j, e2=e2: e.matmul(
                            pso.t[:, 256:512], onesb.t[:], Ec[e2].t[:, j, :], start=False, stop=(j == 1), skip_group_check=True),
                            reads=[onesb, Ec[e2]], writes=[pso])
                    P.op('dve', lambda e, pso=pso: e.reciprocal(rBc.t[:], pso.t[:, 256:512]), reads=[pso], writes=[rBc])
                    P.op('dve', lambda e, pso=pso, pb=pb, hp=hp: e.tensor_tensor(
                        yTn.t[pb:pb + 64, hp, 0:256], pso.t[pb:pb + 64, 0:256], rBc.t[pb:pb + 64, :], op=ALU.mult),
                        reads=[pso, rBc], writes=[yTn])
        t_lo = NCTX if last else 0
        for tl in range(2):
            P.dma('sp', self.YT.t.ap()[384 + tl * 128:384 + (tl + 1) * 128, t_lo:S], yTn.t[:, tl, t_lo:S], reads=[yTn], writes=[self.YT])
        P.barrier()
        P.stack = old


Kern.ph_nat = _ph_nat


class _Ops:
    def __init__(self, P):
        self.P = P

    def TT(self, eng, out, a, b, op, reads, writes):
        self.P.op(eng, lambda e: e.tensor_tensor(out, a, b, op=op), reads, writes)

    def TS(self, eng, out, a, s1, s2, op0, op1, reads, writes):
        if op1 is None:
            self.P.op(eng, lambda e: e.tensor_scalar(out, a, s1, None, op0), reads, writes)
        else:
            self.P.op(eng, lambda e: e.tensor_scalar(out, a, s1, s2, op0, op1), reads, writes)

    def STT(self, out, a, s, b, op0, op1, reads, writes):
        self.P.op('dve', lambda e: e.scalar_tensor_tensor(out, a, s, b, op0, op1), reads, writes)

    def ACT(self, out, in_, func, reads, writes, **kw):
        self.P.op('act', lambda e: e.activation(out, in_, func, **kw), reads, writes)

    def CP(self, eng, out, in_, reads, writes):
        if eng == 'act':
            self.P.op('act', lambda e: e.activation(out, in_, AF.Identity), reads, writes)
        else:
            self.P.op(eng, lambda e: e.tensor_copy(out, in_), reads, writes)

    def MM(self, out, lhsT, rhs, start, stop, reads, writes):
        self.P.op('pe', lambda e: e.matmul(out, lhsT, rhs, start=start, stop=stop, skip_group_check=True), reads, writes)

    def TR(self, out, in_, ident, reads, writes):
        self.P.op('pe', lambda e: e.transpose(out, in_, ident), reads, writes)


def _ph_scan(self, b, l, xT, dxT, kind):
    P = self.P
    O = _Ops(P)
    rw = (kind == 'rw')
    NK = 16 if rw else 8
    PS = self.PS
    cst = self.cst_sb
    pc = self.pcol_sb
    with contextlib.ExitStack() as st:
        P.stack, old = st, P.stack
        Of = self.OF
        of_sb = P.sbuf("s_of", [128, 384], F32)
        COEF = P.sbuf("s_coef", [128, NT, 8], F32) if rw else None
        mskF = P.sbuf("s_mskF", [128, 256], BF16)
        mskB = P.sbuf("s_mskB", [128, 256], BF16)
        lmF = P.sbuf("s_lmF", [128, 128], BF16)
        lmB = P.sbuf("s_lmB", [128, 128], BF16)
        hselb = P.sbuf("s_hsel", [128, 2], BF16)
        O.CP('dve', mskF.t[:], cst.t[:, C_MASKF:C_MASKF + 256], [cst], [mskF])
        O.CP('dve', mskB.t[:], cst.t[:, C_MASKB:C_MASKB + 256], [cst], [mskB])
        O.CP('dve', lmF.t[:], cst.t[:, C_LMF:C_LMF + 128], [cst], [lmF])
        O.CP('dve', lmB.t[:], cst.t[:, C_LMB:C_LMB + 128], [cst], [lmB])
        O.CP('dve', hselb.t[:], cst.t[:, C_HSEL:C_HSEL + 2], [cst], [hselb])
        prw = P.sbuf("s_prow", [128, 3 * 384], F32)
        P.dma('sp', prw.t[:], self.prow[l][:, PR_GNORM:PR_GNORM + 3 * 384].partition_broadcast(128), writes=[prw])
        if rw:
            ncolt = 9
            W = P.sbuf("s_W", [128, ncolt, 16, 128], BF16)
            Wv = P.sbuf("s_Wv", [128, 16, 384], BF16)
            with contextlib.ExitStack() as st2:
                P.stack = st2
                mub = P.sbuf("s_mub", [128, 1536], F32)
                P.dma('sp', mub.t[:], self.prow[l][:, PR_MU:PR_MU + 1536].partition_broadcast(128), writes=[mub])
                colt = [RR, RR + 128, RR + 256, RK, RK + 128, RK + 256, RDD, RAD, RGD]
                for i, c0 in enumerate(colt):
                    P.dma('pool', W.t[:, i, 0:8, :], self.w_in[l][:, c0:c0 + 128].rearrange("(k p) c -> p k c", p=128), writes=[W])
                    m0 = c0 - RR
                    O.TT('dve', W.t[:, i, 8:16, :], W.t[:, i, 0:8, :], mub.t[:, m0:m0 + 128][:, None, :].to_broadcast([128, 8, 128]), ALU.mult,
                         [W, mub], [W])
                P.dma('pool', Wv.t[:, 0:8, :], self.w_in[l][:, RV:RV + 384].rearrange("(k p) c -> p k c", p=128), writes=[Wv])
                O.TT('dve', Wv.t[:, 8:16, :], Wv.t[:, 0:8, :], mub.t[:, RV - RR:RV - RR + 384][:, None, :].to_broadcast([128, 8, 384]), ALU.mult,
                     [Wv, mub], [Wv])
                P.barrier()
            P.stack = st
            wd2 = P.sbuf("s_wd2", [128, 384], BF16)
            wa2 = P.sbuf("s_wa2", [128, 384], BF16)
            wg2 = P.sbuf("s_wg2", [128, 384], BF16)
            P.dma('pool', wd2.t[:], self.wd2[l], writes=[wd2])
            P.dma('pool', wa2.t[:], self.wa2[l], writes=[wa2])
            P.dma('pool', wg2.t[:], self.wg2[l], writes=[wg2])
            bones = cst.t[:, C_BONES:C_BONES + 128]
        else:
            W = P.sbuf("s_Wg", [128, 8, 4, 384], BF16)
            Wdn = P.sbuf("s_Wdn", [128, 8, 48], BF16)
            Wv = P.sbuf("s_Wv", [128, 8, 384], BF16)
            Wgt = P.sbuf("s_Wgt", [128, 8, 384], BF16)
            gup = P.sbuf("s_gup", [48, 384], BF16)
            P.op('pool', lambda e: e.memset(W.t[:], 0.0), writes=[W])
            P.op('pool', lambda e: e.memset(Wdn.t[:], 0.0), writes=[Wdn])
            P.op('pool', lambda e: e.memset(gup.t[:], 0.0), writes=[gup])
            with contextlib.ExitStack() as st2:
                P.stack = st2
                wqk = P.sbuf("s_wqk", [128, 8, 384], BF16)
                P.dma('pool', wqk.t[:], self.w_in[l][:, GQ:GQ + 384].rearrange("(k p) c -> p k c", p=128), writes=[wqk])
                for qi in range(2):
                    src = wqk.t[:, :, qi * 192:(qi + 1) * 192].rearrange("p k (h c) -> p k h c", h=6)
                    dst = W.t[:, :, 2 * qi, :].rearrange("p k (h c) -> p k h c", h=6)[:, :, :, 0:32]
                    dstp = W.t[:, :, 2 * qi + 1, :].rearrange("p k (h c) -> p k h c", h=6)
                    for kk in range(8):
                        O.CP('dve', dst[:, kk], src[:, kk], [wqk], [W])
                        O.TS('dve', dstp[:, kk, :, 0:32:2], src[:, kk, :, 1:32:2], -1.0, None, ALU.mult, None, [wqk], [W])
                        O.CP('dve', dstp[:, kk, :, 1:32:2], src[:, kk, :, 0:32:2], [wqk], [W])
                P.barrier()
            P.stack = st
            P.dma('pool', Wdn.t[:, :, 0:16], self.w_in[l][:, GDN:GDN + 16].rearrange("(k p) c -> p k c", p=128), writes=[Wdn])
            P.dma('pool', Wdn.t[:, :, 32:48], self.w_in[l][:, GDN + 16:GDN + 32].rearrange("(k p) c -> p k c", p=128), writes=[Wdn])
            P.dma('pool', Wv.t[:], self.w_in[l][:, GV:GV + 384].rearrange("(k p) c -> p k c", p=128), writes=[Wv])
            P.dma('pool', Wgt.t[:], self.w_in[l][:, GG:GG + 384].rearrange("(k p) c -> p k c", p=128), writes=[Wgt])
            P.dma('pool', gup.t[0:16, :], self.gup[l, 0], writes=[gup])
            P.dma('pool', gup.t[32:48, :], self.gup[l, 1], writes=[gup])
            ropeC = P.sbuf("s_ropeC", [128, 256], F32)
            ropeS = P.sbuf("s_ropeS", [128, 256], F32)
        if Prog.limit is not None:
            print("MS weights", P.n_ops)
        KR = [P.sbuf("s_KR%d" % p, [128, 2, 2, 128], BF16) for p in range(3)]
        Kg = [P.sbuf("s_Kg%d" % p, [128, 256], BF16) for p in range(3)]
        Kgp = [P.sbuf("s_Kgp%d" % p, [128, 256], F32) for p in range(3)]
        if rw:
            Bg = [P.sbuf("s_Bg%d" % p, [128, 256], BF16) for p in range(3)]
            Bgp = [P.sbuf("s_Bgp%d" % p, [128, 256], F32) for p in range(3)]
        else:
            for p in range(3):
                P.op('pool', lambda e, p=p: e.memset(KR[p].t[:], 0.0), writes=[KR[p]])
        GC = P.sbuf("s_GC", [128, 3, 2], F32)
        V = P.sbuf("s_V", [128, 2, 384], BF16)
        Vf = P.sbuf("s_Vf", [128, 2, 384], F32)
        ft = {n: P.sbuf("s_f_" + n, [128, 256], F32) for n in
              (["rf", "kf", "sg", "css", "g", "gi", "gex", "gp", "a", "kk", "t1", "kmod", "kka", "t2"] if rw else
               ["sg", "css", "g", "gi", "gp", "t1", "t2", "qr", "kr"])}
        if rw:
            tdd = P.sbuf("s_tdd", [128, 256], BF16)
            adb = P.sbuf("s_adb", [128, 256], BF16)
            sgd = P.sbuf("s_sgd", [128, 256], BF16)
            prodb = P.sbuf("s_prodb", [128, 256], BF16)
        else:
            dnb = P.sbuf("s_dnb", [48, 256], BF16)
        LMC = P.sbuf("s_LMC", [128, 6, 2, 256], BF16)
        Lm = P.sbuf("s_L", [128, 6, 128], BF16)
        PTs = [P.sbuf("s_PT%d" % i, [128, 6, 128], BF16) for i in range(2)]
        Psq = [P.sbuf("s_Pq%d" % i, [128, 6, 128], BF16) for i in range(2)]
        Ub = P.sbuf("s_Ub", [128, 384], BF16)
        BKT = P.sbuf("s_BKT", [128, 6, 128], BF16)
        Tw = P.sbuf("s_Tw", [128, 3, 128], F32)
        Twb = P.sbuf("s_Twb", [128, 3, 128], BF16)
        ttmp = P.sbuf("s_ttmp", [128, 3, 128], F32)
        o_sb = P.sbuf("s_o", [128, 384], F32)
        y_sb = P.sbuf("s_y", [128, 384], F32)
        sq_sb = P.sbuf("s_sq", [128, 384], F32)
        gt_sb = P.sbuf("s_gt", [128, 384], F32)
        yb = P.sbuf("s_yb", [128, 384], F32)
        stt = P.sbuf("s_stat", [128, 8, 8], F32)
        yTo = P.sbuf("s_yTo", [128, 3, 256], BF16)
        nrm = prw.t[:, 0:384]
        gnw = prw.t[:, 384:768]
        gnb = prw.t[:, 768:1152]
        bdm = cst.t[:, C_BD:C_BD + 128]
        ident_b = self.identb
        feat0 = 5 * 128 if rw else 0

        for d in range(2):
            fwd = (d == 0)
            msk = mskF if fwd else mskB
            lm = lmF if fwd else lmB
            scm = cst.t[:, C_SCF:C_SCF + 256] if fwd else cst.t[:, C_SCB:C_SCB + 256]
            cend = 127 if fwd else 0
            O.P.op('pool', lambda e: e.memset(Tw.t[:], 0.0), writes=[Tw])
            O.P.op('pool', lambda e: e.memset(Twb.t[:], 0.0), writes=[Twb])
            border = list(range(NBLK)) if fwd else [0] + list(range(NBLK - 1, 0, -1))
            for bi in border:
                t0 = bi * CB
                lat = bi > 0

                def rhs(k):
                    return xT.t[:, k, t0:t0 + CB] if k < 8 else dxT.t[:, k - 8, t0:t0 + CB]

                def lhs_tok(k, c):
                    return xT.t[:, k, t0 + c * 128:t0 + (c + 1) * 128] if k < 8 else dxT.t[:, k - 8, t0 + c * 128:t0 + (c + 1) * 128]
                xr = [xT, dxT] if rw else [xT]
                for c in range(2):
                    ps = PS[c]
                    for k in range(NK):
                        O.MM(ps.t[:, 0:384], lhs_tok(k, c), Wv.t[:, k, :], k == 0, k == NK - 1, xr + [Wv], [ps])
                    O.CP('act', V.t[:, c, :], ps.t[:, 0:384], [ps], [V])
                if rw:
                    def projF(i, ps, half):
                        o = ps.t[:, half * 256:(half + 1) * 256]
                        for k in range(16):
                            O.MM(o, W.t[:, i, k, :], rhs(k), k == 0, k == 15, xr + [W], [ps])
                        return o
                    o_dd = projF(6, PS[2], 0)
                    O.ACT(tdd.t[:], o_dd, AF.Tanh, [PS[2]], [tdd])
                    o_ad = projF(7, PS[2], 1)
                    O.CP('dve', adb.t[:], o_ad, [PS[2]], [adb])
                    if not fwd:
                        o_gd = projF(8, PS[3], 0)
                        O.ACT(sgd.t[:], o_gd, AF.Sigmoid, [PS[3]], [sgd])
                    for p in range(3):
                        f = ft
                        o_r = projF(p, PS[4], 0)
                        O.CP('act', f["rf"].t[:], o_r, [PS[4]], [f["rf"]])
                        o_k = projF(3 + p, PS[4], 1)
                        O.CP('dve', f["kf"].t[:], o_k, [PS[4]], [f["kf"]])
                        o_d = PS[5].t[:, 0:256]
                        O.MM(o_d, wd2.t[d * 64:(d + 1) * 64, p * 128:(p + 1) * 128], tdd.t[d * 64:(d + 1) * 64, :], True, True, [wd2, tdd], [PS[5]])
                        O.ACT(f["sg"].t[:], o_d, AF.Sigmoid, [PS[5], pc], [f["sg"]], bias=pc.t[:, PC_W0 + d * 3 + p:PC_W0 + d * 3 + p + 1])
                        o_a = PS[5].t[:, 256:512]
                        O.MM(o_a, wa2.t[d * 64:(d + 1) * 64, p * 128:(p + 1) * 128], adb.t[d * 64:(d + 1) * 64, :], True, True, [wa2, adb], [PS[5]])
                        O.ACT(f["a"].t[:], o_a, AF.Sigmoid, [PS[5], pc], [f["a"]], bias=pc.t[:, PC_A0 + d * 3 + p:PC_A0 + d * 3 + p + 1])
                        self._scan_gates(O, f, scm, fwd, cend, GC, p, DECAY)
                        O.TS('pool', f["kk"].t[:], f["kf"].t[:], pc.t[:, PC_KK + p:PC_KK + p + 1], None, ALU.mult, None, [f["kf"], pc], [f["kk"]])
                        O.TT('pool', f["t1"].t[:], f["kk"].t[:], f["kk"].t[:], ALU.mult, [f["kk"]], [f["t1"]])
                        o_ss = PS[6].t[:, 0:256]
                        O.MM(o_ss, bones, f["t1"].t[:], True, True, [cst, f["t1"]], [PS[6]])
                        O.ACT(f["t2"].t[:], o_ss, AF.Sqrt, [PS[6], cst], [f["t2"]], bias=cst.t[:, C_E12:C_E12 + 1])
                        O.P.op('dve', lambda e: e.reciprocal(f["t1"].t[:], f["t2"].t[:]), [f["t2"]], [f["t1"]])
                        O.TT('pool', f["kk"].t[:], f["kk"].t[:], f["t1"].t[:], ALU.mult, [f["kk"], f["t1"]], [f["kk"]])
                        O.TS('dve', f["t1"].t[:], f["a"].t[:], pc.t[:, PC_KA + p:PC_KA + p + 1], pc.t[:, NPC + p:NPC + p + 1], ALU.mult, ALU.add,
                             [f["a"], pc], [f["t1"]])
                        O.TT('pool', f["kmod"].t[:], f["kf"].t[:], f["t1"].t[:], ALU.mult, [f["kf"], f["t1"]], [f["kmod"]])
                        O.TT('pool', f["kka"].t[:], f["kk"].t[:], f["a"].t[:], ALU.mult, [f["kk"], f["a"]], [f["kka"]])
                        O.TT('dve', KR[p].t[:, :, 0, :], f["kk"].t[:].rearrange("p (c t) -> p c t", c=2), f["gex"].t[:].rearrange("p (c t) -> p c t", c=2),
                             ALU.mult, [f["kk"], f["gex"]], [KR[p]])
                        O.TT('pool', KR[p].t[:, :, 1, :], f["rf"].t[:].rearrange("p (c t) -> p c t", c=2), f["g"].t[:].rearrange("p (c t) -> p c t", c=2),
                             ALU.mult, [f["rf"], f["g"]], [KR[p]])
                        O.TT('dve', Kg[p].t[:], f["kmod"].t[:], f["gi"].t[:], ALU.mult, [f["kmod"], f["gi"]], [Kg[p]])
                        O.TT('pool', Kgp[p].t[:], f["kmod"].t[:], f["gp"].t[:], ALU.mult, [f["kmod"], f["gp"]], [Kgp[p]])
                        O.STT(Bg[p].t[:], f["kka"].t[:], -1.0, f["gi"].t[:], ALU.mult, ALU.mult, [f["kka"], f["gi"]], [Bg[p]])
                        O.STT(Bgp[p].t[:], f["kka"].t[:], -1.0, f["gp"].t[:], ALU.mult, ALU.mult, [f["kka"], f["gp"]], [Bgp[p]])
                        O.TT('pool', f["t1"].t[:], f["rf"].t[:], f["kmod"].t[:], ALU.mult, [f["rf"], f["kmod"]], [f["t1"]])
                        O.TS('dve', prodb.t[:], f["t1"].t[:], pc.t[:, PC_RK + p:PC_RK + p + 1], None, ALU.mult, None, [f["t1"], pc], [prodb])
                        for c in range(2):
                            O.MM(PS[7].t[:, c * 8 + p * 2:c * 8 + p * 2 + 2], prodb.t[:, c * 128:(c + 1) * 128], hselb.t[:], True, True,
                                 [prodb, hselb], [PS[7]])
                    for c in range(2):
                        ch = bi * 2 + c
                        if fwd:
                            O.CP('dve', COEF.t[:, ch, 0:6], PS[7].t[:, c * 8:c * 8 + 6], [PS[7]], [COEF])
                        else:
                            O.TT('dve', COEF.t[:, ch, 0:6], PS[7].t[:, c * 8:c * 8 + 6], COEF.t[:, ch, 0:6], ALU.add, [PS[7], COEF], [COEF])
                else:
                    f = ft
                    o_dn = PS[2].t[0:48, 0:256]
                    for k in range(8):
                        O.MM(o_dn, Wdn.t[:, k, :], rhs(k), k == 0, k == 7, [xT, Wdn], [PS[2]])
                    O.CP('dve', dnb.t[:], o_dn, [PS[2]], [dnb])
                    if lat:
                        lt0 = t0 - NCTX
                        P.dma('sp', ropeC.t[:], self.ropeC[:, lt0:lt0 + CB], writes=[ropeC])
                        P.dma('sp', ropeS.t[:], self.ropeS[:, lt0:lt0 + CB], writes=[ropeS])
                    for p in range(3):
                        o_z = PS[3].t[:, 0:256]
                        O.MM(o_z, gup.t[d * 32:d * 32 + 16, p * 128:(p + 1) * 128], dnb.t[d * 32:d * 32 + 16, :], True, True, [gup, dnb], [PS[3]])
                        O.ACT(f["t1"].t[:], o_z, AF.Exp, [PS[3], pc], [f["t1"]], scale=-1.0, bias=pc.t[:, NPC + 3 + d * 3 + p:NPC + 3 + d * 3 + p + 1])
                        O.ACT(f["sg"].t[:], f["t1"].t[:], AF.Ln, [f["t1"], cst], [f["sg"]], bias=cst.t[:, C_ONE:C_ONE + 1])
                        self._scan_gates(O, f, scm, fwd, cend, GC, p, 1.0 / 16.0)
                        for qi, (dst, nm) in enumerate(((None, "qr"), (None, "kr"))):
                            def projq(j, half):
                                o = PS[4 + (half // 2)].t[:, (half % 2) * 256:(half % 2 + 1) * 256]
                                for k in range(8):
                                    O.MM(o, W.t[:, k, j, p * 128:(p + 1) * 128], rhs(k), k == 0, k == 7, [xT, W], [PS[4 + (half // 2)]])
                                return o, PS[4 + (half // 2)]
                            o1, b1 = projq(2 * qi, 2 * qi)
                            if lat:
                                o2, b2 = projq(2 * qi + 1, 2 * qi + 1)
                                O.TT('dve', f["t1"].t[:], o1, ropeC.t[:], ALU.mult, [b1, ropeC], [f["t1"]])
                                O.TT('dve', f["t2"].t[:], o2, ropeS.t[:], ALU.mult, [b2, ropeS], [f["t2"]])
                                O.TT('pool', f[nm].t[:], f["t1"].t[:], f["t2"].t[:], ALU.add, [f["t1"], f["t2"]], [f[nm]])
                            else:
                                O.CP('dve', f[nm].t[:], o1, [b1], [f[nm]])
                        O.STT(KR[p].t[:, :, 1, :], f["qr"].t[:].rearrange("p (c t) -> p c t", c=2), 32.0 ** -0.5,
                              f["g"].t[:].rearrange("p (c t) -> p c t", c=2), ALU.mult, ALU.mult, [f["qr"], f["g"]], [KR[p]])
                        O.TT('dve', Kg[p].t[:], f["kr"].t[:], f["gi"].t[:], ALU.mult, [f["kr"], f["gi"]], [Kg[p]])
                        O.TT('pool', Kgp[p].t[:], f["kr"].t[:], f["gp"].t[:], ALU.mult, [f["kr"], f["gp"]], [Kgp[p]])
                if Prog.limit is not None:
                    print("MS prep", d, bi, P.n_ops)
                for c in ([0, 1] if fwd else [1, 0]):
                    ch = bi * 2 + c
                    cs = slice(c * 128, (c + 1) * 128)
                    mb = msk.t[:, None, :].to_broadcast([128, 2, 256])
                    for p in range(3):
                        for e2 in range(2):
                            h = 2 * p + e2
                            pb = e2 * 64
                            ps = PS[e2]
                            krr = KR[p].t[pb:pb + 64, c, :, :].rearrange("p a t -> p (a t)")
                            if rw:
                                O.MM(ps.t[:, 0:256], Bg[p].t[pb:pb + 64, cs], krr, True, True, [Bg[p], KR[p]], [ps])
                            O.MM(ps.t[:, 256:512], Kg[p].t[pb:pb + 64, cs], krr, True, True, [Kg[p], KR[p]], [ps])
                            if rw:
                                O.TT('dve', LMC.t[:, h, :, :], ps.t[:].rearrange("p (a t) -> p a t", a=2), mb, ALU.mult, [ps, msk], [LMC])
                            else:
                                O.TT('dve', LMC.t[:, h, 1, :], ps.t[:, 256:512], msk.t[:], ALU.mult, [ps, msk], [LMC])
                    if rw:
                        for e2 in range(2):
                            pb = e2 * 64
                            ps = PS[e2]
                            for pp in range(3):
                                O.MM(ps.t[:, pp * 128:(pp + 1) * 128], KR[pp].t[pb:pb + 64, c, 0, :], Bg[pp].t[pb:pb + 64, cs], True, True,
                                     [KR[pp], Bg[pp]], [ps])
                            O.TT('dve', Lm.t[:, e2:6:2, :], ps.t[:, 0:384].rearrange("p (a t) -> p a t", a=3),
                                 lm.t[:, None, :].to_broadcast([128, 3, 128]), ALU.mult, [ps, lm], [Lm])
                        first = True
                        for p in range(3):
                            O.MM(PS[2].t[:, p * 128:(p + 1) * 128], KR[p].t[:, c, 0, :], Twb.t[:, p, :], first, False, [KR[p], Twb], [PS[2]])
                            first = False
                        for h in range(6):
                            O.MM(PS[2].t[:, h * 64:(h + 1) * 64], LMC.t[:, h, 1, 0:128], V.t[:, c, h * 64:(h + 1) * 64], False, False, [LMC, V], [PS[2]])
                        O.CP('act', Ub.t[:], PS[2].t[:, 0:384], [PS[2]], [Ub])
                        for lv in range(7):
                            def PT(h):
                                return LMC.t[:, h, 0, 0:128] if lv == 0 else PTs[lv % 2].t[:, h, :]

                            def PP(h):
                                return Lm.t[:, h, :] if lv == 0 else Psq[lv % 2].t[:, h, :]
                            ptb = LMC if lv == 0 else PTs[lv % 2]
                            ppb = Lm if lv == 0 else Psq[lv % 2]
                            for h in range(6):
                                O.MM(PS[2].t[:, h * 64:(h + 1) * 64], PT(h), Ub.t[:, h * 64:(h + 1) * 64], False, lv == 6, [ptb, Ub], [PS[2]])
                            O.CP('act', Ub.t[:], PS[2].t[:, 0:384], [PS[2]], [Ub])
                            if lv < 6:
                                for hg in range(2):
                                    ps = PS[4 + hg]
                                    for hh in range(3):
                                        h = hg * 3 + hh
                                        O.MM(ps.t[:, hh * 128:(hh + 1) * 128], PP(h), PT(h), True, True, [ptb, ppb], [ps])
                                    O.CP('dve', PTs[(lv + 1) % 2].t[:, hg * 3:hg * 3 + 3, :], ps.t[:, 0:384].rearrange("p (a t) -> p a t", a=3),
                                         [ps], [PTs[(lv + 1) % 2]])
                                if lv < 5:
                                    for hg in range(2):
                                        ps = PS[6 + hg]
                                        for hh in range(3):
                                            h = hg * 3 + hh
                                            O.MM(ps.t[:, hh * 128:(hh + 1) * 128], PT(h), PP(h), True, True, [ptb, ppb], [ps])
                                        O.CP('dve' if hg == 0 else 'act', Psq[(lv + 1) % 2].t[:, hg * 3:hg * 3 + 3, :],
                                             ps.t[:, 0:384].rearrange("p (a t) -> p a t", a=3), [ps], [Psq[(lv + 1) % 2]])
                    first = True
                    for p in range(3):
                        O.MM(PS[3].t[:, p * 128:(p + 1) * 128], KR[p].t[:, c, 1, :], Twb.t[:, p, :], first, False, [KR[p], Twb], [PS[3]])
                        first = False
                    for h in range(6):
                        if rw:
                            O.MM(PS[3].t[:, h * 64:(h + 1) * 64], LMC.t[:, h, 0, 128:256], Ub.t[:, h * 64:(h + 1) * 64], False, False, [LMC, Ub], [PS[3]])
                        O.MM(PS[3].t[:, h * 64:(h + 1) * 64], LMC.t[:, h, 1, 128:256], V.t[:, c, h * 64:(h + 1) * 64], False, h == 5, [LMC, V], [PS[3]])
                    if fwd:
                        O.CP('dve', o_sb.t[:], PS[3].t[:, 0:384], [PS[3]], [o_sb])
                        P.dma('sp', Of.t.ap()[ch], o_sb.t[:], reads=[o_sb], writes=[Of])
                    else:
                        P.dma('sp', of_sb.t[:], Of.t.ap()[ch], reads=[Of], writes=[of_sb])
                        O.TT('dve', o_sb.t[:], PS[3].t[:, 0:384], of_sb.t[:], ALU.add, [PS[3], of_sb], [o_sb])
                    idf = cst.t[:, C_IDENT:C_IDENT + 128]
                    for p in range(3):
                        if rw:
                            O.TR(PS[6].t[:, p * 128:(p + 1) * 128], Bgp[p].t[:, cs], idf, [Bgp[p], cst], [PS[6]])
                        O.TR(PS[0].t[:, p * 128:(p + 1) * 128], Kgp[p].t[:, cs], idf, [Kgp[p], cst], [PS[0]])
                    if rw:
                        O.CP('dve', BKT.t[:, 0:3, :], PS[6].t[:, 0:384].rearrange("p (a t) -> p a t", a=3), [PS[6]], [BKT])
                    O.CP('act', BKT.t[:, 3:6, :], PS[0].t[:, 0:384].rearrange("p (a t) -> p a t", a=3), [PS[0]], [BKT])
                    first = True
                    for p in range(3):
                        if rw:
                            O.MM(PS[7].t[:, p * 128:(p + 1) * 128], BKT.t[:, p, :], Ub.t[:, p * 128:(p + 1) * 128], first, False, [BKT, Ub], [PS[7]])
                            first = False
                        O.MM(PS[7].t[:, p * 128:(p + 1) * 128], BKT.t[:, 3 + p, :], V.t[:, c, p * 128:(p + 1) * 128], first, p == 2, [BKT, V], [PS[7]])
                        first = False
                    O.TT('dve', ttmp.t[:], PS[7].t[:, 0:384].rearrange("p (a t) -> p a t", a=3), bdm[:, None, :].to_broadcast([128, 3, 128]), ALU.mult,
                         [PS[7], cst], [ttmp])
                    for p in range(3):
                        O.STT(Tw.t[:, p, :], Tw.t[:, p, :], GC.t[:, p, c:c + 1], ttmp.t[:, p, :], ALU.mult, ALU.add, [Tw, GC, ttmp], [Tw])
                    O.CP('act', Twb.t[:], Tw.t[:], [Tw], [Twb])
                    if Prog.limit is not None:
                        print("MS chunk", d, bi, c, P.n_ops)
                    if not fwd:
                        o3 = o_sb.t[:].rearrange("p (h v) -> p h v", h=6)
                        if rw:
                            O.P.op('dve', lambda e: e.tensor_reduce(stt.t[:, 0, 0:6], o3, AX.X, ALU.add), [o_sb], [stt])
                        O.TT('pool', sq_sb.t[:], o_sb.t[:], o_sb.t[:], ALU.mult, [o_sb], [sq_sb])
                        O.P.op('dve', lambda e: e.tensor_reduce(stt.t[:, 1, 0:6], sq_sb.t[:].rearrange("p (h v) -> p h v", h=6), AX.X, ALU.add), [sq_sb], [stt])
                        if rw:
                            O.TS('dve', stt.t[:, 2, 0:6], stt.t[:, 0, 0:6], 1.0 / 64, None, ALU.mult, None, [stt], [stt])
                            O.TT('dve', stt.t[:, 3, 0:6], stt.t[:, 2, 0:6], stt.t[:, 2, 0:6], ALU.mult, [stt], [stt])
                            O.STT(stt.t[:, 4, 0:6], stt.t[:, 1, 0:6], 1.0 / 64, stt.t[:, 3, 0:6], ALU.mult, ALU.subtract, [stt], [stt])
                            O.TS('dve', stt.t[:, 4, 0:6], stt.t[:, 4, 0:6], RW_GN_EPS, None, ALU.add, None, [stt], [stt])
                        else:
                            O.TS('dve', stt.t[:, 4, 0:6], stt.t[:, 1, 0:6], 1.0 / 64, LN_EPS, ALU.mult, ALU.add, [stt], [stt])
                        O.ACT(stt.t[:, 5, 0:6], stt.t[:, 4, 0:6], AF.Sqrt, [stt], [stt])
                        O.P.op('dve', lambda e: e.reciprocal(stt.t[:, 6, 0:6], stt.t[:, 5, 0:6]), [stt], [stt])
                        y3 = y_sb.t[:].rearrange("p (h v) -> p h v", h=6)
                        rstd_b = stt.t[:, 6, 0:6][:, :, None].to_broadcast([128, 6, 64])
                        psg = PS[0]
                        if rw:
                            O.MM(psg.t[:, 0:384], sgd.t[:, cs], wg2.t[:], True, True, [sgd, wg2], [psg])
                            mean_b = stt.t[:, 2, 0:6][:, :, None].to_broadcast([128, 6, 64])
                            O.TT('dve', y3, o3, mean_b, ALU.subtract, [o_sb, stt], [y_sb])
                            O.TT('dve', y3, y3, rstd_b, ALU.mult, [y_sb, stt], [y_sb])
                            O.TT('pool', y_sb.t[:], y_sb.t[:], gnw, ALU.mult, [y_sb, prw], [y_sb])
                            O.TT('pool', y_sb.t[:], y_sb.t[:], gnb, ALU.add, [y_sb, prw], [y_sb])
                            coef_b = COEF.t[:, ch, 0:6][:, :, None].to_broadcast([128, 6, 64])
                            O.TT('dve', sq_sb.t[:].rearrange("p (h v) -> p h v", h=6), V.t[:, c, :].rearrange("p (h v) -> p h v", h=6), coef_b, ALU.mult,
                                 [V, COEF], [sq_sb])
                            O.TT('pool', y_sb.t[:], y_sb.t[:], sq_sb.t[:], ALU.add, [y_sb, sq_sb], [y_sb])
                            O.TT('dve', yb.t[:], psg.t[:, 0:384], y_sb.t[:], ALU.mult, [y_sb, psg], [yb])
                        else:
                            for k in range(8):
                                O.MM(psg.t[:, 0:384], lhs_tok(k, c), Wgt.t[:, k, :], k == 0, k == 7, [xT, Wgt], [psg])
                            O.ACT(gt_sb.t[:], psg.t[:, 0:384], AF.Silu, [psg], [gt_sb])
                            O.TT('dve', y3, o3, rstd_b, ALU.mult, [o_sb, stt], [y_sb])
                            O.TT('pool', y_sb.t[:], y_sb.t[:], nrm, ALU.mult, [y_sb, prw], [y_sb])
                            O.TT('pool', yb.t[:], y_sb.t[:], gt_sb.t[:], ALU.mult, [y_sb, gt_sb], [yb])
                        for p in range(3):
                            O.TR(PS[1].t[:, p * 128:(p + 1) * 128], yb.t[:, p * 128:(p + 1) * 128], cst.t[:, C_IDENT:C_IDENT + 128], [yb, cst], [PS[1]])
                        O.CP('act', yTo.t[:, :, cs], PS[1].t[:, 0:384].rearrange("p (a t) -> p a t", a=3), [PS[1]], [yTo])
                if not fwd:
                    for p in range(3):
                        P.dma('sp', self.YT.t.ap()[feat0 + p * 128:feat0 + (p + 1) * 128, t0:t0 + CB], yTo.t[:, p, :], reads=[yTo], writes=[self.YT])
        P.barrier()
        P.stack = old


def _scan_gates(self, O, f, scm, fwd, cend, GC, p, rate):
    sg, css = f["sg"], f["css"]
    if fwd:
        O.P.op('dve', lambda e: e.tensor_tensor_scan(css.t[:], scm, sg.t[:], 0.0, ALU.mult, ALU.add), [sg, self.cst_sb], [css])
    else:
        O.P.op('dve', lambda e: e.tensor_tensor_scan(css.t[:, ::-1], scm[:, ::-1], sg.t[:, ::-1], 0.0, ALU.mult, ALU.add), [sg, self.cst_sb], [css])
    O.ACT(f["g"].t[:], css.t[:], AF.Exp, [css], [f["g"]], scale=-rate)
    O.ACT(f["gi"].t[:], css.t[:], AF.Exp, [css], [f["gi"]], scale=rate)
    if "gex" in f:
        O.TT('pool', f["t2"].t[:], css.t[:], sg.t[:], ALU.subtract, [css, sg], [f["t2"]])
        O.ACT(f["gex"].t[:], f["t2"].t[:], AF.Exp, [f["t2"]], [f["gex"]], scale=-rate)
    c3 = css.t[:].rearrange("p (c t) -> p c t", c=2)
    cC = c3[:, :, cend:cend + 1]
    O.TT('dve', f["t2"].t[:].rearrange("p (c t) -> p c t", c=2), cC.to_broadcast([128, 2, 128]), c3, ALU.subtract, [css], [f["t2"]])
    O.ACT(f["gp"].t[:], f["t2"].t[:], AF.Exp, [f["t2"]], [f["gp"]], scale=-rate)
    O.ACT(GC.t[:, p, :], css.t[:, cend:256:128], AF.Exp, [css], [GC], scale=-rate)


Kern.ph_scan = _ph_scan
Kern._scan_gates = _scan_gates


def build_full(nb=NB):
    K = Kern(nb=nb)
    P = K.P
    for l in range(DEPTH):
        K.set_layer(l)
        K.ph_mod(l)
    for b in range(nb):
        for l in range(DEPTH):
            K.set_layer(l)
            P.barrier()
            with contextlib.ExitStack() as st:
                P.stack, old = st, P.stack
                xT, dxT = K.ph_x(b, l, need_dx=True)
                K.ph_nat(b, l, xT)
                K.ph_scan(b, l, xT, dxT, 'gla')
                K.ph_scan(b, l, xT, dxT, 'rw')
                P.barrier()
                P.stack = old
            K.ph_wout_ln1(b, l)
            K.ph_ffn_ln2(b, l)
    P.finish()
    return K


_CACHE = {}


def kernel(**inputs):
    n_cores = 8
    if "K" not in _CACHE:
        _CACHE["K"] = build_full(NB)
    K = _CACHE["K"]
    in_maps = [host_inputs(inputs, c, NB) for c in range(n_cores)]
    res = run_bass_kernel_spmd(K.nc, in_maps, core_ids=list(range(n_cores)))
    out = np.concatenate([np.asarray(r["y"]) for r in res.results], axis=0)
    return out.astype(np.float32)
```

```python
import contextlib
import numpy as np
import concourse.bass as bass
import concourse.mybir as mybir
from concourse.bass_utils import run_bass_kernel_spmd

F32 = mybir.dt.float32
BF16 = mybir.dt.bfloat16
AF = mybir.ActivationFunctionType
ALU = mybir.AluOpType
AX = mybir.AxisListType

ENGS = ('pe', 'dve', 'act', 'pool', 'sp')


class Buf:
    def __init__(self, t, name):
        self.t = t
        self.name = name
        self.last_write = None
        self.reads = {}


class Prog:
    SAME_ENGINE_SYNC = True

    def __init__(self, nc, n_dma_sems=12):
        self.nc = nc
        self.stack = contextlib.ExitStack()
        self.ops = {e: [] for e in ENGS}
        self.sem = {}
        for e in ('pe', 'dve', 'act', 'pool'):
            self.sem[e] = self.stack.enter_context(nc.semaphore("s_" + e))
        self.cnt = {e: 0 for e in ('pe', 'dve', 'act', 'pool')}
        self.seen = {e: {} for e in ENGS}
        self.dsems = {}
        self.dcur = {}
        for q in ('sp', 'pool', 'act'):
            self.dsems[q] = []
            for i in range(n_dma_sems):
                key = "d_%s_%d" % (q, i)
                self.sem[key] = self.stack.enter_context(nc.semaphore(key))
                self.dsems[q].append([key, 0])
            self.dcur[q] = 0
        self.n_ops = 0

    _uid = 0

    def sbuf(self, name, shape, dtype):
        Prog._uid += 1
        name = "%s_u%d" % (name, Prog._uid)
        t = self.stack.enter_context(self.nc.sbuf_tensor(name, list(shape), dtype))
        return Buf(t, name)

    def psum(self, name, shape, dtype):
        t = self.stack.enter_context(self.nc.psum_tensor(name, list(shape), dtype))
        return Buf(t, name)

    def view(self, buf, name=None):
        return Buf(buf.t, name or buf.name)

    def _needs(self, reads, writes):
        need = {}

        def add(tok):
            if tok is None:
                return
            k, v = tok
            if need.get(k, 0) < v:
                need[k] = v
        for b in reads:
            add(b.last_write)
        for b in writes:
            add(b.last_write)
            for k, v in b.reads.items():
                add((k, v))
        return need

    def _waits(self, eng, need):
        waits = []
        for k, v in need.items():
            if k == eng and not (self.SAME_ENGINE_SYNC and eng != 'pe'):
                continue
            if self.seen[eng].get(k, 0) >= v:
                continue
            self.seen[eng][k] = v
            waits.append((k, v))
        return waits

    limit = None
    trace_range = None

    def op(self, eng, fn, reads=(), writes=()):
        if self.limit is not None and self.n_ops >= self.limit:
            return
        if self.trace_range and self.trace_range[0] <= self.n_ops < self.trace_range[1]:
            import inspect
            fr = inspect.stack()
            print("OP", self.n_ops, eng, [f.lineno for f in fr[1:4]])
        need = self._needs(reads, writes)
        waits = self._waits(eng, need)
        self.cnt[eng] += 1
        c = self.cnt[eng]
        self.ops[eng].append((waits, fn, (eng, 1)))
        tok = (eng, c)
        for b in reads:
            b.reads[eng] = c
        for b in writes:
            b.last_write = tok
            b.reads = {}
        self.n_ops += 1

    def dma(self, q, out_ap, in_ap, reads=(), writes=(), **kw):
        if self.limit is not None and self.n_ops >= self.limit and not kw.pop("force", False):
            return
        kw.pop("force", None)
        need = self._needs(reads, writes)
        slot = self.dsems[q][self.dcur[q]]
        self.dcur[q] = (self.dcur[q] + 1) % len(self.dsems[q])
        key, val = slot
        if val > 0:
            if need.get(key, 0) < val:
                need[key] = val
        waits = self._waits(q, need)
        slot[1] = val + 16
        tok = (key, val + 16)
        self.ops[q].append((waits, lambda e: e.dma_start(out=out_ap, in_=in_ap, **kw), (key, 16)))
        for b in reads:
            b.reads[key] = val + 16
        for b in writes:
            b.last_write = tok
            b.reads = {}
        self.n_ops += 1

    def barrier(self):
        need = {e: c for e, c in self.cnt.items() if c > 0}
        for q in self.dsems:
            for key, val in self.dsems[q]:
                if val > 0:
                    need[key] = val
        for eng in ENGS:
            waits = self._waits(eng, dict(need))
            if waits:
                self.ops[eng].append((waits, None, None))

    def finish(self):
        self.barrier()
        nc = self.nc
        sem = self.sem
        ops = self.ops

        def replay(eng_name):
            def run(e):
                for waits, fn, inc in ops[eng_name]:
                    for k, v in waits:
                        e.wait_ge(sem[k], v)
                    if fn is not None:
                        ins = fn(e)
                        ins.then_inc(sem[inc[0]], inc[1])
            return run

        with nc.Block() as block:
            block.tensor(replay('pe'))
            block.vector(replay('dve'))
            block.scalar(replay('act'))
            block.gpsimd(replay('pool'))
            block.sync(replay('sp'))
        self.stack.close()


D = 1024
NCTX = 256
LAT = 2048
S = NCTX + LAT
NT = S // 128
NB = 4
DEPTH = 2
HID = 2816
NHC = HID // 128
IN_COLS = 3488
GRID_W = 64
ALPHA = (2.0 * DEPTH) ** 0.25
LN_EPS = 1e-5
RW_GN_EPS = 64e-5
DECAY = float(np.exp(-0.5))
CB = 256
NBLK = S // CB
GQ, GK, GV, GG, GDN = 0, 192, 384, 768, 1152
NQ, NK, NV = 1184, 1440, 1696
RR, RK, RV, RDD, RAD, RGD = 1952, 2336, 2720, 3104, 3232, 3360
NEG = -30000.0

PC_W0 = 0
PC_A0 = 6
PC_KK = 12
PC_KA = 15
PC_RK = 18
PC_GB = 21
PC_BMOD = 27
NPC = 75
PR_LN1W, PR_LN1B, PR_LN2W, PR_LN2B = 0, 1024, 2048, 3072
PR_GNORM = 4096
PR_GNW = 4480
PR_GNB = 4864
PR_MU = 5248
NPR = 6784
C_IDENT = 0
C_MASKF = 128
C_MASKB = 384
C_LMF = 640
C_LMB = 768
C_SCF = 896
C_SCB = 1152
C_BONES = 1408
C_BD = 1536
C_HSEL = 1664
C_ONE = 1666
C_E12 = 1667
NCST = 1668


def _host_constants():
    c = np.zeros((128, NCST), np.float32)
    i = np.arange(128)
    c[:, C_IDENT:C_IDENT + 128] = np.eye(128, dtype=np.float32)
    j, t = i[:, None], i[None, :]
    c[:, C_MASKF:C_MASKF + 128] = (t > j)
    c[:, C_MASKF + 128:C_MASKF + 256] = (t >= j)
    c[:, C_MASKB:C_MASKB + 128] = (t < j)
    c[:, C_MASKB + 128:C_MASKB + 256] = (t <= j)
    c[:, C_LMF:C_LMF + 128] = (i[None, :] < i[:, None])
    c[:, C_LMB:C_LMB + 128] = (i[None, :] > i[:, None])
    sc = np.ones((256,), np.float32); sc[0] = 0; sc[128] = 0
    c[:, C_SCF:C_SCF + 256] = sc[None]
    sc = np.ones((256,), np.float32); sc[127] = 0; sc[255] = 0
    c[:, C_SCB:C_SCB + 256] = sc[None]
    blk = (i[:, None] // 64 == i[None, :] // 64).astype(np.float32)
    c[:, C_BONES:C_BONES + 128] = blk
    c[:, C_BD:C_BD + 128] = blk
    c[:, C_HSEL] = (i < 64)
    c[:, C_HSEL + 1] = (i >= 64)
    c[:, C_ONE] = 1.0
    c[:, C_E12] = 1e-12
    return c


def _rope_tables():
    tt = np.arange(LAT)
    row = (tt // GRID_W).astype(np.float32)
    col = (tt % GRID_W).astype(np.float32)
    n_freq = 8
    inv = (10000.0 ** (-np.arange(n_freq, dtype=np.float32) / n_freq)).astype(np.float32)
    ang = np.concatenate([row[:, None] * inv, col[:, None] * inv], axis=-1).astype(np.float32)
    cos, sin = np.cos(ang).astype(np.float32), np.sin(ang).astype(np.float32)
    C = np.zeros((128, LAT), np.float32)
    Sn = np.zeros((128, LAT), np.float32)
    for e in range(2):
        for kp in range(32):
            C[e * 64 + kp] = cos[:, kp // 2]
            Sn[e * 64 + kp] = sin[:, kp // 2]
    return C, Sn


def _nat_tables(rpb):
    cq = np.arange(GRID_W)
    col0 = np.clip(cq - 8, 0, GRID_W - 16)
    in_win = (cq[None, :] >= col0[:, None]) & (cq[None, :] < col0[:, None] + 16)
    dc = np.clip(cq[None, :] - cq[:, None], -15, 15) + 15
    out = np.full((DEPTH, 4, 128, 14, 64), NEG, np.float32)
    for e in range(2):
        for idx in range(14):
            g = rpb[:, :, idx + e, :][:, :, dc]
            g = np.where(in_win[None, None], g, np.float32(NEG))
            out[:, :, e * 64:(e + 1) * 64, idx, :] = np.transpose(g, (0, 1, 3, 2))
    return out.reshape(DEPTH, 4, 128, 14 * 64)


def _vec3(v):
    return np.ascontiguousarray(v.reshape(3, 128).T)


def _pad_gla(v):
    o = np.zeros((3, 2, 64), np.float32)
    o[:, :, :32] = v.reshape(3, 2, 32)
    return np.ascontiguousarray(o.reshape(3, 128).T)


class Kern:
    def __init__(self, nb=NB, depth=DEPTH, test=None):
        self.nb = nb
        self.depth = depth
        self.test = test
        nc = self.nc = bass.Bass("TRN2", target_bir_lowering=False)
        P = self.P = Prog(nc)

        def din(name, shape, dt=F32):
            return nc.dram_tensor(name, list(shape), dt, kind="ExternalInput").ap()
        self.x = din("x", [nb, LAT, D])
        self.ctx = din("ctx", [nb, NCTX, D])
        self.cc = din("cc", [NB + 1, D])
        self.w_mod = din("w_mod", [DEPTH, D, 6 * D])
        self.w_in = din("w_in", [DEPTH, D, IN_COLS])
        self.w_out = din("w_out", [DEPTH, D, D])
        self.w13 = din("ffn_w13", [DEPTH, D, 2 * HID])
        self.w2 = din("ffn_w2", [DEPTH, HID, D])
        self.pcol = din("pcol", [DEPTH, 128, NPC])
        self.prow = din("prow", [DEPTH, 1, NPR])
        self.cst = din("cst", [128, NCST])
        self.ropeC = din("ropeC", [128, LAT])
        self.ropeS = din("ropeS", [128, LAT])
        self.nattab = din("nattab", [DEPTH, 4, 128, 14 * 64])
        self.gup = din("gup", [DEPTH, 2, 16, 384])
        self.wd2 = din("wd2", [DEPTH, 128, 384])
        self.wa2 = din("wa2", [DEPTH, 128, 384])
        self.wg2 = din("wg2", [DEPTH, 128, 384])
        self.y = nc.dram_tensor("y", [nb, LAT, D], F32, kind="ExternalOutput").ap()
        self.XA = Buf(nc.dram_tensor("XA", [S, D], F32), "XA")
        self.XB = Buf(nc.dram_tensor("XB", [S, D], F32), "XB")
        if test == 'dense':
            self.YT = Buf(nc.dram_tensor("YT", [D, S], BF16, kind="ExternalInput"), "YT")
        else:
            self.YT = Buf(nc.dram_tensor("YT", [D, S], BF16), "YT")
        self.OF = Buf(nc.dram_tensor("OFs", [NT, 128, 384], F32), "OF")
        self.dbg = {}
        self.cst_sb = P.sbuf("cst_sb", [128, NCST], F32)
        P.dma('sp', self.cst_sb.t[:], self.cst, writes=[self.cst_sb])
        self.identb = P.sbuf("identb", [128, 128], BF16)
        P.op('dve', lambda e: e.tensor_copy(self.identb.t[:], self.cst_sb.t[:, C_IDENT:C_IDENT + 128]),
             reads=[self.cst_sb], writes=[self.identb])
        self.mTs = [P.sbuf("mT%d" % l, [128, 48, 8], F32) for l in range(DEPTH)]
        self.pcols = [P.sbuf("pcol_sb%d" % l, [128, NPC + 12], F32) for l in range(DEPTH)]
        self.set_layer(0)
        self.PS = [P.psum("psb%d" % i, [128, 512], F32) for i in range(8)]

    def set_layer(self, l):
        self.mT = self.mTs[l]
        self.pcol_sb = self.pcols[l]

    def ident(self):
        return self.cst_sb.t[:, C_IDENT:C_IDENT + 128]

    def dbg_out(self, name, shape, dt=F32):
        ap = self.nc.dram_tensor(name, list(shape), dt, kind="ExternalOutput").ap()
        self.dbg[name] = ap
        return ap

    def ph_mod(self, l):
        P, nc = self.P, self.nc
        P.barrier()
        pcol_sb, mT = self.pcol_sb, self.mT
        with contextlib.ExitStack() as st:
            P.stack, old = st, P.stack
            ccT = P.sbuf("ccT", [128, 8, 8], F32)
            sc = P.sbuf("scT", [128, 8, 8], F32)
            P.dma('sp', pcol_sb.t[:, 0:NPC], self.pcol[l], writes=[pcol_sb])
            P.op('dve', lambda e: e.memset(ccT.t[:], 0.0), writes=[ccT])
            for j in range(NB + 1):
                P.dma('sp', ccT.t[:, :, j:j + 1], self.cc[j].rearrange("(k p o) -> p k o", p=128, o=1), writes=[ccT],
                      allow_slow_non_contiguous=True)
            P.op('act', lambda e: e.activation(sc.t[:], ccT.t[:], AF.Silu), reads=[ccT], writes=[sc])
            P.op('dve', lambda e: e.tensor_scalar(pcol_sb.t[:, NPC:NPC + 3], pcol_sb.t[:, PC_KA:PC_KA + 3], -1.0, 1.0,
                                                  ALU.mult, ALU.add), reads=[pcol_sb], writes=[pcol_sb])
            P.op('dve', lambda e: e.tensor_scalar(pcol_sb.t[:, NPC + 3:NPC + 9], pcol_sb.t[:, PC_GB:PC_GB + 6], -1.0, None,
                                                  ALU.mult), reads=[pcol_sb], writes=[pcol_sb])
            wm = [P.sbuf("wm%d" % i, [128, 8, 512], F32) for i in range(2)]
            ps = self.PS[0]
            for cg in range(12):
                w = wm[cg % 2]
                P.dma('sp', w.t[:], self.w_mod[l][:, cg * 512:(cg + 1) * 512].rearrange("(k p) c -> p k c", p=128), writes=[w])
                first = True
                for c in range(4):
                    for k in range(8):
                        P.op('pe', lambda e, w=w, c=c, k=k, first=first: e.matmul(
                            ps.t[:, c * 8:c * 8 + 8], w.t[:, k, c * 128:(c + 1) * 128], sc.t[:, k, :], start=first, stop=(k == 7),
                            skip_group_check=True), reads=[w, sc], writes=[ps])
                        first = False
                for c in range(4):
                    ch = cg * 4 + c
                    isscale = (8 <= ch < 16) or (32 <= ch < 40)
                    P.op('dve', lambda e, c=c, ch=ch, isscale=isscale: e.tensor_scalar(
                        mT.t[:, ch, :], ps.t[:, c * 8:c * 8 + 8], pcol_sb.t[:, PC_BMOD + ch:PC_BMOD + ch + 1],
                        1.0 if isscale else 0.0, ALU.add, ALU.add), reads=[ps, pcol_sb], writes=[mT])
            P.barrier()
            P.stack = old

    def build_xT(self, src_rows, xT, ntok, shift_ch, scale_ch, jb, segs, xt_bufs):
        P = self.P
        gi = 0
        for (tok0, n, jcol) in segs:
            nt = n // 128
            xt = xt_bufs[gi % 2]
            gi += 1
            P.dma('sp', xt.t[:, 0:nt, :], src_rows(tok0, n).rearrange("(j p) d -> p j d", p=128), reads=self._src_reads, writes=[xt])
            for dc in range(8):
                ps = self.PS[dc % 2]
                for j in range(nt):
                    P.op('pe', lambda e, ps=ps, xt=xt, j=j, dc=dc: e.transpose(
                        ps.t[:, j * 128:(j + 1) * 128], xt.t[:, j, dc * 128:(dc + 1) * 128], self.ident()),
                        reads=[xt, self.cst_sb], writes=[ps])
                eng = 'dve' if dc % 2 == 0 else 'act'
                o = xT.t[:, dc, tok0:tok0 + n]
                sc_ap = self.mT.t[:, scale_ch + dc, jcol:jcol + 1]
                sh_ap = self.mT.t[:, shift_ch + dc, jcol:jcol + 1]
                if eng == 'dve':
                    P.op('dve', lambda e, o=o, ps=ps, n=n, sc_ap=sc_ap, sh_ap=sh_ap: e.tensor_scalar(
                        o, ps.t[:, 0:n], sc_ap, sh_ap, ALU.mult, ALU.add), reads=[ps, self.mT], writes=[xT])
                else:
                    P.op('act', lambda e, o=o, ps=ps, n=n, sc_ap=sc_ap, sh_ap=sh_ap: e.activation(
                        o, ps.t[:, 0:n], AF.Identity, bias=sh_ap, scale=sc_ap), reads=[ps, self.mT], writes=[xT])

    def seq_segs(self, b, with_ctx=True):
        segs = []
        if with_ctx:
            segs.append((0, NCTX, NB))
        for g in range(4):
            segs.append((NCTX + g * 512, 512, b))
        return segs

    def src_rows_fn(self, b, l):
        if l == 0:
            def f(tok0, n):
                if tok0 < NCTX:
                    return self.ctx[b][tok0:tok0 + n, :]
                return self.x[b][tok0 - NCTX:tok0 - NCTX + n, :]
            return f, []
        XB = self.XB

        def f2(tok0, n):
            return XB.t.ap()[tok0:tok0 + n, :]
        return f2, [XB]

    def ln_tail(self, pre, rows_w, rows_b, out_tile, tmp):
        P = self.P
        st = self.ln_st
        mv = self.ln_mv
        P.op('dve', lambda e: e.tensor_reduce(mv.t[:, 5:6], pre.t[:], AX.X, ALU.add), reads=[pre], writes=[mv])
        P.op('act', lambda e: e.activation(tmp.t[:], pre.t[:], AF.Square, accum_out=st.t[:, 0:1]), reads=[pre], writes=[tmp, st])
        P.op('dve', lambda e: e.tensor_scalar(mv.t[:, 0:1], mv.t[:, 5:6], 1.0 / D, None, ALU.mult), reads=[mv], writes=[mv])
        P.op('dve', lambda e: e.tensor_tensor(mv.t[:, 1:2], mv.t[:, 0:1], mv.t[:, 0:1], op=ALU.mult), reads=[mv], writes=[mv])
        P.op('dve', lambda e: e.scalar_tensor_tensor(mv.t[:, 2:3], st.t[:, 0:1], 1.0 / D, mv.t[:, 1:2], ALU.mult, ALU.subtract),
             reads=[mv, st], writes=[mv])
        P.op('dve', lambda e: e.tensor_scalar(mv.t[:, 2:3], mv.t[:, 2:3], LN_EPS, None, ALU.add), reads=[mv], writes=[mv])
        P.op('act', lambda e: e.activation(mv.t[:, 3:4], mv.t[:, 2:3], AF.Sqrt), reads=[mv], writes=[mv])
        P.op('dve', lambda e: e.reciprocal(mv.t[:, 4:5], mv.t[:, 3:4]), reads=[mv], writes=[mv])
        P.op('dve', lambda e: e.tensor_scalar(tmp.t[:], pre.t[:], mv.t[:, 0:1], mv.t[:, 4:5], ALU.subtract, ALU.mult),
             reads=[pre, mv], writes=[tmp])
        P.op('pool', lambda e: e.tensor_tensor(tmp.t[:], tmp.t[:], rows_w, op=ALU.mult), reads=[tmp, self.prow_sb], writes=[tmp])
        P.op('pool', lambda e: e.tensor_tensor(out_tile.t[:], tmp.t[:], rows_b, op=ALU.add), reads=[tmp, self.prow_sb], writes=[out_tile])

    def gate_bcast(self, gate_ch, jcol, gb):
        P = self.P
        dg = self.diag_tmp
        mT = self.mT
        ones_f = self.ones_f
        for c in range(8):
            ps = self.PS[2 + (c // 4)]
            P.op('dve', lambda e, c=c: e.tensor_scalar(dg.t[:], self.ident(), mT.t[:, gate_ch + c, jcol:jcol + 1], None, ALU.mult),
                 reads=[mT, self.cst_sb], writes=[dg])
            P.op('pe', lambda e, c=c, ps=ps: e.matmul(ps.t[:, (c % 4) * 128:(c % 4 + 1) * 128], ones_f.t[:], dg.t[:],
                                                      start=True, stop=True), reads=[dg, ones_f], writes=[ps])
            if c % 4 == 3:
                h = c // 4
                P.op('act', lambda e, ps=ps, h=h: e.activation(gb.t[:, h * 512:(h + 1) * 512], ps.t[:], AF.Identity),
                     reads=[ps], writes=[gb])

    def ph_wout_ln1(self, b, l, ntiles_from=0):
        P, nc = self.P, self.nc
        P.barrier()
        last = (l == self.depth - 1)
        src, src_reads = self.src_rows_fn(b, l)
        with contextlib.ExitStack() as st:
            P.stack, old = st, P.stack
            wo = P.sbuf("wo", [128, 8, D], BF16)
            for k in range(8):
                P.dma('pool', wo.t[:, k, :], self.w_out[l][k * 128:(k + 1) * 128, :], writes=[wo])
            self.prow_sb = P.sbuf("prow_sb", [128, 2048], F32)
            P.dma('sp', self.prow_sb.t[:], self.prow[l][:, PR_LN1W:PR_LN1W + 2048].partition_broadcast(128), writes=[self.prow_sb])
            self.ln_st = P.sbuf("ln_st", [128, 12], F32)
            self.ln_mv = P.sbuf("ln_mv", [128, 8], F32)
            self.diag_tmp = P.sbuf("diag_tmp", [128, 128], F32)
            self.ones_f = ones_f_ = P.sbuf("ones_f", [128, 128], F32)
            P.op('pool', lambda e: e.memset(ones_f_.t[:], 1.0), writes=[ones_f_])
            gbl = P.sbuf("gbl", [128, D], F32)
            gbc = P.sbuf("gbc", [128, D], F32)
            self.gate_bcast(16, b, gbl)
            self.gate_bcast(16, NB, gbc)
            yt = [P.sbuf("yt%d" % i, [128, 8, 128], BF16) for i in range(2)]
            xr = [P.sbuf("xr%d" % i, [128, D], F32) for i in range(2)]
            pre = [P.sbuf("pre%d" % i, [128, D], F32) for i in range(2)]
            tmp = P.sbuf("lntmp", [128, D], F32)
            ot = [P.sbuf("ot%d" % i, [128, D], F32) for i in range(2)]
            t0 = 2 if last else 0
            for ti in range(t0, NT):
                i2 = ti % 2
                gb = gbc if ti < 2 else gbl
                P.dma('sp', yt[i2].t[:], self.YT.t.ap()[:, ti * 128:(ti + 1) * 128].rearrange("(k p) t -> p k t", p=128),
                      reads=[self.YT], writes=[yt[i2]])
                P.dma('sp', xr[i2].t[:], src(ti * 128, 128), reads=src_reads, writes=[xr[i2]])
                for h in range(2):
                    ps = self.PS[4 + h]
                    for k in range(8):
                        P.op('pe', lambda e, ps=ps, k=k, h=h, i2=i2: e.matmul(ps.t[:], yt[i2].t[:, k, :], wo.t[:, k, h * 512:(h + 1) * 512],
                                                                           start=(k == 0), stop=(k == 7)), reads=[yt[i2], wo], writes=[ps])
                    P.op('dve', lambda e, ps=ps, h=h, i2=i2, gb=gb: e.tensor_tensor(pre[i2].t[:, h * 512:(h + 1) * 512], ps.t[:],
                                                                             gb.t[:, h * 512:(h + 1) * 512], op=ALU.mult),
                         reads=[ps, gb], writes=[pre[i2]])
                P.op('dve', lambda e, i2=i2: e.scalar_tensor_tensor(pre[i2].t[:], xr[i2].t[:], ALPHA, pre[i2].t[:], ALU.mult, ALU.add),
                     reads=[xr[i2], pre[i2]], writes=[pre[i2]])
                self.ln_tail(pre[i2], self.prow_sb.t[:, 0:1024], self.prow_sb.t[:, 1024:2048], ot[i2], tmp)
                P.dma('pool', self.XA.t.ap()[ti * 128:(ti + 1) * 128, :], ot[i2].t[:], reads=[ot[i2]], writes=[self.XA])
            P.barrier()
            P.stack = old

    def ph_ffn_ln2(self, b, l):
        P, nc = self.P, self.nc
        P.barrier()
        last = (l == self.depth - 1)
        tok_lo = NCTX if last else 0
        XA = self.XA
        with contextlib.ExitStack() as st:
            P.stack, old = st, P.stack
            hT = P.sbuf("hT", [128, NHC, S], BF16)
            with contextlib.ExitStack() as st2:
                P.stack = st2
                xT = P.sbuf("x2T", [128, 8, S], BF16)
                xtb = [P.sbuf("xtb%d" % i, [128, 4, D], F32) for i in range(2)]
                self._src_reads = [XA]
                segs = self.seq_segs(b, with_ctx=not last)
                self.build_xT(lambda tok0, n: XA.t.ap()[tok0:tok0 + n, :], xT, S, 24, 32, b, segs, xtb)
                wg = [P.sbuf("wg%d" % i, [128, 8, 128], BF16) for i in range(2)]
                wu = [P.sbuf("wu%d" % i, [128, 8, 128], BF16) for i in range(2)]
                sg = [P.sbuf("sg%d" % i, [128, 512], F32) for i in range(2)]
                blocks = [(t, min(512, S - t)) for t in range(tok_lo, S, 512)]
                for hc in range(NHC):
                    i2 = hc % 2
                    P.dma('pool', wg[i2].t[:], self.w13[l][:, hc * 128:(hc + 1) * 128].rearrange("(k p) c -> p k c", p=128), writes=[wg[i2]])
                    P.dma('pool', wu[i2].t[:], self.w13[l][:, HID + hc * 128:HID + (hc + 1) * 128].rearrange("(k p) c -> p k c", p=128),
                          writes=[wu[i2]])
                    for bi, (t0, n) in enumerate(blocks):
                        pg = self.PS[(bi % 2) * 2]
                        pu = self.PS[(bi % 2) * 2 + 1]
                        for k in range(8):
                            P.op('pe', lambda e, pg=pg, k=k, i2=i2, t0=t0, n=n: e.matmul(pg.t[:, 0:n], wg[i2].t[:, k, :], xT.t[:, k, t0:t0 + n],
                                                                                     start=(k == 0), stop=(k == 7)), reads=[wg[i2], xT], writes=[pg])
                        for k in range(8):
                            P.op('pe', lambda e, pu=pu, k=k, i2=i2, t0=t0, n=n: e.matmul(pu.t[:, 0:n], wu[i2].t[:, k, :], xT.t[:, k, t0:t0 + n],
                                                                                     start=(k == 0), stop=(k == 7)), reads=[wu[i2], xT], writes=[pu])
                        s = sg[bi % 2]
                        P.op('act', lambda e, s=s, pg=pg, n=n: e.activation(s.t[:, 0:n], pg.t[:, 0:n], AF.Silu), reads=[pg], writes=[s])
                        P.op('dve', lambda e, s=s, pu=pu, n=n, hc=hc, t0=t0: e.tensor_tensor(hT.t[:, hc, t0:t0 + n], s.t[:, 0:n], pu.t[:, 0:n],
                                                                                       op=ALU.mult), reads=[s, pu], writes=[hT])
                P.barrier()
            P.stack = st
            w2 = P.sbuf("w2", [128, NHC, D], BF16)
            for hc in range(NHC):
                P.dma('pool', w2.t[:, hc, :], self.w2[l][hc * 128:(hc + 1) * 128, :], writes=[w2])
            self.prow_sb = P.sbuf("prow_sb2", [128, 2048], F32)
            P.dma('sp', self.prow_sb.t[:], self.prow[l][:, PR_LN2W:PR_LN2W + 2048].partition_broadcast(128), writes=[self.prow_sb])
            self.ln_st = P.sbuf("ln_st2", [128, 12], F32)
            self.ln_mv = P.sbuf("ln_mv2", [128, 8], F32)
            self.diag_tmp = P.sbuf("diag_tmp2", [128, 128], F32)
            self.ones_f = ones_f_ = P.sbuf("ones_f2", [128, 128], F32)
            P.op('pool', lambda e: e.memset(ones_f_.t[:], 1.0), writes=[ones_f_])
            gbl = P.sbuf("gbl2", [128, D], F32)
            gbc = P.sbuf("gbc2", [128, D], F32)
            self.gate_bcast(40, b, gbl)
            self.gate_bcast(40, NB, gbc)
            xr = [P.sbuf("xr2%d" % i, [128, D], F32) for i in range(2)]
            pre = [P.sbuf("pre2%d" % i, [128, D], F32) for i in range(2)]
            tmp = P.sbuf("lntmp2", [128, D], F32)
            ot = [P.sbuf("ot2%d" % i, [128, D], F32) for i in range(2)]
            for ti in range(2 if last else 0, NT):
                i2 = ti % 2
                gb = gbc if ti < 2 else gbl
                P.dma('sp', xr[i2].t[:], XA.t.ap()[ti * 128:(ti + 1) * 128, :], reads=[XA], writes=[xr[i2]])
                for h in range(2):
                    ps = self.PS[4 + h]
                    for hc in range(NHC):
                        P.op('pe', lambda e, ps=ps, hc=hc, h=h, ti=ti: e.matmul(ps.t[:], hT.t[:, hc, ti * 128:(ti + 1) * 128],
                                                                             w2.t[:, hc, h * 512:(h + 1) * 512], start=(hc == 0), stop=(hc == NHC - 1)),
                             reads=[hT, w2], writes=[ps])
                    P.op('dve', lambda e, ps=ps, h=h, i2=i2, gb=gb: e.tensor_tensor(pre[i2].t[:, h * 512:(h + 1) * 512], ps.t[:],
                                                                             gb.t[:, h * 512:(h + 1) * 512], op=ALU.mult),
                         reads=[ps, gb], writes=[pre[i2]])
                P.op('dve', lambda e, i2=i2: e.scalar_tensor_tensor(pre[i2].t[:], xr[i2].t[:], ALPHA, pre[i2].t[:], ALU.mult, ALU.add),
                     reads=[xr[i2], pre[i2]], writes=[pre[i2]])
                self.ln_tail(pre[i2], self.prow_sb.t[:, 0:1024], self.prow_sb.t[:, 1024:2048], ot[i2], tmp)
                if last:
                    P.dma('pool', self.y[b][(ti - 2) * 128:(ti - 1) * 128, :], ot[i2].t[:], reads=[ot[i2]])
                else:
                    P.dma('pool', self.XB.t.ap()[ti * 128:(ti + 1) * 128, :], ot[i2].t[:], reads=[ot[i2]], writes=[self.XB])
            P.barrier()
            P.stack = old

    def dump(self, buf, name, shape, dt=F32):
        ap = self.dbg_out(name, shape, dt)
        self.P.dma('sp', ap, buf.t.ap(), reads=[buf])


def host_inputs(inputs, core, nb=NB):
    f = lambda a: np.ascontiguousarray(np.asarray(a, dtype=np.float32))
    b0 = core * nb
    m = {}
    m["x"] = f(inputs["x"][b0:b0 + nb])
    m["ctx"] = f(inputs["ctx"][b0:b0 + nb])
    cc = np.zeros((NB + 1, D), np.float32)
    cc[:nb] = inputs["c"][b0:b0 + nb]
    cc[NB] = inputs["c_ctx"]
    m["cc"] = cc
    for k in ("w_mod", "w_in", "w_out", "ffn_w13", "ffn_w2"):
        m[k] = f(inputs[k])
    pcol = np.zeros((DEPTH, 128, NPC), np.float32)
    prow = np.zeros((DEPTH, 1, NPR), np.float32)
    gup = np.zeros((DEPTH, 2, 16, 3, 2, 64), np.float32)
    for l in range(DEPTH):
        for d in range(2):
            pcol[l, :, PC_W0 + d * 3:PC_W0 + d * 3 + 3] = _vec3(inputs["rw_w0"][l, d])
            pcol[l, :, PC_A0 + d * 3:PC_A0 + d * 3 + 3] = _vec3(inputs["rw_a0"][l, d])
            pcol[l, :, PC_GB + d * 3:PC_GB + d * 3 + 3] = _pad_gla(inputs["gla_gate_b"][l, d])
            gup[l, d, :, :, :, :32] = inputs["gla_gate_up"][l, d].reshape(16, 3, 2, 32)
        pcol[l, :, PC_KK:PC_KK + 3] = _vec3(inputs["rw_k_k"][l])
        pcol[l, :, PC_KA:PC_KA + 3] = _vec3(inputs["rw_k_a"][l])
        pcol[l, :, PC_RK:PC_RK + 3] = _vec3(inputs["rw_r_k"][l])
        pcol[l, :, PC_BMOD:PC_BMOD + 48] = inputs["b_mod"][l].reshape(48, 128).T
        prow[l, 0, PR_LN1W:PR_LN1W + 1024] = inputs["ln1_w"][l]
        prow[l, 0, PR_LN1B:PR_LN1B + 1024] = inputs["ln1_b"][l]
        prow[l, 0, PR_LN2W:PR_LN2W + 1024] = inputs["ln2_w"][l]
        prow[l, 0, PR_LN2B:PR_LN2B + 1024] = inputs["ln2_b"][l]
        prow[l, 0, PR_GNORM:PR_GNORM + 384] = np.tile(inputs["gla_norm_w"][l], 6)
        prow[l, 0, PR_GNW:PR_GNW + 384] = inputs["rw_gn_w"][l]
        prow[l, 0, PR_GNB:PR_GNB + 384] = inputs["rw_gn_b"][l]
        prow[l, 0, PR_MU:PR_MU + 1536] = inputs["rw_mu"][l]
    m["pcol"] = pcol
    m["prow"] = prow
    m["gup"] = np.ascontiguousarray(gup.reshape(DEPTH, 2, 16, 384))
    m["cst"] = _host_constants()
    C, Sn = _rope_tables()
    m["ropeC"], m["ropeS"] = C, Sn
    m["nattab"] = _nat_tables(np.asarray(inputs["nat_rpb"], np.float32))
    m["wd2"] = f(inputs["rw_wd2"]).reshape(DEPTH, 128, 384)
    m["wa2"] = f(inputs["rw_wa2"]).reshape(DEPTH, 128, 384)
    m["wg2"] = f(inputs["rw_wg2"])
    return m


def _ph_x(self, b, l, need_dx=True):
    P = self.P
    xT = P.sbuf("xmodT", [128, 8, S], BF16)
    dxT = P.sbuf("dxT", [128, 8, S], BF16) if need_dx else None
    outer = P.stack
    with contextlib.ExitStack() as st:
        P.stack = st
        xtb = [P.sbuf("xtb%d" % i, [128, 4, D], F32) for i in range(2)]
        src, src_reads = self.src_rows_fn(b, l)
        self._src_reads = src_reads
        self.build_xT(src, xT, S, 0, 8, b, self.seq_segs(b), xtb)
        if need_dx:
            tmp = P.sbuf("dxtmp", [128, 2, LAT], F32)
            for (t0, n) in ((0, NCTX), (NCTX, LAT)):
                for c in range(0, 8, 2):
                    P.op('dve', lambda e, t0=t0, n=n, c=c: e.tensor_tensor(tmp.t[:, :, 0:n - 2], xT.t[:, c:c + 2, t0:t0 + n - 2],
                                                                        xT.t[:, c:c + 2, t0 + 2:t0 + n], op=ALU.add), reads=[xT], writes=[tmp])
                    P.op('dve', lambda e, t0=t0, n=n, c=c: e.scalar_tensor_tensor(dxT.t[:, c:c + 2, t0 + 1:t0 + n - 1], tmp.t[:, :, 0:n - 2], 0.5,
                                                                               xT.t[:, c:c + 2, t0 + 1:t0 + n - 1], ALU.mult, ALU.subtract),
                         reads=[tmp, xT], writes=[dxT])
                P.op('dve', lambda e, t0=t0: e.scalar_tensor_tensor(dxT.t[:, :, t0:t0 + 1], xT.t[:, :, t0 + 1:t0 + 2], 0.5, xT.t[:, :, t0:t0 + 1],
                                                                 ALU.mult, ALU.subtract), reads=[xT], writes=[dxT])
                P.op('dve', lambda e, t0=t0, n=n: e.scalar_tensor_tensor(dxT.t[:, :, t0 + n - 1:t0 + n], xT.t[:, :, t0 + n - 2:t0 + n - 1], 0.5,
                                                                      xT.t[:, :, t0 + n - 1:t0 + n], ALU.mult, ALU.subtract), reads=[xT], writes=[dxT])
        P.barrier()
    P.stack = outer
    return xT, dxT


Kern.ph_x = _ph_x


def _ph_nat(self, b, l, xT):
    P = self.P
    last = (l == self.depth - 1)
    with contextlib.ExitStack() as st:
        P.stack, old = st, P.stack
        qT = P.sbuf("n_qT", [128, 2, S], BF16)
        kT = P.sbuf("n_kT", [128, 2, S], BF16)
        V = P.sbuf("n_V", [128, NT, 256], BF16)
        Vs = P.sbuf("n_Vs", [128, 15, 256], BF16)
        yTn = P.sbuf("n_yT", [128, 2, S], BF16)
        tab = P.sbuf("n_tab", [128, 4, 14 * 64], F32)
        onesb = P.sbuf("n_ones", [128, 128], BF16)
        P.op('pool', lambda e: e.memset(onesb.t[:], 1.0), writes=[onesb])
        for h in range(4):
            P.dma('sp', tab.t[:, h, :], self.nattab[l, h], writes=[tab])
        wq = P.sbuf("n_wq", [128, 8, 256], BF16)
        wk = P.sbuf("n_wk", [128, 8, 256], BF16)
        wv = P.sbuf("n_wv", [128, 8, 256], BF16)
        for (w, c0) in ((wq, NQ), (wk, NK), (wv, NV)):
            P.dma('pool', w.t[:], self.w_in[l][:, c0:c0 + 256].rearrange("(k p) c -> p k c", p=128), writes=[w])
        blocks = [(t, min(512, S - t)) for t in range(0, S, 512)]
        n = 0
        for (w, dst) in ((wq, qT), (wk, kT)):
            for tl in range(2):
                for (t0, nn) in blocks:
                    ps = self.PS[n % 2]
                    for k in range(8):
                        P.op('pe', lambda e, ps=ps, w=w, tl=tl, k=k, t0=t0, nn=nn: e.matmul(
                            ps.t[:, 0:nn], w.t[:, k, tl * 128:(tl + 1) * 128], xT.t[:, k, t0:t0 + nn], start=(k == 0), stop=(k == 7)),
                            reads=[w, xT], writes=[ps])
                    if n % 2 == 0:
                        P.op('dve', lambda e, ps=ps, dst=dst, tl=tl, t0=t0, nn=nn: e.tensor_copy(dst.t[:, tl, t0:t0 + nn], ps.t[:, 0:nn]),
                             reads=[ps], writes=[dst])
                    else:
                        P.op('act', lambda e, ps=ps, dst=dst, tl=tl, t0=t0, nn=nn: e.activation(dst.t[:, tl, t0:t0 + nn], ps.t[:, 0:nn], AF.Identity),
                             reads=[ps], writes=[dst])
                    n += 1
        for (dst, nt, base) in ((V, NT, 0), (Vs, 15, NCTX + 64)):
            for ti in range(nt):
                ps = self.PS[2 + ti % 2]
                t0 = base + ti * 128
                for k in range(8):
                    P.op('pe', lambda e, ps=ps, k=k, t0=t0: e.matmul(ps.t[:, 0:256], xT.t[:, k, t0:t0 + 128], wv.t[:, k, :], start=(k == 0), stop=(k == 7)),
                         reads=[wv, xT], writes=[ps])
                if ti % 2 == 0:
                    P.op('dve', lambda e, ps=ps, dst=dst, ti=ti: e.tensor_copy(dst.t[:, ti, :], ps.t[:, 0:256]), reads=[ps], writes=[dst])
                else:
                    P.op('act', lambda e, ps=ps, dst=dst, ti=ti: e.activation(dst.t[:, ti, :], ps.t[:, 0:256], AF.Identity), reads=[ps], writes=[dst])
        stt = [P.sbuf("n_stt%d" % i, [128, 4, 64], F32) for i in range(2)]
        E = [P.sbuf("n_E%d" % i, [128, 6, 64], BF16) for i in range(2)]
        rB = P.sbuf("n_rB", [128, 2, 64], F32)
        it = 0
        for r in range(32):
            r0 = min(max(r - 4, 0), 24)
            off = r0 - r + 7
            q0 = NCTX + r * 64
            for hp in range(2):
                pso = self.PS[4 + (it % 2)]
                for e2 in range(2):
                    h = hp * 2 + e2
                    pb = e2 * 64
                    pss = self.PS[(it % 2) * 2 + e2]
                    Eh = E[e2]
                    for j in range(6):
                        k0 = (NCTX + r0 * 64 + j * 128) if j < 4 else (j - 4) * 128
                        P.op('pe', lambda e, pss=pss, j=j, pb=pb, hp=hp, k0=k0, q0=q0: e.matmul(
                            pss.t[:, j * 64:(j + 1) * 64], kT.t[pb:pb + 64, hp, k0:k0 + 128], qT.t[pb:pb + 64, hp, q0:q0 + 64], start=True, stop=True),
                            reads=[kT, qT], writes=[pss])
                    P.op('dve', lambda e, pss=pss, e2=e2, h=h, off=off: e.scalar_tensor_tensor(
                        stt[e2].t[:], pss.t[:, 0:256].rearrange("p (a b) -> p a b", a=4), 0.125,
                        tab.t[:, h, :].rearrange("p (a b) -> p a b", a=14)[:, off:off + 7:2, :], ALU.mult, ALU.add),
                        reads=[pss, tab], writes=[stt[e2]])
                    P.op('act', lambda e, Eh=Eh, e2=e2: e.activation(Eh.t[:, 0:4, :], stt[e2].t[:], AF.Exp), reads=[stt[e2]], writes=[Eh])
                    P.op('act', lambda e, Eh=Eh, pss=pss: e.activation(Eh.t[:, 4:6, :], pss.t[:, 256:384].rearrange("p (a b) -> p a b", a=2), AF.Exp, scale=0.125),
                         reads=[pss], writes=[Eh])
                first = True
                for e2 in range(2):
                    Eh = E[e2]
                    for j in range(6):
                        if j < 4:
                            if r0 % 2 == 0:
                                vt = V.t[:, 2 + r0 // 2 + j, hp * 128:(hp + 1) * 128]
                            else:
                                vt = Vs.t[:, (r0 - 1) // 2 + j, hp * 128:(hp + 1) * 128]
                        else:
                            vt = V.t[:, j - 4, hp * 128:(hp + 1) * 128]
                        P.op('pe', lambda e, pso=pso, vt=vt, Eh=Eh, j=j, e2=e2, first=first: e.matmul(
                            pso.t[:, e2 * 64:(e2 + 1) * 64], vt, Eh.t[:, j, :], start=first, stop=(j == 5), skip_group_check=True),
                            reads=[V, Vs, Eh], writes=[pso])
                        first = False
                        P.op('pe', lambda e, pso=pso, Eh=Eh, j=j, e2=e2: e.matmul(
                            pso.t[:, 128 + e2 * 64:128 + (e2 + 1) * 64], onesb.t[:], Eh.t[:, j, :], start=False, stop=(j == 5), skip_group_check=True),
                            reads=[onesb, Eh], writes=[pso])
                P.op('dve', lambda e, pso=pso: e.reciprocal(rB.t[:], pso.t[:, 128:256].rearrange("p (a b) -> p a b", a=2)), reads=[pso], writes=[rB])
                for e2 in range(2):
                    pb = e2 * 64
                    P.op('dve', lambda e, pso=pso, e2=e2, pb=pb, hp=hp, q0=q0: e.tensor_tensor(
                        yTn.t[pb:pb + 64, hp, q0:q0 + 64], pso.t[pb:pb + 64, e2 * 64:(e2 + 1) * 64], rB.t[pb:pb + 64, e2, :], op=ALU.mult),
                        reads=[pso, rB], writes=[yTn])
                it += 1
        if not last:
            Ec = [P.sbuf("n_Ec%d" % i, [128, 2, 256], BF16) for i in range(2)]
            rBc = P.sbuf("n_rBc", [128, 256], F32)
            for hp in range(2):
                for e2 in range(2):
                    pb = e2 * 64
                    pss = self.PS[e2]
                    pso = self.PS[2 + e2]
                    for j in range(2):
                        P.op('pe', lambda e, pss=pss, j=j, pb=pb, hp=hp: e.matmul(
                            pss.t[:, j * 256:(j + 1) * 256], kT.t[pb:pb + 64, hp, j * 128:(j + 1) * 128], qT.t[pb:pb + 64, hp, 0:256], start=True, stop=True),
                            reads=[kT, qT], writes=[pss])
                    P.op('act', lambda e, pss=pss, e2=e2: e.activation(Ec[e2].t[:], pss.t[:].rearrange("p (a b) -> p a b", a=2), AF.Exp, scale=0.125),
                         reads=[pss], writes=[Ec[e2]])
                    for j in range(2):
                        P.op('pe', lambda e, pso=pso, j=j, hp=hp, e2=e2: e.matmul(
                            pso.t[:, 0:256], V.t[:, j, hp * 128:(hp + 1) * 128], Ec[e2].t[:, j, :], start=(j == 0), stop=(j == 1), skip_group_check=True),
                            reads=[V, Ec[e2]], writes=[pso])
                        P.op('pe', lambda e, pso=pso, j=j, e2=e2: e.matmul(
                            pso.t[:, 256:512], onesb.t[:], Ec[e2].t[:, j, :], start=False, stop=(j == 1), skip_group_check=True),
                            reads=[onesb, Ec[e2]], writes=[pso])
                    P.op('dve', lambda e, pso=pso: e.reciprocal(rBc.t[:], pso.t[:, 256:512]), reads=[pso], writes=[rBc])
                    P.op('dve', lambda e, pso=pso, pb=pb, hp=hp: e.tensor_tensor(
                        yTn.t[pb:pb + 64, hp, 0:256], pso.t[pb:pb + 64, 0:256], rBc.t[pb:pb + 64, :], op=ALU.mult),
                        reads=[pso, rBc], writes=[yTn])
        t_lo = NCTX if last else 0
        for tl in range(2):
            P.dma('sp', self.YT.t.ap()[384 + tl * 128:384 + (tl + 1) * 128, t_lo:S], yTn.t[:, tl, t_lo:S], reads=[yTn], writes=[self.YT])
        P.barrier()
        P.stack = old


Kern.ph_nat = _ph_nat


class _Ops:
    def __init__(self, P):
        self.P = P

    def TT(self, eng, out, a, b, op, reads, writes):
        self.P.op(eng, lambda e: e.tensor_tensor(out, a, b, op=op), reads, writes)

    def TS(self, eng, out, a, s1, s2, op0, op1, reads, writes):
        if op1 is None:
            self.P.op(eng, lambda e: e.tensor_scalar(out, a, s1, None, op0), reads, writes)
        else:
            self.P.op(eng, lambda e: e.tensor_scalar(out, a, s1, s2, op0, op1), reads, writes)

    def STT(self, out, a, s, b, op0, op1, reads, writes):
        self.P.op('dve', lambda e: e.scalar_tensor_tensor(out, a, s, b, op0, op1), reads, writes)

    def ACT(self, out, in_, func, reads, writes, **kw):
        self.P.op('act', lambda e: e.activation(out, in_, func, **kw), reads, writes)

    def CP(self, eng, out, in_, reads, writes):
        if eng == 'act':
            self.P.op('act', lambda e: e.activation(out, in_, AF.Identity), reads, writes)
        else:
            self.P.op(eng, lambda e: e.tensor_copy(out, in_), reads, writes)

    def MM(self, out, lhsT, rhs, start, stop, reads, writes):
        self.P.op('pe', lambda e: e.matmul(out, lhsT, rhs, start=start, stop=stop, skip_group_check=True), reads, writes)

    def TR(self, out, in_, ident, reads, writes):
        self.P.op('pe', lambda e: e.transpose(out, in_, ident), reads, writes)


def _ph_scan(self, b, l, xT, dxT, kind):
    P = self.P
    O = _Ops(P)
    rw = (kind == 'rw')
    NK = 16 if rw else 8
    PS = self.PS
    cst = self.cst_sb
    pc = self.pcol_sb
    with contextlib.ExitStack() as st:
        P.stack, old = st, P.stack
        Of = self.OF
        of_sbs = [P.sbuf("s_of%d" % i, [128, 384], F32) for i in range(2)]
        COEF = P.sbuf("s_coef", [128, NT, 8], F32) if rw else None
        mskF = P.sbuf("s_mskF", [128, 256], BF16)
        mskB = P.sbuf("s_mskB", [128, 256], BF16)
        lmF = P.sbuf("s_lmF", [128, 128], BF16)
        lmB = P.sbuf("s_lmB", [128, 128], BF16)
        hselb = P.sbuf("s_hsel", [128, 2], BF16)
        O.CP('dve', mskF.t[:], cst.t[:, C_MASKF:C_MASKF + 256], [cst], [mskF])
        O.CP('dve', mskB.t[:], cst.t[:, C_MASKB:C_MASKB + 256], [cst], [mskB])
        O.CP('dve', lmF.t[:], cst.t[:, C_LMF:C_LMF + 128], [cst], [lmF])
        O.CP('dve', lmB.t[:], cst.t[:, C_LMB:C_LMB + 128], [cst], [lmB])
        O.CP('dve', hselb.t[:], cst.t[:, C_HSEL:C_HSEL + 2], [cst], [hselb])
        prw = P.sbuf("s_prow", [128, 3 * 384], F32)
        P.dma('sp', prw.t[:], self.prow[l][:, PR_GNORM:PR_GNORM + 3 * 384].partition_broadcast(128), writes=[prw])
        if rw:
            ncolt = 9
            W = P.sbuf("s_W", [128, ncolt, 16, 128], BF16)
            Wv = P.sbuf("s_Wv", [128, 16, 384], BF16)
            with contextlib.ExitStack() as st2:
                P.stack = st2
                mub = P.sbuf("s_mub", [128, 1536], F32)
                P.dma('sp', mub.t[:], self.prow[l][:, PR_MU:PR_MU + 1536].partition_broadcast(128), writes=[mub])
                colt = [RR, RR + 128, RR + 256, RK, RK + 128, RK + 256, RDD, RAD, RGD]
                for i, c0 in enumerate(colt):
                    P.dma('pool', W.t[:, i, 0:8, :], self.w_in[l][:, c0:c0 + 128].rearrange("(k p) c -> p k c", p=128), writes=[W])
                    m0 = c0 - RR
                    O.TT('dve', W.t[:, i, 8:16, :], W.t[:, i, 0:8, :], mub.t[:, m0:m0 + 128][:, None, :].to_broadcast([128, 8, 128]), ALU.mult,
                         [W, mub], [W])
                P.dma('pool', Wv.t[:, 0:8, :], self.w_in[l][:, RV:RV + 384].rearrange("(k p) c -> p k c", p=128), writes=[Wv])
                O.TT('dve', Wv.t[:, 8:16, :], Wv.t[:, 0:8, :], mub.t[:, RV - RR:RV - RR + 384][:, None, :].to_broadcast([128, 8, 384]), ALU.mult,
                     [Wv, mub], [Wv])
                P.barrier()
            P.stack = st
            wd2 = P.sbuf("s_wd2", [128, 384], BF16)
            wa2 = P.sbuf("s_wa2", [128, 384], BF16)
            wg2 = P.sbuf("s_wg2", [128, 384], BF16)
            P.dma('pool', wd2.t[:], self.wd2[l], writes=[wd2])
            P.dma('pool', wa2.t[:], self.wa2[l], writes=[wa2])
            P.dma('pool', wg2.t[:], self.wg2[l], writes=[wg2])
            bones = cst.t[:, C_BONES:C_BONES + 128]
        else:
            W = P.sbuf("s_Wg", [128, 8, 4, 384], BF16)
            Wdn = P.sbuf("s_Wdn", [128, 8, 48], BF16)
            Wv = P.sbuf("s_Wv", [128, 8, 384], BF16)
            Wgt = P.sbuf("s_Wgt", [128, 8, 384], BF16)
            gup = P.sbuf("s_gup", [48, 384], BF16)
            P.op('pool', lambda e: e.memset(W.t[:], 0.0), writes=[W])
            P.op('pool', lambda e: e.memset(Wdn.t[:], 0.0), writes=[Wdn])
            P.op('pool', lambda e: e.memset(gup.t[:], 0.0), writes=[gup])
            with contextlib.ExitStack() as st2:
                P.stack = st2
                wqk = P.sbuf("s_wqk", [128, 8, 384], BF16)
                P.dma('pool', wqk.t[:], self.w_in[l][:, GQ:GQ + 384].rearrange("(k p) c -> p k c", p=128), writes=[wqk])
                for qi in range(2):
                    src = wqk.t[:, :, qi * 192:(qi + 1) * 192].rearrange("p k (h c) -> p k h c", h=6)
                    dst = W.t[:, :, 2 * qi, :].rearrange("p k (h c) -> p k h c", h=6)[:, :, :, 0:32]
                    dstp = W.t[:, :, 2 * qi + 1, :].rearrange("p k (h c) -> p k h c", h=6)
                    for kk in range(8):
                        O.CP('dve', dst[:, kk], src[:, kk], [wqk], [W])
                        O.TS('dve', dstp[:, kk, :, 0:32:2], src[:, kk, :, 1:32:2], -1.0, None, ALU.mult, None, [wqk], [W])
                        O.CP('dve', dstp[:, kk, :, 1:32:2], src[:, kk, :, 0:32:2], [wqk], [W])
                P.barrier()
            P.stack = st
            P.dma('pool', Wdn.t[:, :, 0:16], self.w_in[l][:, GDN:GDN + 16].rearrange("(k p) c -> p k c", p=128), writes=[Wdn])
            P.dma('pool', Wdn.t[:, :, 32:48], self.w_in[l][:, GDN + 16:GDN + 32].rearrange("(k p) c -> p k c", p=128), writes=[Wdn])
            P.dma('pool', Wv.t[:], self.w_in[l][:, GV:GV + 384].rearrange("(k p) c -> p k c", p=128), writes=[Wv])
            P.dma('pool', Wgt.t[:], self.w_in[l][:, GG:GG + 384].rearrange("(k p) c -> p k c", p=128), writes=[Wgt])
            P.dma('pool', gup.t[0:16, :], self.gup[l, 0], writes=[gup])
            P.dma('pool', gup.t[32:48, :], self.gup[l, 1], writes=[gup])
            ropeC = P.sbuf("s_ropeC", [128, 256], F32)
            ropeS = P.sbuf("s_ropeS", [128, 256], F32)
        if Prog.limit is not None:
            print("MS weights", P.n_ops)
        KR = [P.sbuf("s_KR%d" % p, [128, 2, 2, 128], BF16) for p in range(3)]
        Kg = [P.sbuf("s_Kg%d" % p, [128, 256], BF16) for p in range(3)]
        Kgp = [P.sbuf("s_Kgp%d" % p, [128, 256], F32) for p in range(3)]
        if rw:
            Bg = [P.sbuf("s_Bg%d" % p, [128, 256], BF16) for p in range(3)]
            Bgp = [P.sbuf("s_Bgp%d" % p, [128, 256], F32) for p in range(3)]
        else:
            for p in range(3):
                P.op('pool', lambda e, p=p: e.memset(KR[p].t[:], 0.0), writes=[KR[p]])
        GC = P.sbuf("s_GC", [128, 3, 2], F32)
        V = P.sbuf("s_V", [128, 2, 384], BF16)
        Vf = P.sbuf("s_Vf", [128, 2, 384], F32)
        ft = {n: P.sbuf("s_f_" + n, [128, 256], F32) for n in
              (["rf", "kf", "sg", "css", "g", "gi", "gex", "gp", "a", "kk", "t1", "kmod", "kka", "t2"] if rw else
               ["sg", "css", "g", "gi", "gp", "t1", "t2", "qr", "kr"])}
        if rw:
            tdd = P.sbuf("s_tdd", [128, 256], BF16)
            adb = P.sbuf("s_adb", [128, 256], BF16)
            sgd = P.sbuf("s_sgd", [128, 256], BF16)
            prodb = P.sbuf("s_prodb", [128, 256], BF16)
        else:
            dnb = P.sbuf("s_dnb", [48, 256], BF16)
        LMC = P.sbuf("s_LMC", [128, 6, 2, 256], BF16)
        Lm = P.sbuf("s_L", [128, 6, 128], BF16)
        PTs = [P.sbuf("s_PT%d" % i, [128, 6, 128], BF16) for i in range(2)]
        Psq = [P.sbuf("s_Pq%d" % i, [128, 6, 128], BF16) for i in range(2)]
        Ub = P.sbuf("s_Ub", [128, 384], BF16)
        BKT = P.sbuf("s_BKT", [128, 6, 128], BF16)
        Tw = P.sbuf("s_Tw", [128, 3, 128], F32)
        Twb = P.sbuf("s_Twb", [128, 3, 128], BF16)
        ttmp = P.sbuf("s_ttmp", [128, 3, 128], F32)
        o_sbs = [P.sbuf("s_o%d" % i, [128, 384], F32) for i in range(2)]
        y_sb = P.sbuf("s_y", [128, 384], F32)
        sq_sb = P.sbuf("s_sq", [128, 384], F32)
        gt_sb = P.sbuf("s_gt", [128, 384], F32)
        yb = P.sbuf("s_yb", [128, 384], F32)
        stt = P.sbuf("s_stat", [128, 8, 8], F32)
        yTo = P.sbuf("s_yTo", [128, 3, 256], BF16)
        nrm = prw.t[:, 0:384]
        gnw = prw.t[:, 384:768]
        gnb = prw.t[:, 768:1152]
        bdm = cst.t[:, C_BD:C_BD + 128]
        ident_b = self.identb
        feat0 = 5 * 128 if rw else 0

        for d in range(2):
            fwd = (d == 0)
            msk = mskF if fwd else mskB
            lm = lmF if fwd else lmB
            scm = cst.t[:, C_SCF:C_SCF + 256] if fwd else cst.t[:, C_SCB:C_SCB + 256]
            cend = 127 if fwd else 0
            O.P.op('pool', lambda e: e.memset(Tw.t[:], 0.0), writes=[Tw])
            O.P.op('pool', lambda e: e.memset(Twb.t[:], 0.0), writes=[Twb])
            border = list(range(NBLK)) if fwd else [0] + list(range(NBLK - 1, 0, -1))
            for bi in border:
                t0 = bi * CB
                lat = bi > 0

                def rhs(k):
                    return xT.t[:, k, t0:t0 + CB] if k < 8 else dxT.t[:, k - 8, t0:t0 + CB]

                def lhs_tok(k, c):
                    return xT.t[:, k, t0 + c * 128:t0 + (c + 1) * 128] if k < 8 else dxT.t[:, k - 8, t0 + c * 128:t0 + (c + 1) * 128]
                xr = [xT, dxT] if rw else [xT]
                for c in range(2):
                    ps = PS[c]
                    for k in range(NK):
                        O.MM(ps.t[:, 0:384], lhs_tok(k, c), Wv.t[:, k, :], k == 0, k == NK - 1, xr + [Wv], [ps])
                    O.CP('act', V.t[:, c, :], ps.t[:, 0:384], [ps], [V])
                if rw:
                    def projF(i, ps, half):
                        o = ps.t[:, half * 256:(half + 1) * 256]
                        for k in range(16):
                            O.MM(o, W.t[:, i, k, :], rhs(k), k == 0, k == 15, xr + [W], [ps])
                        return o
                    o_dd = projF(6, PS[2], 0)
                    O.ACT(tdd.t[:], o_dd, AF.Tanh, [PS[2]], [tdd])
                    o_ad = projF(7, PS[2], 1)
                    O.CP('dve', adb.t[:], o_ad, [PS[2]], [adb])
                    if not fwd:
                        o_gd = projF(8, PS[3], 0)
                        O.ACT(sgd.t[:], o_gd, AF.Sigmoid, [PS[3]], [sgd])
                    for p in range(3):
                        f = ft
                        o_r = projF(p, PS[4], 0)
                        O.CP('act', f["rf"].t[:], o_r, [PS[4]], [f["rf"]])
                        o_k = projF(3 + p, PS[4], 1)
                        O.CP('dve', f["kf"].t[:], o_k, [PS[4]], [f["kf"]])
                        o_d = PS[5].t[:, 0:256]
                        O.MM(o_d, wd2.t[d * 64:(d + 1) * 64, p * 128:(p + 1) * 128], tdd.t[d * 64:(d + 1) * 64, :], True, True, [wd2, tdd], [PS[5]])
                        O.ACT(f["sg"].t[:], o_d, AF.Sigmoid, [PS[5], pc], [f["sg"]], bias=pc.t[:, PC_W0 + d * 3 + p:PC_W0 + d * 3 + p + 1])
                        o_a = PS[5].t[:, 256:512]
                        O.MM(o_a, wa2.t[d * 64:(d + 1) * 64, p * 128:(p + 1) * 128], adb.t[d * 64:(d + 1) * 64, :], True, True, [wa2, adb], [PS[5]])
                        O.ACT(f["a"].t[:], o_a, AF.Sigmoid, [PS[5], pc], [f["a"]], bias=pc.t[:, PC_A0 + d * 3 + p:PC_A0 + d * 3 + p + 1])
                        self._scan_gates(O, f, scm, fwd, cend, GC, p, DECAY)
                        O.TS('pool', f["kk"].t[:], f["kf"].t[:], pc.t[:, PC_KK + p:PC_KK + p + 1], None, ALU.mult, None, [f["kf"], pc], [f["kk"]])
                        O.TT('pool', f["t1"].t[:], f["kk"].t[:], f["kk"].t[:], ALU.mult, [f["kk"]], [f["t1"]])
                        o_ss = PS[6].t[:, 0:256]
                        O.MM(o_ss, bones, f["t1"].t[:], True, True, [cst, f["t1"]], [PS[6]])
                        O.ACT(f["t2"].t[:], o_ss, AF.Sqrt, [PS[6], cst], [f["t2"]], bias=cst.t[:, C_E12:C_E12 + 1])
                        O.P.op('dve', lambda e: e.reciprocal(f["t1"].t[:], f["t2"].t[:]), [f["t2"]], [f["t1"]])
                        O.TT('pool', f["kk"].t[:], f["kk"].t[:], f["t1"].t[:], ALU.mult, [f["kk"], f["t1"]], [f["kk"]])
                        O.TS('dve', f["t1"].t[:], f["a"].t[:], pc.t[:, PC_KA + p:PC_KA + p + 1], pc.t[:, NPC + p:NPC + p + 1], ALU.mult, ALU.add,
                             [f["a"], pc], [f["t1"]])
                        O.TT('pool', f["kmod"].t[:], f["kf"].t[:], f["t1"].t[:], ALU.mult, [f["kf"], f["t1"]], [f["kmod"]])
                        O.TT('pool', f["kka"].t[:], f["kk"].t[:], f["a"].t[:], ALU.mult, [f["kk"], f["a"]], [f["kka"]])
                        O.TT('dve', KR[p].t[:, :, 0, :], f["kk"].t[:].rearrange("p (c t) -> p c t", c=2), f["gex"].t[:].rearrange("p (c t) -> p c t", c=2),
                             ALU.mult, [f["kk"], f["gex"]], [KR[p]])
                        O.TT('pool', KR[p].t[:, :, 1, :], f["rf"].t[:].rearrange("p (c t) -> p c t", c=2), f["g"].t[:].rearrange("p (c t) -> p c t", c=2),
                             ALU.mult, [f["rf"], f["g"]], [KR[p]])
                        O.TT('dve', Kg[p].t[:], f["kmod"].t[:], f["gi"].t[:], ALU.mult, [f["kmod"], f["gi"]], [Kg[p]])
                        O.TT('pool', Kgp[p].t[:], f["kmod"].t[:], f["gp"].t[:], ALU.mult, [f["kmod"], f["gp"]], [Kgp[p]])
                        O.STT(Bg[p].t[:], f["kka"].t[:], -1.0, f["gi"].t[:], ALU.mult, ALU.mult, [f["kka"], f["gi"]], [Bg[p]])
                        O.STT(Bgp[p].t[:], f["kka"].t[:], -1.0, f["gp"].t[:], ALU.mult, ALU.mult, [f["kka"], f["gp"]], [Bgp[p]])
                        O.TT('pool', f["t1"].t[:], f["rf"].t[:], f["kmod"].t[:], ALU.mult, [f["rf"], f["kmod"]], [f["t1"]])
                        O.TS('dve', prodb.t[:], f["t1"].t[:], pc.t[:, PC_RK + p:PC_RK + p + 1], None, ALU.mult, None, [f["t1"], pc], [prodb])
                        for c in range(2):
                            O.MM(PS[7].t[:, c * 8 + p * 2:c * 8 + p * 2 + 2], prodb.t[:, c * 128:(c + 1) * 128], hselb.t[:], True, True,
                                 [prodb, hselb], [PS[7]])
                    for c in range(2):
                        ch = bi * 2 + c
                        if fwd:
                            O.CP('dve', COEF.t[:, ch, 0:6], PS[7].t[:, c * 8:c * 8 + 6], [PS[7]], [COEF])
                        else:
                            O.TT('dve', COEF.t[:, ch, 0:6], PS[7].t[:, c * 8:c * 8 + 6], COEF.t[:, ch, 0:6], ALU.add, [PS[7], COEF], [COEF])
                else:
                    f = ft
                    o_dn = PS[2].t[0:48, 0:256]
                    for k in range(8):
                        O.MM(o_dn, Wdn.t[:, k, :], rhs(k), k == 0, k == 7, [xT, Wdn], [PS[2]])
                    O.CP('dve', dnb.t[:], o_dn, [PS[2]], [dnb])
                    if lat:
                        lt0 = t0 - NCTX
                        P.dma('sp', ropeC.t[:], self.ropeC[:, lt0:lt0 + CB], writes=[ropeC])
                        P.dma('sp', ropeS.t[:], self.ropeS[:, lt0:lt0 + CB], writes=[ropeS])
                    for p in range(3):
                        o_z = PS[3].t[:, 0:256]
                        O.MM(o_z, gup.t[d * 32:d * 32 + 16, p * 128:(p + 1) * 128], dnb.t[d * 32:d * 32 + 16, :], True, True, [gup, dnb], [PS[3]])
                        O.ACT(f["t1"].t[:], o_z, AF.Exp, [PS[3], pc], [f["t1"]], scale=-1.0, bias=pc.t[:, NPC + 3 + d * 3 + p:NPC + 3 + d * 3 + p + 1])
                        O.ACT(f["sg"].t[:], f["t1"].t[:], AF.Ln, [f["t1"], cst], [f["sg"]], bias=cst.t[:, C_ONE:C_ONE + 1])
                        self._scan_gates(O, f, scm, fwd, cend, GC, p, 1.0 / 16.0)
                        for qi, (dst, nm) in enumerate(((None, "qr"), (None, "kr"))):
                            def projq(j, half):
                                o = PS[4 + (half // 2)].t[:, (half % 2) * 256:(half % 2 + 1) * 256]
                                for k in range(8):
                                    O.MM(o, W.t[:, k, j, p * 128:(p + 1) * 128], rhs(k), k == 0, k == 7, [xT, W], [PS[4 + (half // 2)]])
                                return o, PS[4 + (half // 2)]
                            o1, b1 = projq(2 * qi, 2 * qi)
                            if lat:
                                o2, b2 = projq(2 * qi + 1, 2 * qi + 1)
                                O.TT('dve', f["t1"].t[:], o1, ropeC.t[:], ALU.mult, [b1, ropeC], [f["t1"]])
                                O.TT('dve', f["t2"].t[:], o2, ropeS.t[:], ALU.mult, [b2, ropeS], [f["t2"]])
                                O.TT('pool', f[nm].t[:], f["t1"].t[:], f["t2"].t[:], ALU.add, [f["t1"], f["t2"]], [f[nm]])
                            else:
                                O.CP('dve', f[nm].t[:], o1, [b1], [f[nm]])
                        O.STT(KR[p].t[:, :, 1, :], f["qr"].t[:].rearrange("p (c t) -> p c t", c=2), 32.0 ** -0.5,
                              f["g"].t[:].rearrange("p (c t) -> p c t", c=2), ALU.mult, ALU.mult, [f["qr"], f["g"]], [KR[p]])
                        O.TT('dve', Kg[p].t[:], f["kr"].t[:], f["gi"].t[:], ALU.mult, [f["kr"], f["gi"]], [Kg[p]])
                        O.TT('pool', Kgp[p].t[:], f["kr"].t[:], f["gp"].t[:], ALU.mult, [f["kr"], f["gp"]], [Kgp[p]])
                if Prog.limit is not None:
                    print("MS prep", d, bi, P.n_ops)
                for c in ([0, 1] if fwd else [1, 0]):
                    ch = bi * 2 + c
                    cs = slice(c * 128, (c + 1) * 128)
                    o_sb = o_sbs[ch % 2]
                    of_sb = of_sbs[ch % 2]
                    if not fwd:
                        P.dma('sp', of_sb.t[:], Of.t.ap()[ch], reads=[Of], writes=[of_sb])
                    mb = msk.t[:, None, :].to_broadcast([128, 2, 256])
                    for p in range(3):
                        for e2 in range(2):
                            h = 2 * p + e2
                            pb = e2 * 64
                            ps = PS[e2]
                            krr = KR[p].t[pb:pb + 64, c, :, :].rearrange("p a t -> p (a t)")
                            if rw:
                                O.MM(ps.t[:, 0:256], Bg[p].t[pb:pb + 64, cs], krr, True, True, [Bg[p], KR[p]], [ps])
                            O.MM(ps.t[:, 256:512], Kg[p].t[pb:pb + 64, cs], krr, True, True, [Kg[p], KR[p]], [ps])
                            if rw:
                                O.TT('dve', LMC.t[:, h, :, :], ps.t[:].rearrange("p (a t) -> p a t", a=2), mb, ALU.mult, [ps, msk], [LMC])
                            else:
                                O.TT('dve', LMC.t[:, h, 1, :], ps.t[:, 256:512], msk.t[:], ALU.mult, [ps, msk], [LMC])
                    if rw:
                        for e2 in range(2):
                            pb = e2 * 64
                            ps = PS[e2]
                            for pp in range(3):
                                O.MM(ps.t[:, pp * 128:(pp + 1) * 128], KR[pp].t[pb:pb + 64, c, 0, :], Bg[pp].t[pb:pb + 64, cs], True, True,
                                     [KR[pp], Bg[pp]], [ps])
                            O.TT('dve', Lm.t[:, e2:6:2, :], ps.t[:, 0:384].rearrange("p (a t) -> p a t", a=3),
                                 lm.t[:, None, :].to_broadcast([128, 3, 128]), ALU.mult, [ps, lm], [Lm])
                        first = True
                        for p in range(3):
                            O.MM(PS[2].t[:, p * 128:(p + 1) * 128], KR[p].t[:, c, 0, :], Twb.t[:, p, :], first, False, [KR[p], Twb], [PS[2]])
                            first = False
                        for h in range(6):
                            O.MM(PS[2].t[:, h * 64:(h + 1) * 64], LMC.t[:, h, 1, 0:128], V.t[:, c, h * 64:(h + 1) * 64], False, False, [LMC, V], [PS[2]])
                        O.CP('act', Ub.t[:], PS[2].t[:, 0:384], [PS[2]], [Ub])
                        for lv in range(7):
                            def PT(h):
                                return LMC.t[:, h, 0, 0:128] if lv == 0 else PTs[lv % 2].t[:, h, :]

                            def PP(h):
                                return Lm.t[:, h, :] if lv == 0 else Psq[lv % 2].t[:, h, :]
                            ptb = LMC if lv == 0 else PTs[lv % 2]
                            ppb = Lm if lv == 0 else Psq[lv % 2]
                            for h in range(6):
                                O.MM(PS[2].t[:, h * 64:(h + 1) * 64], PT(h), Ub.t[:, h * 64:(h + 1) * 64], False, lv == 6, [ptb, Ub], [PS[2]])
                            O.CP('act', Ub.t[:], PS[2].t[:, 0:384], [PS[2]], [Ub])
                            if lv < 6:
                                for hg in range(2):
                                    ps = PS[4 + hg]
                                    for hh in range(3):
                                        h = hg * 3 + hh
                                        O.MM(ps.t[:, hh * 128:(hh + 1) * 128], PP(h), PT(h), True, True, [ptb, ppb], [ps])
                                    O.CP('dve', PTs[(lv + 1) % 2].t[:, hg * 3:hg * 3 + 3, :], ps.t[:, 0:384].rearrange("p (a t) -> p a t", a=3),
                                         [ps], [PTs[(lv + 1) % 2]])
                                if lv < 5:
                                    for hg in range(2):
                                        ps = PS[6 + hg]
                                        for hh in range(3):
                                            h = hg * 3 + hh
                                            O.MM(ps.t[:, hh * 128:(hh + 1) * 128], PT(h), PP(h), True, True, [ptb, ppb], [ps])
                                        O.CP('dve' if hg == 0 else 'act', Psq[(lv + 1) % 2].t[:, hg * 3:hg * 3 + 3, :],
                                             ps.t[:, 0:384].rearrange("p (a t) -> p a t", a=3), [ps], [Psq[(lv + 1) % 2]])
                    first = True
                    for p in range(3):
                        O.MM(PS[3].t[:, p * 128:(p + 1) * 128], KR[p].t[:, c, 1, :], Twb.t[:, p, :], first, False, [KR[p], Twb], [PS[3]])
                        first = False
                    for h in range(6):
                        if rw:
                            O.MM(PS[3].t[:, h * 64:(h + 1) * 64], LMC.t[:, h, 0, 128:256], Ub.t[:, h * 64:(h + 1) * 64], False, False, [LMC, Ub], [PS[3]])
                        O.MM(PS[3].t[:, h * 64:(h + 1) * 64], LMC.t[:, h, 1, 128:256], V.t[:, c, h * 64:(h + 1) * 64], False, h == 5, [LMC, V], [PS[3]])
                    if fwd:
                        O.CP('dve', o_sb.t[:], PS[3].t[:, 0:384], [PS[3]], [o_sb])
                        P.dma('sp', Of.t.ap()[ch], o_sb.t[:], reads=[o_sb], writes=[Of])
                    else:
                        O.TT('dve', o_sb.t[:], PS[3].t[:, 0:384], of_sb.t[:], ALU.add, [PS[3], of_sb], [o_sb])
                    idf = cst.t[:, C_IDENT:C_IDENT + 128]
                    for p in range(3):
                        if rw:
                            O.TR(PS[6].t[:, p * 128:(p + 1) * 128], Bgp[p].t[:, cs], idf, [Bgp[p], cst], [PS[6]])
                        O.TR(PS[0].t[:, p * 128:(p + 1) * 128], Kgp[p].t[:, cs], idf, [Kgp[p], cst], [PS[0]])
                    if rw:
                        O.CP('dve', BKT.t[:, 0:3, :], PS[6].t[:, 0:384].rearrange("p (a t) -> p a t", a=3), [PS[6]], [BKT])
                    O.CP('act', BKT.t[:, 3:6, :], PS[0].t[:, 0:384].rearrange("p (a t) -> p a t", a=3), [PS[0]], [BKT])
                    first = True
                    for p in range(3):
                        if rw:
                            O.MM(PS[7].t[:, p * 128:(p + 1) * 128], BKT.t[:, p, :], Ub.t[:, p * 128:(p + 1) * 128], first, False, [BKT, Ub], [PS[7]])
                            first = False
                        O.MM(PS[7].t[:, p * 128:(p + 1) * 128], BKT.t[:, 3 + p, :], V.t[:, c, p * 128:(p + 1) * 128], first, p == 2, [BKT, V], [PS[7]])
                        first = False
                    O.TT('dve', ttmp.t[:], PS[7].t[:, 0:384].rearrange("p (a t) -> p a t", a=3), bdm[:, None, :].to_broadcast([128, 3, 128]), ALU.mult,
                         [PS[7], cst], [ttmp])
                    for p in range(3):
                        O.STT(Tw.t[:, p, :], Tw.t[:, p, :], GC.t[:, p, c:c + 1], ttmp.t[:, p, :], ALU.mult, ALU.add, [Tw, GC, ttmp], [Tw])
                    O.CP('act', Twb.t[:], Tw.t[:], [Tw], [Twb])
                    if Prog.limit is not None:
                        print("MS chunk", d, bi, c, P.n_ops)
                    if not fwd:
                        o3 = o_sb.t[:].rearrange("p (h v) -> p h v", h=6)
                        if rw:
                            O.P.op('dve', lambda e, o3=o3: e.tensor_reduce(stt.t[:, 0, 0:6], o3, AX.X, ALU.add), [o_sb], [stt])
                        O.TT('pool', sq_sb.t[:], o_sb.t[:], o_sb.t[:], ALU.mult, [o_sb], [sq_sb])
                        O.P.op('dve', lambda e: e.tensor_reduce(stt.t[:, 1, 0:6], sq_sb.t[:].rearrange("p (h v) -> p h v", h=6), AX.X, ALU.add), [sq_sb], [stt])
                        if rw:
                            O.TS('dve', stt.t[:, 2, 0:6], stt.t[:, 0, 0:6], 1.0 / 64, None, ALU.mult, None, [stt], [stt])
                            O.TT('dve', stt.t[:, 3, 0:6], stt.t[:, 2, 0:6], stt.t[:, 2, 0:6], ALU.mult, [stt], [stt])
                            O.STT(stt.t[:, 4, 0:6], stt.t[:, 1, 0:6], 1.0 / 64, stt.t[:, 3, 0:6], ALU.mult, ALU.subtract, [stt], [stt])
                            O.TS('dve', stt.t[:, 4, 0:6], stt.t[:, 4, 0:6], RW_GN_EPS, None, ALU.add, None, [stt], [stt])
                        else:
                            O.TS('dve', stt.t[:, 4, 0:6], stt.t[:, 1, 0:6], 1.0 / 64, LN_EPS, ALU.mult, ALU.add, [stt], [stt])
                        O.ACT(stt.t[:, 5, 0:6], stt.t[:, 4, 0:6], AF.Sqrt, [stt], [stt])
                        O.P.op('dve', lambda e: e.reciprocal(stt.t[:, 6, 0:6], stt.t[:, 5, 0:6]), [stt], [stt])
                        y3 = y_sb.t[:].rearrange("p (h v) -> p h v", h=6)
                        rstd_b = stt.t[:, 6, 0:6][:, :, None].to_broadcast([128, 6, 64])
                        psg = PS[0]
                        if rw:
                            O.MM(psg.t[:, 0:384], sgd.t[:, cs], wg2.t[:], True, True, [sgd, wg2], [psg])
                            mean_b = stt.t[:, 2, 0:6][:, :, None].to_broadcast([128, 6, 64])
                            O.TT('dve', y3, o3, mean_b, ALU.subtract, [o_sb, stt], [y_sb])
                            O.TT('dve', y3, y3, rstd_b, ALU.mult, [y_sb, stt], [y_sb])
                            O.TT('pool', y_sb.t[:], y_sb.t[:], gnw, ALU.mult, [y_sb, prw], [y_sb])
                            O.TT('pool', y_sb.t[:], y_sb.t[:], gnb, ALU.add, [y_sb, prw], [y_sb])
                            coef_b = COEF.t[:, ch, 0:6][:, :, None].to_broadcast([128, 6, 64])
                            O.TT('dve', sq_sb.t[:].rearrange("p (h v) -> p h v", h=6), V.t[:, c, :].rearrange("p (h v) -> p h v", h=6), coef_b, ALU.mult,
                                 [V, COEF], [sq_sb])
                            O.TT('pool', y_sb.t[:], y_sb.t[:], sq_sb.t[:], ALU.add, [y_sb, sq_sb], [y_sb])
                            O.TT('dve', yb.t[:], psg.t[:, 0:384], y_sb.t[:], ALU.mult, [y_sb, psg], [yb])
                        else:
                            for k in range(8):
                                O.MM(psg.t[:, 0:384], lhs_tok(k, c), Wgt.t[:, k, :], k == 0, k == 7, [xT, Wgt], [psg])
                            O.ACT(gt_sb.t[:], psg.t[:, 0:384], AF.Silu, [psg], [gt_sb])
                            O.TT('dve', y3, o3, rstd_b, ALU.mult, [o_sb, stt], [y_sb])
                            O.TT('pool', y_sb.t[:], y_sb.t[:], nrm, ALU.mult, [y_sb, prw], [y_sb])
                            O.TT('pool', yb.t[:], y_sb.t[:], gt_sb.t[:], ALU.mult, [y_sb, gt_sb], [yb])
                        for p in range(3):
                            O.TR(PS[1].t[:, p * 128:(p + 1) * 128], yb.t[:, p * 128:(p + 1) * 128], cst.t[:, C_IDENT:C_IDENT + 128], [yb, cst], [PS[1]])
                        O.CP('act', yTo.t[:, :, cs], PS[1].t[:, 0:384].rearrange("p (a t) -> p a t", a=3), [PS[1]], [yTo])
                if not fwd:
                    for p in range(3):
                        P.dma('sp', self.YT.t.ap()[feat0 + p * 128:feat0 + (p + 1) * 128, t0:t0 + CB], yTo.t[:, p, :], reads=[yTo], writes=[self.YT])
        P.barrier()
        P.stack = old


def _scan_gates(self, O, f, scm, fwd, cend, GC, p, rate):
    sg, css = f["sg"], f["css"]
    if fwd:
        O.P.op('dve', lambda e: e.tensor_tensor_scan(css.t[:], scm, sg.t[:], 0.0, ALU.mult, ALU.add), [sg, self.cst_sb], [css])
    else:
        O.P.op('dve', lambda e: e.tensor_tensor_scan(css.t[:, ::-1], scm[:, ::-1], sg.t[:, ::-1], 0.0, ALU.mult, ALU.add), [sg, self.cst_sb], [css])
    O.ACT(f["g"].t[:], css.t[:], AF.Exp, [css], [f["g"]], scale=-rate)
    O.ACT(f["gi"].t[:], css.t[:], AF.Exp, [css], [f["gi"]], scale=rate)
    if "gex" in f:
        O.TT('pool', f["t2"].t[:], css.t[:], sg.t[:], ALU.subtract, [css, sg], [f["t2"]])
        O.ACT(f["gex"].t[:], f["t2"].t[:], AF.Exp, [f["t2"]], [f["gex"]], scale=-rate)
    c3 = css.t[:].rearrange("p (c t) -> p c t", c=2)
    cC = c3[:, :, cend:cend + 1]
    O.TT('dve', f["t2"].t[:].rearrange("p (c t) -> p c t", c=2), cC.to_broadcast([128, 2, 128]), c3, ALU.subtract, [css], [f["t2"]])
    O.ACT(f["gp"].t[:], f["t2"].t[:], AF.Exp, [f["t2"]], [f["gp"]], scale=-rate)
    O.ACT(GC.t[:, p, :], css.t[:, cend:256:128], AF.Exp, [css], [GC], scale=-rate)


Kern.ph_scan = _ph_scan
Kern._scan_gates = _scan_gates


def build_full(nb=NB):
    K = Kern(nb=nb)
    P = K.P
    for l in range(DEPTH):
        K.set_layer(l)
        K.ph_mod(l)
    for b in range(nb):
        for l in range(DEPTH):
            K.set_layer(l)
            P.barrier()
            with contextlib.ExitStack() as st:
                P.stack, old = st, P.stack
                xT, dxT = K.ph_x(b, l, need_dx=True)
                K.ph_nat(b, l, xT)
                K.ph_scan(b, l, xT, dxT, 'gla')
                K.ph_scan(b, l, xT, dxT, 'rw')
                P.barrier()
                P.stack = old
            K.ph_wout_ln1(b, l)
            K.ph_ffn_ln2(b, l)
    P.finish()
    return K


_CACHE = {}


def kernel(**inputs):
    n_cores = 8
    if "K" not in _CACHE:
        _CACHE["K"] = build_full(NB)
    K = _CACHE["K"]
    in_maps = [host_inputs(inputs, c, NB) for c in range(n_cores)]
    res = run_bass_kernel_spmd(K.nc, in_maps, core_ids=list(range(n_cores)))
    out = np.concatenate([np.asarray(r["y"]) for r in res.results], axis=0)
    return out.astype(np.float32)
```

```python
import contextlib
import numpy as np
import concourse.bass as bass
import concourse.mybir as mybir
from concourse.bass_utils import run_bass_kernel_spmd

F32 = mybir.dt.float32
BF16 = mybir.dt.bfloat16
AF = mybir.ActivationFunctionType
ALU = mybir.AluOpType
AX = mybir.AxisListType

ENGS = ('pe', 'dve', 'act', 'pool', 'sp')


class Buf:
    def __init__(self, t, name):
        self.t = t
        self.name = name
        self.last_write = None
        self.reads = {}


class Prog:
    SAME_ENGINE_SYNC = True

    def __init__(self, nc, n_dma_sems=12):
        self.nc = nc
        self.stack = contextlib.ExitStack()
        self.ops = {e: [] for e in ENGS}
        self.sem = {}
        for e in ('pe', 'dve', 'act', 'pool'):
            self.sem[e] = self.stack.enter_context(nc.semaphore("s_" + e))
        self.cnt = {e: 0 for e in ('pe', 'dve', 'act', 'pool')}
        self.seen = {e: {} for e in ENGS}
        self.dsems = {}
        self.dcur = {}
        for q in ('sp', 'pool', 'act'):
            self.dsems[q] = []
            for i in range(n_dma_sems):
                key = "d_%s_%d" % (q, i)
                self.sem[key] = self.stack.enter_context(nc.semaphore(key))
                self.dsems[q].append([key, 0])
            self.dcur[q] = 0
        self.n_ops = 0

    _uid = 0

    def sbuf(self, name, shape, dtype):
        Prog._uid += 1
        name = "%s_u%d" % (name, Prog._uid)
        t = self.stack.enter_context(self.nc.sbuf_tensor(name, list(shape), dtype))
        return Buf(t, name)

    def psum(self, name, shape, dtype):
        t = self.stack.enter_context(self.nc.psum_tensor(name, list(shape), dtype))
        return Buf(t, name)

    def view(self, buf, name=None):
        return Buf(buf.t, name or buf.name)

    def _needs(self, reads, writes):
        need = {}

        def add(tok):
            if tok is None:
                return
            k, v = tok
            if need.get(k, 0) < v:
                need[k] = v
        for b in reads:
            add(b.last_write)
        for b in writes:
            add(b.last_write)
            for k, v in b.reads.items():
                add((k, v))
        return need

    def _waits(self, eng, need):
        waits = []
        for k, v in need.items():
            if k == eng and not (self.SAME_ENGINE_SYNC and eng != 'pe'):
                continue
            if self.seen[eng].get(k, 0) >= v:
                continue
            self.seen[eng][k] = v
            waits.append((k, v))
        return waits

    limit = None
    trace_range = None

    def op(self, eng, fn, reads=(), writes=()):
        if self.limit is not None and self.n_ops >= self.limit:
            return
        if self.trace_range and self.trace_range[0] <= self.n_ops < self.trace_range[1]:
            import inspect
            fr = inspect.stack()
            print("OP", self.n_ops, eng, [f.lineno for f in fr[1:4]])
        need = self._needs(reads, writes)
        waits = self._waits(eng, need)
        self.cnt[eng] += 1
        c = self.cnt[eng]
        self.ops[eng].append((waits, fn, (eng, 1)))
        tok = (eng, c)
        for b in reads:
            b.reads[eng] = c
        for b in writes:
            b.last_write = tok
            b.reads = {}
        self.n_ops += 1

    def dma(self, q, out_ap, in_ap, reads=(), writes=(), **kw):
        if self.limit is not None and self.n_ops >= self.limit and not kw.pop("force", False):
            return
        kw.pop("force", None)
        need = self._needs(reads, writes)
        slot = self.dsems[q][self.dcur[q]]
        self.dcur[q] = (self.dcur[q] + 1) % len(self.dsems[q])
        key, val = slot
        if val > 0:
            if need.get(key, 0) < val:
                need[key] = val
        waits = self._waits(q, need)
        slot[1] = val + 16
        tok = (key, val + 16)
        self.ops[q].append((waits, lambda e: e.dma_start(out=out_ap, in_=in_ap, **kw), (key, 16)))
        for b in reads:
            b.reads[key] = val + 16
        for b in writes:
            b.last_write = tok
            b.reads = {}
        self.n_ops += 1

    def barrier(self):
        need = {e: c for e, c in self.cnt.items() if c > 0}
        for q in self.dsems:
            for key, val in self.dsems[q]:
                if val > 0:
                    need[key] = val
        for eng in ENGS:
            waits = self._waits(eng, dict(need))
            if waits:
                self.ops[eng].append((waits, None, None))

    def finish(self):
        self.barrier()
        nc = self.nc
        sem = self.sem
        ops = self.ops

        def replay(eng_name):
            def run(e):
                for waits, fn, inc in ops[eng_name]:
                    for k, v in waits:
                        e.wait_ge(sem[k], v)
                    if fn is not None:
                        ins = fn(e)
                        ins.then_inc(sem[inc[0]], inc[1])
            return run

        with nc.Block() as block:
            block.tensor(replay('pe'))
            block.vector(replay('dve'))
            block.scalar(replay('act'))
            block.gpsimd(replay('pool'))
            block.sync(replay('sp'))
        self.stack.close()


D = 1024
NCTX = 256
LAT = 2048
S = NCTX + LAT
NT = S // 128
NB = 4
DEPTH = 2
HID = 2816
NHC = HID // 128
IN_COLS = 3488
GRID_W = 64
ALPHA = (2.0 * DEPTH) ** 0.25
LN_EPS = 1e-5
RW_GN_EPS = 64e-5
DECAY = float(np.exp(-0.5))
CB = 256
NBLK = S // CB
GQ, GK, GV, GG, GDN = 0, 192, 384, 768, 1152
NQ, NK, NV = 1184, 1440, 1696
RR, RK, RV, RDD, RAD, RGD = 1952, 2336, 2720, 3104, 3232, 3360
NEG = -30000.0

PC_W0 = 0
PC_A0 = 6
PC_KK = 12
PC_KA = 15
PC_RK = 18
PC_GB = 21
PC_BMOD = 27
NPC = 75
PR_LN1W, PR_LN1B, PR_LN2W, PR_LN2B = 0, 1024, 2048, 3072
PR_GNORM = 4096
PR_GNW = 4480
PR_GNB = 4864
PR_MU = 5248
NPR = 6784
C_IDENT = 0
C_MASKF = 128
C_MASKB = 384
C_LMF = 640
C_LMB = 768
C_SCF = 896
C_SCB = 1152
C_BONES = 1408
C_BD = 1536
C_HSEL = 1664
C_ONE = 1666
C_E12 = 1667
NCST = 1668


def _host_constants():
    c = np.zeros((128, NCST), np.float32)
    i = np.arange(128)
    c[:, C_IDENT:C_IDENT + 128] = np.eye(128, dtype=np.float32)
    j, t = i[:, None], i[None, :]
    c[:, C_MASKF:C_MASKF + 128] = (t > j)
    c[:, C_MASKF + 128:C_MASKF + 256] = (t >= j)
    c[:, C_MASKB:C_MASKB + 128] = (t < j)
    c[:, C_MASKB + 128:C_MASKB + 256] = (t <= j)
    c[:, C_LMF:C_LMF + 128] = (i[None, :] < i[:, None])
    c[:, C_LMB:C_LMB + 128] = (i[None, :] > i[:, None])
    sc = np.ones((256,), np.float32); sc[0] = 0; sc[128] = 0
    c[:, C_SCF:C_SCF + 256] = sc[None]
    sc = np.ones((256,), np.float32); sc[127] = 0; sc[255] = 0
    c[:, C_SCB:C_SCB + 256] = sc[None]
    blk = (i[:, None] // 64 == i[None, :] // 64).astype(np.float32)
    c[:, C_BONES:C_BONES + 128] = blk
    c[:, C_BD:C_BD + 128] = blk
    c[:, C_HSEL] = (i < 64)
    c[:, C_HSEL + 1] = (i >= 64)
    c[:, C_ONE] = 1.0
    c[:, C_E12] = 1e-12
    return c


def _rope_tables():
    tt = np.arange(LAT)
    row = (tt // GRID_W).astype(np.float32)
    col = (tt % GRID_W).astype(np.float32)
    n_freq = 8
    inv = (10000.0 ** (-np.arange(n_freq, dtype=np.float32) / n_freq)).astype(np.float32)
    ang = np.concatenate([row[:, None] * inv, col[:, None] * inv], axis=-1).astype(np.float32)
    cos, sin = np.cos(ang).astype(np.float32), np.sin(ang).astype(np.float32)
    C = np.zeros((128, LAT), np.float32)
    Sn = np.zeros((128, LAT), np.float32)
    for e in range(2):
        for kp in range(32):
            C[e * 64 + kp] = cos[:, kp // 2]
            Sn[e * 64 + kp] = sin[:, kp // 2]
    return C, Sn


def _nat_tables(rpb):
    cq = np.arange(GRID_W)
    col0 = np.clip(cq - 8, 0, GRID_W - 16)
    in_win = (cq[None, :] >= col0[:, None]) & (cq[None, :] < col0[:, None] + 16)
    dc = np.clip(cq[None, :] - cq[:, None], -15, 15) + 15
    out = np.full((DEPTH, 4, 128, 14, 64), NEG, np.float32)
    for e in range(2):
        for idx in range(14):
            g = rpb[:, :, idx + e, :][:, :, dc]
            g = np.where(in_win[None, None], g, np.float32(NEG))
            out[:, :, e * 64:(e + 1) * 64, idx, :] = np.transpose(g, (0, 1, 3, 2))
    return out.reshape(DEPTH, 4, 128, 14 * 64)


def _vec3(v):
    return np.ascontiguousarray(v.reshape(3, 128).T)


def _pad_gla(v):
    o = np.zeros((3, 2, 64), np.float32)
    o[:, :, :32] = v.reshape(3, 2, 32)
    return np.ascontiguousarray(o.reshape(3, 128).T)


class Kern:
    def __init__(self, nb=NB, depth=DEPTH, test=None):
        self.nb = nb
        self.depth = depth
        self.test = test
        nc = self.nc = bass.Bass("TRN2", target_bir_lowering=False)
        P = self.P = Prog(nc)

        def din(name, shape, dt=F32):
            return nc.dram_tensor(name, list(shape), dt, kind="ExternalInput").ap()
        self.x = din("x", [nb, LAT, D])
        self.ctx = din("ctx", [nb, NCTX, D])
        self.cc = din("cc", [NB + 1, D])
        self.w_mod = din("w_mod", [DEPTH, D, 6 * D])
        self.w_in = din("w_in", [DEPTH, D, IN_COLS])
        self.w_out = din("w_out", [DEPTH, D, D])
        self.w13 = din("ffn_w13", [DEPTH, D, 2 * HID])
        self.w2 = din("ffn_w2", [DEPTH, HID, D])
        self.pcol = din("pcol", [DEPTH, 128, NPC])
        self.prow = din("prow", [DEPTH, 1, NPR])
        self.cst = din("cst", [128, NCST])
        self.ropeC = din("ropeC", [128, LAT])
        self.ropeS = din("ropeS", [128, LAT])
        self.nattab = din("nattab", [DEPTH, 4, 128, 14 * 64])
        self.gup = din("gup", [DEPTH, 2, 16, 384])
        self.wd2 = din("wd2", [DEPTH, 128, 384])
        self.wa2 = din("wa2", [DEPTH, 128, 384])
        self.wg2 = din("wg2", [DEPTH, 128, 384])
        self.y = nc.dram_tensor("y", [nb, LAT, D], F32, kind="ExternalOutput").ap()
        self.XA = Buf(nc.dram_tensor("XA", [S, D], F32), "XA")
        self.XB = Buf(nc.dram_tensor("XB", [S, D], F32), "XB")
        if test == 'dense':
            self.YT = Buf(nc.dram_tensor("YT", [D, S], BF16, kind="ExternalInput"), "YT")
        else:
            self.YT = Buf(nc.dram_tensor("YT", [D, S], BF16), "YT")
        self.OF = Buf(nc.dram_tensor("OFs", [NT, 128, 384], F32), "OF")
        self.dbg = {}
        self.cst_sb = P.sbuf("cst_sb", [128, NCST], F32)
        P.dma('sp', self.cst_sb.t[:], self.cst, writes=[self.cst_sb])
        self.identb = P.sbuf("identb", [128, 128], BF16)
        P.op('dve', lambda e: e.tensor_copy(self.identb.t[:], self.cst_sb.t[:, C_IDENT:C_IDENT + 128]),
             reads=[self.cst_sb], writes=[self.identb])
        self.mTs = [P.sbuf("mT%d" % l, [128, 48, 8], F32) for l in range(DEPTH)]
        self.pcols = [P.sbuf("pcol_sb%d" % l, [128, NPC + 12], F32) for l in range(DEPTH)]
        self.set_layer(0)
        self.PS = [P.psum("psb%d" % i, [128, 512], F32) for i in range(8)]

    def set_layer(self, l):
        self.mT = self.mTs[l]
        self.pcol_sb = self.pcols[l]

    def ident(self):
        return self.cst_sb.t[:, C_IDENT:C_IDENT + 128]

    def dbg_out(self, name, shape, dt=F32):
        ap = self.nc.dram_tensor(name, list(shape), dt, kind="ExternalOutput").ap()
        self.dbg[name] = ap
        return ap

    def ph_mod(self, l):
        P, nc = self.P, self.nc
        P.barrier()
        pcol_sb, mT = self.pcol_sb, self.mT
        with contextlib.ExitStack() as st:
            P.stack, old = st, P.stack
            ccT = P.sbuf("ccT", [128, 8, 8], F32)
            sc = P.sbuf("scT", [128, 8, 8], F32)
            P.dma('sp', pcol_sb.t[:, 0:NPC], self.pcol[l], writes=[pcol_sb])
            P.op('dve', lambda e: e.memset(ccT.t[:], 0.0), writes=[ccT])
            for j in range(NB + 1):
                P.dma('sp', ccT.t[:, :, j:j + 1], self.cc[j].rearrange("(k p o) -> p k o", p=128, o=1), writes=[ccT],
                      allow_slow_non_contiguous=True)
            P.op('act', lambda e: e.activation(sc.t[:], ccT.t[:], AF.Silu), reads=[ccT], writes=[sc])
            P.op('dve', lambda e: e.tensor_scalar(pcol_sb.t[:, NPC:NPC + 3], pcol_sb.t[:, PC_KA:PC_KA + 3], -1.0, 1.0,
                                                  ALU.mult, ALU.add), reads=[pcol_sb], writes=[pcol_sb])
            P.op('dve', lambda e: e.tensor_scalar(pcol_sb.t[:, NPC + 3:NPC + 9], pcol_sb.t[:, PC_GB:PC_GB + 6], -1.0, None,
                                                  ALU.mult), reads=[pcol_sb], writes=[pcol_sb])
            wm = [P.sbuf("wm%d" % i, [128, 8, 512], F32) for i in range(2)]
            ps = self.PS[0]
            for cg in range(12):
                w = wm[cg % 2]
                P.dma('sp', w.t[:], self.w_mod[l][:, cg * 512:(cg + 1) * 512].rearrange("(k p) c -> p k c", p=128), writes=[w])
                first = True
                for c in range(4):
                    for k in range(8):
                        P.op('pe', lambda e, w=w, c=c, k=k, first=first: e.matmul(
                            ps.t[:, c * 8:c * 8 + 8], w.t[:, k, c * 128:(c + 1) * 128], sc.t[:, k, :], start=first, stop=(k == 7),
                            skip_group_check=True), reads=[w, sc], writes=[ps])
                        first = False
                for c in range(4):
                    ch = cg * 4 + c
                    isscale = (8 <= ch < 16) or (32 <= ch < 40)
                    P.op('dve', lambda e, c=c, ch=ch, isscale=isscale: e.tensor_scalar(
                        mT.t[:, ch, :], ps.t[:, c * 8:c * 8 + 8], pcol_sb.t[:, PC_BMOD + ch:PC_BMOD + ch + 1],
                        1.0 if isscale else 0.0, ALU.add, ALU.add), reads=[ps, pcol_sb], writes=[mT])
            P.barrier()
            P.stack = old

    def build_xT(self, src_rows, xT, ntok, shift_ch, scale_ch, jb, segs, xt_bufs):
        P = self.P
        gi = 0
        for (tok0, n, jcol) in segs:
            nt = n // 128
            xt = xt_bufs[gi % 2]
            gi += 1
            P.dma('sp', xt.t[:, 0:nt, :], src_rows(tok0, n).rearrange("(j p) d -> p j d", p=128), reads=self._src_reads, writes=[xt])
            for dc in range(8):
                ps = self.PS[dc % 2]
                for j in range(nt):
                    P.op('pe', lambda e, ps=ps, xt=xt, j=j, dc=dc: e.transpose(
                        ps.t[:, j * 128:(j + 1) * 128], xt.t[:, j, dc * 128:(dc + 1) * 128], self.ident()),
                        reads=[xt, self.cst_sb], writes=[ps])
                eng = 'dve' if dc % 2 == 0 else 'act'
                o = xT.t[:, dc, tok0:tok0 + n]
                sc_ap = self.mT.t[:, scale_ch + dc, jcol:jcol + 1]
                sh_ap = self.mT.t[:, shift_ch + dc, jcol:jcol + 1]
                if eng == 'dve':
                    P.op('dve', lambda e, o=o, ps=ps, n=n, sc_ap=sc_ap, sh_ap=sh_ap: e.tensor_scalar(
                        o, ps.t[:, 0:n], sc_ap, sh_ap, ALU.mult, ALU.add), reads=[ps, self.mT], writes=[xT])
                else:
                    P.op('act', lambda e, o=o, ps=ps, n=n, sc_ap=sc_ap, sh_ap=sh_ap: e.activation(
                        o, ps.t[:, 0:n], AF.Identity, bias=sh_ap, scale=sc_ap), reads=[ps, self.mT], writes=[xT])

    def seq_segs(self, b, with_ctx=True):
        segs = []
        if with_ctx:
            segs.append((0, NCTX, NB))
        for g in range(4):
            segs.append((NCTX + g * 512, 512, b))
        return segs

    def src_rows_fn(self, b, l):
        if l == 0:
            def f(tok0, n):
                if tok0 < NCTX:
                    return self.ctx[b][tok0:tok0 + n, :]
                return self.x[b][tok0 - NCTX:tok0 - NCTX + n, :]
            return f, []
        XB = self.XB

        def f2(tok0, n):
            return XB.t.ap()[tok0:tok0 + n, :]
        return f2, [XB]

    def ln_tail(self, pre, rows_w, rows_b, out_tile, tmp):
        P = self.P
        st = self.ln_st
        mv = self.ln_mv
        P.op('dve', lambda e: e.tensor_reduce(mv.t[:, 5:6], pre.t[:], AX.X, ALU.add), reads=[pre], writes=[mv])
        P.op('act', lambda e: e.activation(tmp.t[:], pre.t[:], AF.Square, accum_out=st.t[:, 0:1]), reads=[pre], writes=[tmp, st])
        P.op('dve', lambda e: e.tensor_scalar(mv.t[:, 0:1], mv.t[:, 5:6], 1.0 / D, None, ALU.mult), reads=[mv], writes=[mv])
        P.op('dve', lambda e: e.tensor_tensor(mv.t[:, 1:2], mv.t[:, 0:1], mv.t[:, 0:1], op=ALU.mult), reads=[mv], writes=[mv])
        P.op('dve', lambda e: e.scalar_tensor_tensor(mv.t[:, 2:3], st.t[:, 0:1], 1.0 / D, mv.t[:, 1:2], ALU.mult, ALU.subtract),
             reads=[mv, st], writes=[mv])
        P.op('dve', lambda e: e.tensor_scalar(mv.t[:, 2:3], mv.t[:, 2:3], LN_EPS, None, ALU.add), reads=[mv], writes=[mv])
        P.op('act', lambda e: e.activation(mv.t[:, 3:4], mv.t[:, 2:3], AF.Sqrt), reads=[mv], writes=[mv])
        P.op('dve', lambda e: e.reciprocal(mv.t[:, 4:5], mv.t[:, 3:4]), reads=[mv], writes=[mv])
        P.op('dve', lambda e: e.tensor_scalar(tmp.t[:], pre.t[:], mv.t[:, 0:1], mv.t[:, 4:5], ALU.subtract, ALU.mult),
             reads=[pre, mv], writes=[tmp])
        P.op('pool', lambda e: e.tensor_tensor(tmp.t[:], tmp.t[:], rows_w, op=ALU.mult), reads=[tmp, self.prow_sb], writes=[tmp])
        P.op('pool', lambda e: e.tensor_tensor(out_tile.t[:], tmp.t[:], rows_b, op=ALU.add), reads=[tmp, self.prow_sb], writes=[out_tile])

    def gate_bcast(self, gate_ch, jcol, gb):
        P = self.P
        dg = self.diag_tmp
        mT = self.mT
        ones_f = self.ones_f
        for c in range(8):
            ps = self.PS[2 + (c // 4)]
            P.op('dve', lambda e, c=c: e.tensor_scalar(dg.t[:], self.ident(), mT.t[:, gate_ch + c, jcol:jcol + 1], None, ALU.mult),
                 reads=[mT, self.cst_sb], writes=[dg])
            P.op('pe', lambda e, c=c, ps=ps: e.matmul(ps.t[:, (c % 4) * 128:(c % 4 + 1) * 128], ones_f.t[:], dg.t[:],
                                                      start=True, stop=True), reads=[dg, ones_f], writes=[ps])
            if c % 4 == 3:
                h = c // 4
                P.op('act', lambda e, ps=ps, h=h: e.activation(gb.t[:, h * 512:(h + 1) * 512], ps.t[:], AF.Identity),
                     reads=[ps], writes=[gb])

    def ph_wout_ln1(self, b, l, ntiles_from=0):
        P, nc = self.P, self.nc
        P.barrier()
        last = (l == self.depth - 1)
        src, src_reads = self.src_rows_fn(b, l)
        with contextlib.ExitStack() as st:
            P.stack, old = st, P.stack
            wo = P.sbuf("wo", [128, 8, D], BF16)
            for k in range(8):
                P.dma('pool', wo.t[:, k, :], self.w_out[l][k * 128:(k + 1) * 128, :], writes=[wo])
            self.prow_sb = P.sbuf("prow_sb", [128, 2048], F32)
            P.dma('sp', self.prow_sb.t[:], self.prow[l][:, PR_LN1W:PR_LN1W + 2048].partition_broadcast(128), writes=[self.prow_sb])
            self.ln_st = P.sbuf("ln_st", [128, 12], F32)
            self.ln_mv = P.sbuf("ln_mv", [128, 8], F32)
            self.diag_tmp = P.sbuf("diag_tmp", [128, 128], F32)
            self.ones_f = ones_f_ = P.sbuf("ones_f", [128, 128], F32)
            P.op('pool', lambda e: e.memset(ones_f_.t[:], 1.0), writes=[ones_f_])
            gbl = P.sbuf("gbl", [128, D], F32)
            gbc = P.sbuf("gbc", [128, D], F32)
            self.gate_bcast(16, b, gbl)
            self.gate_bcast(16, NB, gbc)
            yt = [P.sbuf("yt%d" % i, [128, 8, 128], BF16) for i in range(2)]
            xr = [P.sbuf("xr%d" % i, [128, D], F32) for i in range(2)]
            pre = [P.sbuf("pre%d" % i, [128, D], F32) for i in range(2)]
            tmp = P.sbuf("lntmp", [128, D], F32)
            ot = [P.sbuf("ot%d" % i, [128, D], F32) for i in range(2)]
            t0 = 2 if last else 0
            for ti in range(t0, NT):
                i2 = ti % 2
                gb = gbc if ti < 2 else gbl
                P.dma('sp', yt[i2].t[:], self.YT.t.ap()[:, ti * 128:(ti + 1) * 128].rearrange("(k p) t -> p k t", p=128),
                      reads=[self.YT], writes=[yt[i2]])
                P.dma('sp', xr[i2].t[:], src(ti * 128, 128), reads=src_reads, writes=[xr[i2]])
                for h in range(2):
                    ps = self.PS[4 + h]
                    for k in range(8):
                        P.op('pe', lambda e, ps=ps, k=k, h=h, i2=i2: e.matmul(ps.t[:], yt[i2].t[:, k, :], wo.t[:, k, h * 512:(h + 1) * 512],
                                                                           start=(k == 0), stop=(k == 7)), reads=[yt[i2], wo], writes=[ps])
                    P.op('dve', lambda e, ps=ps, h=h, i2=i2, gb=gb: e.tensor_tensor(pre[i2].t[:, h * 512:(h + 1) * 512], ps.t[:],
                                                                             gb.t[:, h * 512:(h + 1) * 512], op=ALU.mult),
                         reads=[ps, gb], writes=[pre[i2]])
                P.op('dve', lambda e, i2=i2: e.scalar_tensor_tensor(pre[i2].t[:], xr[i2].t[:], ALPHA, pre[i2].t[:], ALU.mult, ALU.add),
                     reads=[xr[i2], pre[i2]], writes=[pre[i2]])
                self.ln_tail(pre[i2], self.prow_sb.t[:, 0:1024], self.prow_sb.t[:, 1024:2048], ot[i2], tmp)
                P.dma('pool', self.XA.t.ap()[ti * 128:(ti + 1) * 128, :], ot[i2].t[:], reads=[ot[i2]], writes=[self.XA])
            P.barrier()
            P.stack = old

    def ph_ffn_ln2(self, b, l):
        P, nc = self.P, self.nc
        P.barrier()
        last = (l == self.depth - 1)
        tok_lo = NCTX if last else 0
        XA = self.XA
        with contextlib.ExitStack() as st:
            P.stack, old = st, P.stack
            hT = P.sbuf("hT", [128, NHC, S], BF16)
            with contextlib.ExitStack() as st2:
                P.stack = st2
                xT = P.sbuf("x2T", [128, 8, S], BF16)
                xtb = [P.sbuf("xtb%d" % i, [128, 4, D], F32) for i in range(2)]
                self._src_reads = [XA]
                segs = self.seq_segs(b, with_ctx=not last)
                self.build_xT(lambda tok0, n: XA.t.ap()[tok0:tok0 + n, :], xT, S, 24, 32, b, segs, xtb)
                wg = [P.sbuf("wg%d" % i, [128, 8, 128], BF16) for i in range(2)]
                wu = [P.sbuf("wu%d" % i, [128, 8, 128], BF16) for i in range(2)]
                sg = [P.sbuf("sg%d" % i, [128, 512], F32) for i in range(2)]
                blocks = [(t, min(512, S - t)) for t in range(tok_lo, S, 512)]
                for hc in range(NHC):
                    i2 = hc % 2
                    P.dma('pool', wg[i2].t[:], self.w13[l][:, hc * 128:(hc + 1) * 128].rearrange("(k p) c -> p k c", p=128), writes=[wg[i2]])
                    P.dma('pool', wu[i2].t[:], self.w13[l][:, HID + hc * 128:HID + (hc + 1) * 128].rearrange("(k p) c -> p k c", p=128),
                          writes=[wu[i2]])
                    for bi, (t0, n) in enumerate(blocks):
                        pg = self.PS[(bi % 2) * 2]
                        pu = self.PS[(bi % 2) * 2 + 1]
                        for k in range(8):
                            P.op('pe', lambda e, pg=pg, k=k, i2=i2, t0=t0, n=n: e.matmul(pg.t[:, 0:n], wg[i2].t[:, k, :], xT.t[:, k, t0:t0 + n],
                                                                                     start=(k == 0), stop=(k == 7)), reads=[wg[i2], xT], writes=[pg])
                        for k in range(8):
                            P.op('pe', lambda e, pu=pu, k=k, i2=i2, t0=t0, n=n: e.matmul(pu.t[:, 0:n], wu[i2].t[:, k, :], xT.t[:, k, t0:t0 + n],
                                                                                     start=(k == 0), stop=(k == 7)), reads=[wu[i2], xT], writes=[pu])
                        s = sg[bi % 2]
                        P.op('act', lambda e, s=s, pg=pg, n=n: e.activation(s.t[:, 0:n], pg.t[:, 0:n], AF.Silu), reads=[pg], writes=[s])
                        P.op('dve', lambda e, s=s, pu=pu, n=n, hc=hc, t0=t0: e.tensor_tensor(hT.t[:, hc, t0:t0 + n], s.t[:, 0:n], pu.t[:, 0:n],
                                                                                       op=ALU.mult), reads=[s, pu], writes=[hT])
                P.barrier()
            P.stack = st
            w2 = P.sbuf("w2", [128, NHC, D], BF16)
            for hc in range(NHC):
                P.dma('pool', w2.t[:, hc, :], self.w2[l][hc * 128:(hc + 1) * 128, :], writes=[w2])
            self.prow_sb = P.sbuf("prow_sb2", [128, 2048], F32)
            P.dma('sp', self.prow_sb.t[:], self.prow[l][:, PR_LN2W:PR_LN2W + 2048].partition_broadcast(128), writes=[self.prow_sb])
            self.ln_st = P.sbuf("ln_st2", [128, 12], F32)
            self.ln_mv = P.sbuf("ln_mv2", [128, 8], F32)
            self.diag_tmp = P.sbuf("diag_tmp2", [128, 128], F32)
            self.ones_f = ones_f_ = P.sbuf("ones_f2", [128, 128], F32)
            P.op('pool', lambda e: e.memset(ones_f_.t[:], 1.0), writes=[ones_f_])
            gbl = P.sbuf("gbl2", [128, D], F32)
            gbc = P.sbuf("gbc2", [128, D], F32)
            self.gate_bcast(40, b, gbl)
            self.gate_bcast(40, NB, gbc)
            xr = [P.sbuf("xr2%d" % i, [128, D], F32) for i in range(2)]
            pre = [P.sbuf("pre2%d" % i, [128, D], F32) for i in range(2)]
            tmp = P.sbuf("lntmp2", [128, D], F32)
            ot = [P.sbuf("ot2%d" % i, [128, D], F32) for i in range(2)]
            for ti in range(2 if last else 0, NT):
                i2 = ti % 2
                gb = gbc if ti < 2 else gbl
                P.dma('sp', xr[i2].t[:], XA.t.ap()[ti * 128:(ti + 1) * 128, :], reads=[XA], writes=[xr[i2]])
                for h in range(2):
                    ps = self.PS[4 + h]
                    for hc in range(NHC):
                        P.op('pe', lambda e, ps=ps, hc=hc, h=h, ti=ti: e.matmul(ps.t[:], hT.t[:, hc, ti * 128:(ti + 1) * 128],
                                                                             w2.t[:, hc, h * 512:(h + 1) * 512], start=(hc == 0), stop=(hc == NHC - 1)),
                             reads=[hT, w2], writes=[ps])
                    P.op('dve', lambda e, ps=ps, h=h, i2=i2, gb=gb: e.tensor_tensor(pre[i2].t[:, h * 512:(h + 1) * 512], ps.t[:],
                                                                             gb.t[:, h * 512:(h + 1) * 512], op=ALU.mult),
                         reads=[ps, gb], writes=[pre[i2]])
                P.op('dve', lambda e, i2=i2: e.scalar_tensor_tensor(pre[i2].t[:], xr[i2].t[:], ALPHA, pre[i2].t[:], ALU.mult, ALU.add),
                     reads=[xr[i2], pre[i2]], writes=[pre[i2]])
                self.ln_tail(pre[i2], self.prow_sb.t[:, 0:1024], self.prow_sb.t[:, 1024:2048], ot[i2], tmp)
                if last:
                    P.dma('pool', self.y[b][(ti - 2) * 128:(ti - 1) * 128, :], ot[i2].t[:], reads=[ot[i2]])
                else:
                    P.dma('pool', self.XB.t.ap()[ti * 128:(ti + 1) * 128, :], ot[i2].t[:], reads=[ot[i2]], writes=[self.XB])
            P.barrier()
            P.stack = old

    def dump(self, buf, name, shape, dt=F32):
        ap = self.dbg_out(name, shape, dt)
        self.P.dma('sp', ap, buf.t.ap(), reads=[buf])


def host_inputs(inputs, core, nb=NB):
    f = lambda a: np.ascontiguousarray(np.asarray(a, dtype=np.float32))
    b0 = core * nb
    m = {}
    m["x"] = f(inputs["x"][b0:b0 + nb])
    m["ctx"] = f(inputs["ctx"][b0:b0 + nb])
    cc = np.zeros((NB + 1, D), np.float32)
    cc[:nb] = inputs["c"][b0:b0 + nb]
    cc[NB] = inputs["c_ctx"]
    m["cc"] = cc
    for k in ("w_mod", "w_in", "w_out", "ffn_w13", "ffn_w2"):
        m[k] = f(inputs[k])
    pcol = np.zeros((DEPTH, 128, NPC), np.float32)
    prow = np.zeros((DEPTH, 1, NPR), np.float32)
    gup = np.zeros((DEPTH, 2, 16, 3, 2, 64), np.float32)
    for l in range(DEPTH):
        for d in range(2):
            pcol[l, :, PC_W0 + d * 3:PC_W0 + d * 3 + 3] = _vec3(inputs["rw_w0"][l, d])
            pcol[l, :, PC_A0 + d * 3:PC_A0 + d * 3 + 3] = _vec3(inputs["rw_a0"][l, d])
            pcol[l, :, PC_GB + d * 3:PC_GB + d * 3 + 3] = _pad_gla(inputs["gla_gate_b"][l, d])
            gup[l, d, :, :, :, :32] = inputs["gla_gate_up"][l, d].reshape(16, 3, 2, 32)
        pcol[l, :, PC_KK:PC_KK + 3] = _vec3(inputs["rw_k_k"][l])
        pcol[l, :, PC_KA:PC_KA + 3] = _vec3(inputs["rw_k_a"][l])
        pcol[l, :, PC_RK:PC_RK + 3] = _vec3(inputs["rw_r_k"][l])
        pcol[l, :, PC_BMOD:PC_BMOD + 48] = inputs["b_mod"][l].reshape(48, 128).T
        prow[l, 0, PR_LN1W:PR_LN1W + 1024] = inputs["ln1_w"][l]
        prow[l, 0, PR_LN1B:PR_LN1B + 1024] = inputs["ln1_b"][l]
        prow[l, 0, PR_LN2W:PR_LN2W + 1024] = inputs["ln2_w"][l]
        prow[l, 0, PR_LN2B:PR_LN2B + 1024] = inputs["ln2_b"][l]
        prow[l, 0, PR_GNORM:PR_GNORM + 384] = np.tile(inputs["gla_norm_w"][l], 6)
        prow[l, 0, PR_GNW:PR_GNW + 384] = inputs["rw_gn_w"][l]
        prow[l, 0, PR_GNB:PR_GNB + 384] = inputs["rw_gn_b"][l]
        prow[l, 0, PR_MU:PR_MU + 1536] = inputs["rw_mu"][l]
    m["pcol"] = pcol
    m["prow"] = prow
    m["gup"] = np.ascontiguousarray(gup.reshape(DEPTH, 2, 16, 384))
    m["cst"] = _host_constants()
    C, Sn = _rope_tables()
    m["ropeC"], m["ropeS"] = C, Sn
    m["nattab"] = _nat_tables(np.asarray(inputs["nat_rpb"], np.float32))
    m["wd2"] = f(inputs["rw_wd2"]).reshape(DEPTH, 128, 384)
    m["wa2"] = f(inputs["rw_wa2"]).reshape(DEPTH, 128, 384)
    m["wg2"] = f(inputs["rw_wg2"])
    return m


def _ph_x(self, b, l, need_dx=True):
    P = self.P
    xT = P.sbuf("xmodT", [128, 8, S], BF16)
    dxT = P.sbuf("dxT", [128, 8, S], BF16) if need_dx else None
    outer = P.stack
    with contextlib.ExitStack() as st:
        P.stack = st
        xtb = [P.sbuf("xtb%d" % i, [128, 4, D], F32) for i in range(2)]
        src, src_reads = self.src_rows_fn(b, l)
        self._src_reads = src_reads
        self.build_xT(src, xT, S, 0, 8, b, self.seq_segs(b), xtb)
        if need_dx:
            tmp = P.sbuf("dxtmp", [128, 2, LAT], F32)
            for (t0, n) in ((0, NCTX), (NCTX, LAT)):
                for c in range(0, 8, 2):
                    P.op('dve', lambda e, t0=t0, n=n, c=c: e.tensor_tensor(tmp.t[:, :, 0:n - 2], xT.t[:, c:c + 2, t0:t0 + n - 2],
                                                                        xT.t[:, c:c + 2, t0 + 2:t0 + n], op=ALU.add), reads=[xT], writes=[tmp])
                    P.op('dve', lambda e, t0=t0, n=n, c=c: e.scalar_tensor_tensor(dxT.t[:, c:c + 2, t0 + 1:t0 + n - 1], tmp.t[:, :, 0:n - 2], 0.5,
                                                                               xT.t[:, c:c + 2, t0 + 1:t0 + n - 1], ALU.mult, ALU.subtract),
                         reads=[tmp, xT], writes=[dxT])
                P.op('dve', lambda e, t0=t0: e.scalar_tensor_tensor(dxT.t[:, :, t0:t0 + 1], xT.t[:, :, t0 + 1:t0 + 2], 0.5, xT.t[:, :, t0:t0 + 1],
                                                                 ALU.mult, ALU.subtract), reads=[xT], writes=[dxT])
                P.op('dve', lambda e, t0=t0, n=n: e.scalar_tensor_tensor(dxT.t[:, :, t0 + n - 1:t0 + n], xT.t[:, :, t0 + n - 2:t0 + n - 1], 0.5,
                                                                      xT.t[:, :, t0 + n - 1:t0 + n], ALU.mult, ALU.subtract), reads=[xT], writes=[dxT])
        P.barrier()
    P.stack = outer
    return xT, dxT


Kern.ph_x = _ph_x


def _ph_nat(self, b, l, xT):
    P = self.P
    last = (l == self.depth - 1)
    with contextlib.ExitStack() as st:
        P.stack, old = st, P.stack
        qT = P.sbuf("n_qT", [128, 2, S], BF16)
        kT = P.sbuf("n_kT", [128, 2, S], BF16)
        V = P.sbuf("n_V", [128, NT, 256], BF16)
        Vs = P.sbuf("n_Vs", [128, 15, 256], BF16)
        yTn = P.sbuf("n_yT", [128, 2, S], BF16)
        tab = P.sbuf("n_tab", [128, 4, 14 * 64], F32)
        onesb = P.sbuf("n_ones", [128, 128], BF16)
        P.op('pool', lambda e: e.memset(onesb.t[:], 1.0), writes=[onesb])
        for h in range(4):
            P.dma('sp', tab.t[:, h, :], self.nattab[l, h], writes=[tab])
        wq = P.sbuf("n_wq", [128, 8, 256], BF16)
        wk = P.sbuf("n_wk", [128, 8, 256], BF16)
        wv = P.sbuf("n_wv", [128, 8, 256], BF16)
        for (w, c0) in ((wq, NQ), (wk, NK), (wv, NV)):
            P.dma('pool', w.t[:], self.w_in[l][:, c0:c0 + 256].rearrange("(k p) c -> p k c", p=128), writes=[w])
        blocks = [(t, min(512, S - t)) for t in range(0, S, 512)]
        n = 0
        for (w, dst) in ((wq, qT), (wk, kT)):
            for tl in range(2):
                for (t0, nn) in blocks:
                    ps = self.PS[n % 2]
                    for k in range(8):
                        P.op('pe', lambda e, ps=ps, w=w, tl=tl, k=k, t0=t0, nn=nn: e.matmul(
                            ps.t[:, 0:nn], w.t[:, k, tl * 128:(tl + 1) * 128], xT.t[:, k, t0:t0 + nn], start=(k == 0), stop=(k == 7)),
                            reads=[w, xT], writes=[ps])
                    if n % 2 == 0:
                        P.op('dve', lambda e, ps=ps, dst=dst, tl=tl, t0=t0, nn=nn: e.tensor_copy(dst.t[:, tl, t0:t0 + nn], ps.t[:, 0:nn]),
                             reads=[ps], writes=[dst])
                    else:
                        P.op('act', lambda e, ps=ps, dst=dst, tl=tl, t0=t0, nn=nn: e.activation(dst.t[:, tl, t0:t0 + nn], ps.t[:, 0:nn], AF.Identity),
                             reads=[ps], writes=[dst])
                    n += 1
        for (dst, nt, base) in ((V, NT, 0), (Vs, 15, NCTX + 64)):
            for ti in range(nt):
                ps = self.PS[2 + ti % 2]
                t0 = base + ti * 128
                for k in range(8):
                    P.op('pe', lambda e, ps=ps, k=k, t0=t0: e.matmul(ps.t[:, 0:256], xT.t[:, k, t0:t0 + 128], wv.t[:, k, :], start=(k == 0), stop=(k == 7)),
                         reads=[wv, xT], writes=[ps])
                if ti % 2 == 0:
                    P.op('dve', lambda e, ps=ps, dst=dst, ti=ti: e.tensor_copy(dst.t[:, ti, :], ps.t[:, 0:256]), reads=[ps], writes=[dst])
                else:
                    P.op('act', lambda e, ps=ps, dst=dst, ti=ti: e.activation(dst.t[:, ti, :], ps.t[:, 0:256], AF.Identity), reads=[ps], writes=[dst])
        stt = [P.sbuf("n_stt%d" % i, [128, 4, 64], F32) for i in range(2)]
        E = [P.sbuf("n_E%d" % i, [128, 6, 64], BF16) for i in range(2)]
        rB = P.sbuf("n_rB", [128, 2, 64], F32)
        it = 0
        for r in range(32):
            r0 = min(max(r - 4, 0), 24)
            off = r0 - r + 7
            q0 = NCTX + r * 64
            for hp in range(2):
                pso = self.PS[4 + (it % 2)]
                for e2 in range(2):
                    h = hp * 2 + e2
                    pb = e2 * 64
                    pss = self.PS[(it % 2) * 2 + e2]
                    Eh = E[e2]
                    for j in range(6):
                        k0 = (NCTX + r0 * 64 + j * 128) if j < 4 else (j - 4) * 128
                        P.op('pe', lambda e, pss=pss, j=j, pb=pb, hp=hp, k0=k0, q0=q0: e.matmul(
                            pss.t[:, j * 64:(j + 1) * 64], kT.t[pb:pb + 64, hp, k0:k0 + 128], qT.t[pb:pb + 64, hp, q0:q0 + 64], start=True, stop=True),
                            reads=[kT, qT], writes=[pss])
                    P.op('dve', lambda e, pss=pss, e2=e2, h=h, off=off: e.scalar_tensor_tensor(
                        stt[e2].t[:], pss.t[:, 0:256].rearrange("p (a b) -> p a b", a=4), 0.125,
                        tab.t[:, h, :].rearrange("p (a b) -> p a b", a=14)[:, off:off + 7:2, :], ALU.mult, ALU.add),
                        reads=[pss, tab], writes=[stt[e2]])
                    P.op('act', lambda e, Eh=Eh, e2=e2: e.activation(Eh.t[:, 0:4, :], stt[e2].t[:], AF.Exp), reads=[stt[e2]], writes=[Eh])
                    P.op('act', lambda e, Eh=Eh, pss=pss: e.activation(Eh.t[:, 4:6, :], pss.t[:, 256:384].rearrange("p (a b) -> p a b", a=2), AF.Exp, scale=0.125),
                         reads=[pss], writes=[Eh])
                first = True
                for e2 in range(2):
                    Eh = E[e2]
                    for j in range(6):
                        if j < 4:
                            if r0 % 2 == 0:
                                vt = V.t[:, 2 + r0 // 2 + j, hp * 128:(hp + 1) * 128]
                            else:
                                vt = Vs.t[:, (r0 - 1) // 2 + j, hp * 128:(hp + 1) * 128]
                        else:
                            vt = V.t[:, j - 4, hp * 128:(hp + 1) * 128]
                        P.op('pe', lambda e, pso=pso, vt=vt, Eh=Eh, j=j, e2=e2, first=first: e.matmul(
                            pso.t[:, e2 * 64:(e2 + 1) * 64], vt, Eh.t[:, j, :], start=first, stop=(j == 5), skip_group_check=True),
                            reads=[V, Vs, Eh], writes=[pso])
                        first = False
                        P.op('pe', lambda e, pso=pso, Eh=Eh, j=j, e2=e2: e.matmul(
                            pso.t[:, 128 + e2 * 64:128 + (e2 + 1) * 64], onesb.t[:], Eh.t[:, j, :], start=False, stop=(j == 5), skip_group_check=True),
                            reads=[onesb, Eh], writes=[pso])
                P.op('dve', lambda e, pso=pso: e.reciprocal(rB.t[:], pso.t[:, 128:256].rearrange("p (a b) -> p a b", a=2)), reads=[pso], writes=[rB])
                for e2 in range(2):
                    pb = e2 * 64
                    P.op('dve', lambda e, pso=pso, e2=e2, pb=pb, hp=hp, q0=q0: e.tensor_tensor(
                        yTn.t[pb:pb + 64, hp, q0:q0 + 64], pso.t[pb:pb + 64, e2 * 64:(e2 + 1) * 64], rB.t[pb:pb + 64, e2, :], op=ALU.mult),
                        reads=[pso, rB], writes=[yTn])
                it += 1
        if not last:
            Ec = [P.sbuf("n_Ec%d" % i, [128, 2, 256], BF16) for i in range(2)]
            rBc = P.sbuf("n_rBc", [128, 256], F32)
            for hp in range(2):
                for e2 in range(2):
                    pb = e2 * 64
                    pss = self.PS[e2]
                    pso = self.PS[2 + e2]
                    for j in range(2):
                        P.op('pe', lambda e, pss=pss, j=j, pb=pb, hp=hp: e.matmul(
                            pss.t[:, j * 256:(j + 1) * 256], kT.t[pb:pb + 64, hp, j * 128:(j + 1) * 128], qT.t[pb:pb + 64, hp, 0:256], start=True, stop=True),
                            reads=[kT, qT], writes=[pss])
                    P.op('act', lambda e, pss=pss, e2=e2: e.activation(Ec[e2].t[:], pss.t[:].rearrange("p (a b) -> p a b", a=2), AF.Exp, scale=0.125),
                         reads=[pss], writes=[Ec[e2]])
                    for j in range(2):
                        P.op('pe', lambda e, pso=pso, j=j, hp=hp, e2=e2: e.matmul(
                            pso.t[:, 0:256], V.t[:, j, hp * 128:(hp + 1) * 128], Ec[e2].t[:, j, :], start=(j == 0), stop=(j == 1), skip_group_check=True),
                            reads=[V, Ec[e2]], writes=[pso])
                        P.op('pe', lambda e, pso=pso, j=j, e2=e2: e.matmul(
                            pso.t[:, 256:512], onesb.t[:], Ec[e2].t[:, j, :], start=False, stop=(j == 1), skip_group_check=True),
                            reads=[onesb, Ec[e2]], writes=[pso])
                    P.op('dve', lambda e, pso=pso: e.reciprocal(rBc.t[:], pso.t[:, 256:512]), reads=[pso], writes=[rBc])
                    P.op('dve', lambda e, pso=pso, pb=pb, hp=hp: e.tensor_tensor(
                        yTn.t[pb:pb + 64, hp, 0:256], pso.t[pb:pb + 64, 0:256], rBc.t[pb:pb + 64, :], op=ALU.mult),
                        reads=[pso, rBc], writes=[yTn])
        t_lo = NCTX if last else 0
        for tl in range(2):
            P.dma('sp', self.YT.t.ap()[384 + tl * 128:384 + (tl + 1) * 128, t_lo:S], yTn.t[:, tl, t_lo:S], reads=[yTn], writes=[self.YT])
        P.barrier()
        P.stack = old


Kern.ph_nat = _ph_nat


class _Ops:
    def __init__(self, P):
        self.P = P

    def TT(self, eng, out, a, b, op, reads, writes):
        self.P.op(eng, lambda e: e.tensor_tensor(out, a, b, op=op), reads, writes)

    def TS(self, eng, out, a, s1, s2, op0, op1, reads, writes):
        if op1 is None:
            self.P.op(eng, lambda e: e.tensor_scalar(out, a, s1, None, op0), reads, writes)
        else:
            self.P.op(eng, lambda e: e.tensor_scalar(out, a, s1, s2, op0, op1), reads, writes)

    def STT(self, out, a, s, b, op0, op1, reads, writes):
        self.P.op('dve', lambda e: e.scalar_tensor_tensor(out, a, s, b, op0, op1), reads, writes)

    def ACT(self, out, in_, func, reads, writes, **kw):
        self.P.op('act', lambda e: e.activation(out, in_, func, **kw), reads, writes)

    def CP(self, eng, out, in_, reads, writes):
        if eng == 'act':
            self.P.op('act', lambda e: e.activation(out, in_, AF.Identity), reads, writes)
        else:
            self.P.op(eng, lambda e: e.tensor_copy(out, in_), reads, writes)

    def MM(self, out, lhsT, rhs, start, stop, reads, writes):
        self.P.op('pe', lambda e: e.matmul(out, lhsT, rhs, start=start, stop=stop, skip_group_check=True), reads, writes)

    def TR(self, out, in_, ident, reads, writes):
        self.P.op('pe', lambda e: e.transpose(out, in_, ident), reads, writes)


def _ph_scan(self, b, l, xT, dxT, kind):
    P = self.P
    O = _Ops(P)
    rw = (kind == 'rw')
    NK = 16 if rw else 8
    PS = self.PS
    cst = self.cst_sb
    pc = self.pcol_sb
    with contextlib.ExitStack() as st:
        P.stack, old = st, P.stack
        Of = self.OF
        of_sbs = [P.sbuf("s_of%d" % i, [128, 384], F32) for i in range(2)]
        COEF = P.sbuf("s_coef", [128, NT, 8], F32) if rw else None
        mskF = P.sbuf("s_mskF", [128, 256], BF16)
        mskB = P.sbuf("s_mskB", [128, 256], BF16)
        lmF = P.sbuf("s_lmF", [128, 128], BF16)
        lmB = P.sbuf("s_lmB", [128, 128], BF16)
        hselb = P.sbuf("s_hsel", [128, 2], BF16)
        O.CP('dve', mskF.t[:], cst.t[:, C_MASKF:C_MASKF + 256], [cst], [mskF])
        O.CP('dve', mskB.t[:], cst.t[:, C_MASKB:C_MASKB + 256], [cst], [mskB])
        O.CP('dve', lmF.t[:], cst.t[:, C_LMF:C_LMF + 128], [cst], [lmF])
        O.CP('dve', lmB.t[:], cst.t[:, C_LMB:C_LMB + 128], [cst], [lmB])
        O.CP('dve', hselb.t[:], cst.t[:, C_HSEL:C_HSEL + 2], [cst], [hselb])
        prw = P.sbuf("s_prow", [128, 3 * 384], F32)
        P.dma('sp', prw.t[:], self.prow[l][:, PR_GNORM:PR_GNORM + 3 * 384].partition_broadcast(128), writes=[prw])
        if rw:
            ncolt = 9
            W = P.sbuf("s_W", [128, ncolt, 16, 128], BF16)
            Wv = P.sbuf("s_Wv", [128, 16, 384], BF16)
            with contextlib.ExitStack() as st2:
                P.stack = st2
                mub = P.sbuf("s_mub", [128, 1536], F32)
                P.dma('sp', mub.t[:], self.prow[l][:, PR_MU:PR_MU + 1536].partition_broadcast(128), writes=[mub])
                colt = [RR, RR + 128, RR + 256, RK, RK + 128, RK + 256, RDD, RAD, RGD]
                for i, c0 in enumerate(colt):
                    P.dma('pool', W.t[:, i, 0:8, :], self.w_in[l][:, c0:c0 + 128].rearrange("(k p) c -> p k c", p=128), writes=[W])
                    m0 = c0 - RR
                    O.TT('dve', W.t[:, i, 8:16, :], W.t[:, i, 0:8, :], mub.t[:, m0:m0 + 128][:, None, :].to_broadcast([128, 8, 128]), ALU.mult,
                         [W, mub], [W])
                P.dma('pool', Wv.t[:, 0:8, :], self.w_in[l][:, RV:RV + 384].rearrange("(k p) c -> p k c", p=128), writes=[Wv])
                O.TT('dve', Wv.t[:, 8:16, :], Wv.t[:, 0:8, :], mub.t[:, RV - RR:RV - RR + 384][:, None, :].to_broadcast([128, 8, 384]), ALU.mult,
                     [Wv, mub], [Wv])
                P.barrier()
            P.stack = st
            wd2 = P.sbuf("s_wd2", [128, 384], BF16)
            wa2 = P.sbuf("s_wa2", [128, 384], BF16)
            wg2 = P.sbuf("s_wg2", [128, 384], BF16)
            P.dma('pool', wd2.t[:], self.wd2[l], writes=[wd2])
            P.dma('pool', wa2.t[:], self.wa2[l], writes=[wa2])
            P.dma('pool', wg2.t[:], self.wg2[l], writes=[wg2])
            bones = cst.t[:, C_BONES:C_BONES + 128]
        else:
            W = P.sbuf("s_Wg", [128, 8, 4, 384], BF16)
            Wdn = P.sbuf("s_Wdn", [128, 8, 48], BF16)
            Wv = P.sbuf("s_Wv", [128, 8, 384], BF16)
            Wgt = P.sbuf("s_Wgt", [128, 8, 384], BF16)
            gup = P.sbuf("s_gup", [48, 384], BF16)
            P.op('pool', lambda e: e.memset(W.t[:], 0.0), writes=[W])
            P.op('pool', lambda e: e.memset(Wdn.t[:], 0.0), writes=[Wdn])
            P.op('pool', lambda e: e.memset(gup.t[:], 0.0), writes=[gup])
            with contextlib.ExitStack() as st2:
                P.stack = st2
                wqk = P.sbuf("s_wqk", [128, 8, 384], BF16)
                P.dma('pool', wqk.t[:], self.w_in[l][:, GQ:GQ + 384].rearrange("(k p) c -> p k c", p=128), writes=[wqk])
                for qi in range(2):
                    src = wqk.t[:, :, qi * 192:(qi + 1) * 192].rearrange("p k (h c) -> p k h c", h=6)
                    dst = W.t[:, :, 2 * qi, :].rearrange("p k (h c) -> p k h c", h=6)[:, :, :, 0:32]
                    dstp = W.t[:, :, 2 * qi + 1, :].rearrange("p k (h c) -> p k h c", h=6)
                    for kk in range(8):
                        O.CP('dve', dst[:, kk], src[:, kk], [wqk], [W])
                        O.TS('dve', dstp[:, kk, :, 0:32:2], src[:, kk, :, 1:32:2], -1.0, None, ALU.mult, None, [wqk], [W])
                        O.CP('dve', dstp[:, kk, :, 1:32:2], src[:, kk, :, 0:32:2], [wqk], [W])
                P.barrier()
            P.stack = st
            P.dma('pool', Wdn.t[:, :, 0:16], self.w_in[l][:, GDN:GDN + 16].rearrange("(k p) c -> p k c", p=128), writes=[Wdn])
            P.dma('pool', Wdn.t[:, :, 32:48], self.w_in[l][:, GDN + 16:GDN + 32].rearrange("(k p) c -> p k c", p=128), writes=[Wdn])
            P.dma('pool', Wv.t[:], self.w_in[l][:, GV:GV + 384].rearrange("(k p) c -> p k c", p=128), writes=[Wv])
            P.dma('pool', Wgt.t[:], self.w_in[l][:, GG:GG + 384].rearrange("(k p) c -> p k c", p=128), writes=[Wgt])
            P.dma('pool', gup.t[0:16, :], self.gup[l, 0], writes=[gup])
            P.dma('pool', gup.t[32:48, :], self.gup[l, 1], writes=[gup])
            ropeC = P.sbuf("s_ropeC", [128, 256], F32)
            ropeS = P.sbuf("s_ropeS", [128, 256], F32)
        if Prog.limit is not None:
            print("MS weights", P.n_ops)
        KR = [P.sbuf("s_KR%d" % p, [128, 2, 2, 128], BF16) for p in range(3)]
        Kg = [P.sbuf("s_Kg%d" % p, [128, 256], BF16) for p in range(3)]
        Kgp = [P.sbuf("s_Kgp%d" % p, [128, 256], F32) for p in range(3)]
        if rw:
            Bg = [P.sbuf("s_Bg%d" % p, [128, 256], BF16) for p in range(3)]
            Bgp = [P.sbuf("s_Bgp%d" % p, [128, 256], F32) for p in range(3)]
        else:
            for p in range(3):
                P.op('pool', lambda e, p=p: e.memset(KR[p].t[:], 0.0), writes=[KR[p]])
        GC = P.sbuf("s_GC", [128, 3, 2], F32)
        V = P.sbuf("s_V", [128, 2, 384], BF16)
        Vf = P.sbuf("s_Vf", [128, 2, 384], F32)
        ft = {n: P.sbuf("s_f_" + n, [128, 256], F32) for n in
              (["rf", "kf", "sg", "css", "g", "gi", "gex", "gp", "a", "kk", "t1", "kmod", "kka", "t2"] if rw else
               ["sg", "css", "g", "gi", "gp", "t1", "t2", "qr", "kr"])}
        if rw:
            tdd = P.sbuf("s_tdd", [128, 256], BF16)
            adb = P.sbuf("s_adb", [128, 256], BF16)
            sgd = P.sbuf("s_sgd", [128, 256], BF16)
            prodb = P.sbuf("s_prodb", [128, 256], BF16)
        else:
            dnb = P.sbuf("s_dnb", [48, 256], BF16)
        LMC = P.sbuf("s_LMC", [128, 6, 2, 256], BF16)
        Lm = P.sbuf("s_L", [128, 6, 128], BF16)
        PTs = [P.sbuf("s_PT%d" % i, [128, 6, 128], BF16) for i in range(2)]
        Psq = [P.sbuf("s_Pq%d" % i, [128, 6, 128], BF16) for i in range(2)]
        Ub = P.sbuf("s_Ub", [128, 384], BF16)
        BKT = P.sbuf("s_BKT", [128, 6, 128], BF16)
        Tw = P.sbuf("s_Tw", [128, 3, 128], F32)
        Twb = P.sbuf("s_Twb", [128, 3, 128], BF16)
        ttmp = P.sbuf("s_ttmp", [128, 3, 128], F32)
        o_sbs = [P.sbuf("s_o%d" % i, [128, 384], F32) for i in range(2)]
        y_sb = P.sbuf("s_y", [128, 384], F32)
        sq_sb = P.sbuf("s_sq", [128, 384], F32)
        gt_sb = P.sbuf("s_gt", [128, 384], F32)
        yb = P.sbuf("s_yb", [128, 384], F32)
        stt = P.sbuf("s_stat", [128, 8, 8], F32)
        yTo = P.sbuf("s_yTo", [128, 3, 256], BF16)
        nrm = prw.t[:, 0:384]
        gnw = prw.t[:, 384:768]
        gnb = prw.t[:, 768:1152]
        bdm = cst.t[:, C_BD:C_BD + 128]
        ident_b = self.identb
        feat0 = 5 * 128 if rw else 0

        for d in range(2):
            fwd = (d == 0)
            msk = mskF if fwd else mskB
            lm = lmF if fwd else lmB
            scm = cst.t[:, C_SCF:C_SCF + 256] if fwd else cst.t[:, C_SCB:C_SCB + 256]
            cend = 127 if fwd else 0
            O.P.op('pool', lambda e: e.memset(Tw.t[:], 0.0), writes=[Tw])
            O.P.op('pool', lambda e: e.memset(Twb.t[:], 0.0), writes=[Twb])
            border = list(range(NBLK)) if fwd else [0] + list(range(NBLK - 1, 0, -1))
            for bi in border:
                t0 = bi * CB
                lat = bi > 0

                def rhs(k):
                    return xT.t[:, k, t0:t0 + CB] if k < 8 else dxT.t[:, k - 8, t0:t0 + CB]

                def lhs_tok(k, c):
                    return xT.t[:, k, t0 + c * 128:t0 + (c + 1) * 128] if k < 8 else dxT.t[:, k - 8, t0 + c * 128:t0 + (c + 1) * 128]
                xr = [xT, dxT] if rw else [xT]
                for c in range(2):
                    ps = PS[c]
                    for k in range(NK):
                        O.MM(ps.t[:, 0:384], lhs_tok(k, c), Wv.t[:, k, :], k == 0, k == NK - 1, xr + [Wv], [ps])
                    O.CP('act', V.t[:, c, :], ps.t[:, 0:384], [ps], [V])
                if rw:
                    def projF(i, ps, half):
                        o = ps.t[:, half * 256:(half + 1) * 256]
                        for k in range(16):
                            O.MM(o, W.t[:, i, k, :], rhs(k), k == 0, k == 15, xr + [W], [ps])
                        return o
                    o_dd = projF(6, PS[2], 0)
                    O.ACT(tdd.t[:], o_dd, AF.Tanh, [PS[2]], [tdd])
                    o_ad = projF(7, PS[2], 1)
                    O.CP('dve', adb.t[:], o_ad, [PS[2]], [adb])
                    if not fwd:
                        o_gd = projF(8, PS[3], 0)
                        O.ACT(sgd.t[:], o_gd, AF.Sigmoid, [PS[3]], [sgd])
                    for p in range(3):
                        f = ft
                        o_r = projF(p, PS[4], 0)
                        O.CP('act', f["rf"].t[:], o_r, [PS[4]], [f["rf"]])
                        o_k = projF(3 + p, PS[4], 1)
                        O.CP('dve', f["kf"].t[:], o_k, [PS[4]], [f["kf"]])
                        o_d = PS[5].t[:, 0:256]
                        O.MM(o_d, wd2.t[d * 64:(d + 1) * 64, p * 128:(p + 1) * 128], tdd.t[d * 64:(d + 1) * 64, :], True, True, [wd2, tdd], [PS[5]])
                        O.ACT(f["sg"].t[:], o_d, AF.Sigmoid, [PS[5], pc], [f["sg"]], bias=pc.t[:, PC_W0 + d * 3 + p:PC_W0 + d * 3 + p + 1])
                        o_a = PS[5].t[:, 256:512]
                        O.MM(o_a, wa2.t[d * 64:(d + 1) * 64, p * 128:(p + 1) * 128], adb.t[d * 64:(d + 1) * 64, :], True, True, [wa2, adb], [PS[5]])
                        O.ACT(f["a"].t[:], o_a, AF.Sigmoid, [PS[5], pc], [f["a"]], bias=pc.t[:, PC_A0 + d * 3 + p:PC_A0 + d * 3 + p + 1])
                        self._scan_gates(O, f, scm, fwd, cend, GC, p, DECAY)
                        O.TS('pool', f["kk"].t[:], f["kf"].t[:], pc.t[:, PC_KK + p:PC_KK + p + 1], None, ALU.mult, None, [f["kf"], pc], [f["kk"]])
                        O.TT('pool', f["t1"].t[:], f["kk"].t[:], f["kk"].t[:], ALU.mult, [f["kk"]], [f["t1"]])
                        o_ss = PS[6].t[:, 0:256]
                        O.MM(o_ss, bones, f["t1"].t[:], True, True, [cst, f["t1"]], [PS[6]])
                        O.ACT(f["t2"].t[:], o_ss, AF.Sqrt, [PS[6], cst], [f["t2"]], bias=cst.t[:, C_E12:C_E12 + 1])
                        O.P.op('dve', lambda e: e.reciprocal(f["t1"].t[:], f["t2"].t[:]), [f["t2"]], [f["t1"]])
                        O.TT('pool', f["kk"].t[:], f["kk"].t[:], f["t1"].t[:], ALU.mult, [f["kk"], f["t1"]], [f["kk"]])
                        O.TS('dve', f["t1"].t[:], f["a"].t[:], pc.t[:, PC_KA + p:PC_KA + p + 1], pc.t[:, NPC + p:NPC + p + 1], ALU.mult, ALU.add,
                             [f["a"], pc], [f["t1"]])
                        O.TT('pool', f["kmod"].t[:], f["kf"].t[:], f["t1"].t[:], ALU.mult, [f["kf"], f["t1"]], [f["kmod"]])
                        O.TT('pool', f["kka"].t[:], f["kk"].t[:], f["a"].t[:], ALU.mult, [f["kk"], f["a"]], [f["kka"]])
                        O.TT('dve', KR[p].t[:, :, 0, :], f["kk"].t[:].rearrange("p (c t) -> p c t", c=2), f["gex"].t[:].rearrange("p (c t) -> p c t", c=2),
                             ALU.mult, [f["kk"], f["gex"]], [KR[p]])
                        O.TT('pool', KR[p].t[:, :, 1, :], f["rf"].t[:].rearrange("p (c t) -> p c t", c=2), f["g"].t[:].rearrange("p (c t) -> p c t", c=2),
                             ALU.mult, [f["rf"], f["g"]], [KR[p]])
                        O.TT('dve', Kg[p].t[:], f["kmod"].t[:], f["gi"].t[:], ALU.mult, [f["kmod"], f["gi"]], [Kg[p]])
                        O.TT('pool', Kgp[p].t[:], f["kmod"].t[:], f["gp"].t[:], ALU.mult, [f["kmod"], f["gp"]], [Kgp[p]])
                        O.STT(Bg[p].t[:], f["kka"].t[:], -1.0, f["gi"].t[:], ALU.mult, ALU.mult, [f["kka"], f["gi"]], [Bg[p]])
                        O.STT(Bgp[p].t[:], f["kka"].t[:], -1.0, f["gp"].t[:], ALU.mult, ALU.mult, [f["kka"], f["gp"]], [Bgp[p]])
                        O.TT('pool', f["t1"].t[:], f["rf"].t[:], f["kmod"].t[:], ALU.mult, [f["rf"], f["kmod"]], [f["t1"]])
                        O.TS('dve', prodb.t[:], f["t1"].t[:], pc.t[:, PC_RK + p:PC_RK + p + 1], None, ALU.mult, None, [f["t1"], pc], [prodb])
                        for c in range(2):
                            O.MM(PS[7].t[:, c * 8 + p * 2:c * 8 + p * 2 + 2], prodb.t[:, c * 128:(c + 1) * 128], hselb.t[:], True, True,
                                 [prodb, hselb], [PS[7]])
                    for c in range(2):
                        ch = bi * 2 + c
                        if fwd:
                            O.CP('dve', COEF.t[:, ch, 0:6], PS[7].t[:, c * 8:c * 8 + 6], [PS[7]], [COEF])
                        else:
                            O.TT('dve', COEF.t[:, ch, 0:6], PS[7].t[:, c * 8:c * 8 + 6], COEF.t[:, ch, 0:6], ALU.add, [PS[7], COEF], [COEF])
                else:
                    f = ft
                    o_dn = PS[2].t[0:48, 0:256]
                    for k in range(8):
                        O.MM(o_dn, Wdn.t[:, k, :], rhs(k), k == 0, k == 7, [xT, Wdn], [PS[2]])
                    O.CP('dve', dnb.t[:], o_dn, [PS[2]], [dnb])
                    if lat:
                        lt0 = t0 - NCTX
                        P.dma('sp', ropeC.t[:], self.ropeC[:, lt0:lt0 + CB], writes=[ropeC])
                        P.dma('sp', ropeS.t[:], self.ropeS[:, lt0:lt0 + CB], writes=[ropeS])
                    for p in range(3):
                        o_z = PS[3].t[:, 0:256]
                        O.MM(o_z, gup.t[d * 32:d * 32 + 16, p * 128:(p + 1) * 128], dnb.t[d * 32:d * 32 + 16, :], True, True, [gup, dnb], [PS[3]])
                        O.ACT(f["t1"].t[:], o_z, AF.Exp, [PS[3], pc], [f["t1"]], scale=-1.0, bias=pc.t[:, NPC + 3 + d * 3 + p:NPC + 3 + d * 3 + p + 1])
                        O.ACT(f["sg"].t[:], f["t1"].t[:], AF.Ln, [f["t1"], cst], [f["sg"]], bias=cst.t[:, C_ONE:C_ONE + 1])
                        self._scan_gates(O, f, scm, fwd, cend, GC, p, 1.0 / 16.0)
                        for qi, (dst, nm) in enumerate(((None, "qr"), (None, "kr"))):
                            def projq(j, half):
                                o = PS[4 + (half // 2)].t[:, (half % 2) * 256:(half % 2 + 1) * 256]
                                for k in range(8):
                                    O.MM(o, W.t[:, k, j, p * 128:(p + 1) * 128], rhs(k), k == 0, k == 7, [xT, W], [PS[4 + (half // 2)]])
                                return o, PS[4 + (half // 2)]
                            o1, b1 = projq(2 * qi, 2 * qi)
                            if lat:
                                o2, b2 = projq(2 * qi + 1, 2 * qi + 1)
                                O.TT('dve', f["t1"].t[:], o1, ropeC.t[:], ALU.mult, [b1, ropeC], [f["t1"]])
                                O.TT('dve', f["t2"].t[:], o2, ropeS.t[:], ALU.mult, [b2, ropeS], [f["t2"]])
                                O.TT('pool', f[nm].t[:], f["t1"].t[:], f["t2"].t[:], ALU.add, [f["t1"], f["t2"]], [f[nm]])
                            else:
                                O.CP('dve', f[nm].t[:], o1, [b1], [f[nm]])
                        O.STT(KR[p].t[:, :, 1, :], f["qr"].t[:].rearrange("p (c t) -> p c t", c=2), 32.0 ** -0.5,
                              f["g"].t[:].rearrange("p (c t) -> p c t", c=2), ALU.mult, ALU.mult, [f["qr"], f["g"]], [KR[p]])
                        O.TT('dve', Kg[p].t[:], f["kr"].t[:], f["gi"].t[:], ALU.mult, [f["kr"], f["gi"]], [Kg[p]])
                        O.TT('pool', Kgp[p].t[:], f["kr"].t[:], f["gp"].t[:], ALU.mult, [f["kr"], f["gp"]], [Kgp[p]])
                if Prog.limit is not None:
                    print("MS prep", d, bi, P.n_ops)
                for c in ([0, 1] if fwd else [1, 0]):
                    ch = bi * 2 + c
                    cs = slice(c * 128, (c + 1) * 128)
                    o_sb = o_sbs[ch % 2]
                    of_sb = of_sbs[ch % 2]
                    if not fwd:
                        P.dma('sp', of_sb.t[:], Of.t.ap()[ch], reads=[Of], writes=[of_sb])
                    mb = msk.t[:, None, :].to_broadcast([128, 2, 256])
                    for p in range(3):
                        for e2 in range(2):
                            h = 2 * p + e2
                            pb = e2 * 64
                            ps = PS[e2]
                            krr = KR[p].t[pb:pb + 64, c, :, :].rearrange("p a t -> p (a t)")
                            if rw:
                                O.MM(ps.t[:, 0:256], Bg[p].t[pb:pb + 64, cs], krr, True, True, [Bg[p], KR[p]], [ps])
                            O.MM(ps.t[:, 256:512], Kg[p].t[pb:pb + 64, cs], krr, True, True, [Kg[p], KR[p]], [ps])
                            if rw:
                                O.TT('dve', LMC.t[:, h, :, :], ps.t[:].rearrange("p (a t) -> p a t", a=2), mb, ALU.mult, [ps, msk], [LMC])
                            else:
                                O.TT('dve', LMC.t[:, h, 1, :], ps.t[:, 256:512], msk.t[:], ALU.mult, [ps, msk], [LMC])
                    if rw:
                        for e2 in range(2):
                            pb = e2 * 64
                            ps = PS[e2]
                            for pp in range(3):
                                O.MM(ps.t[:, pp * 128:(pp + 1) * 128], KR[pp].t[pb:pb + 64, c, 0, :], Bg[pp].t[pb:pb + 64, cs], True, True,
                                     [KR[pp], Bg[pp]], [ps])
                            O.TT('dve', Lm.t[:, e2:6:2, :], ps.t[:, 0:384].rearrange("p (a t) -> p a t", a=3),
                                 lm.t[:, None, :].to_broadcast([128, 3, 128]), ALU.mult, [ps, lm], [Lm])
                        first = True
                        for p in range(3):
                            O.MM(PS[2].t[:, p * 128:(p + 1) * 128], KR[p].t[:, c, 0, :], Twb.t[:, p, :], first, False, [KR[p], Twb], [PS[2]])
                            first = False
                        for h in range(6):
                            O.MM(PS[2].t[:, h * 64:(h + 1) * 64], LMC.t[:, h, 1, 0:128], V.t[:, c, h * 64:(h + 1) * 64], False, False, [LMC, V], [PS[2]])
                        O.CP('act', Ub.t[:], PS[2].t[:, 0:384], [PS[2]], [Ub])
                        for lv in range(7):
                            def PT(h):
                                return LMC.t[:, h, 0, 0:128] if lv == 0 else PTs[lv % 2].t[:, h, :]

                            def PP(h):
                                return Lm.t[:, h, :] if lv == 0 else Psq[lv % 2].t[:, h, :]
                            ptb = LMC if lv == 0 else PTs[lv % 2]
                            ppb = Lm if lv == 0 else Psq[lv % 2]
                            if lv < 6:
                                for hg in range(2):
                                    ps = PS[4 + hg]
                                    for hh in range(3):
                                        h = hg * 3 + hh
                                        O.MM(ps.t[:, hh * 128:(hh + 1) * 128], PP(h), PT(h), True, True, [ptb, ppb], [ps])
                                    O.CP('dve', PTs[(lv + 1) % 2].t[:, hg * 3:hg * 3 + 3, :], ps.t[:, 0:384].rearrange("p (a t) -> p a t", a=3),
                                         [ps], [PTs[(lv + 1) % 2]])
                                if lv < 5:
                                    for hg in range(2):
                                        ps = PS[6 + hg]
                                        for hh in range(3):
                                            h = hg * 3 + hh
                                            O.MM(ps.t[:, hh * 128:(hh + 1) * 128], PT(h), PP(h), True, True, [ptb, ppb], [ps])
                                        O.CP('dve' if hg == 0 else 'act', Psq[(lv + 1) % 2].t[:, hg * 3:hg * 3 + 3, :],
                                             ps.t[:, 0:384].rearrange("p (a t) -> p a t", a=3), [ps], [Psq[(lv + 1) % 2]])
                            for h in range(6):
                                O.MM(PS[2].t[:, h * 64:(h + 1) * 64], PT(h), Ub.t[:, h * 64:(h + 1) * 64], False, lv == 6, [ptb, Ub], [PS[2]])
                            O.CP('act', Ub.t[:], PS[2].t[:, 0:384], [PS[2]], [Ub])
                    first = True
                    for p in range(3):
                        O.MM(PS[3].t[:, p * 128:(p + 1) * 128], KR[p].t[:, c, 1, :], Twb.t[:, p, :], first, False, [KR[p], Twb], [PS[3]])
                        first = False
                    for h in range(6):
                        if rw:
                            O.MM(PS[3].t[:, h * 64:(h + 1) * 64], LMC.t[:, h, 0, 128:256], Ub.t[:, h * 64:(h + 1) * 64], False, False, [LMC, Ub], [PS[3]])
                        O.MM(PS[3].t[:, h * 64:(h + 1) * 64], LMC.t[:, h, 1, 128:256], V.t[:, c, h * 64:(h + 1) * 64], False, h == 5, [LMC, V], [PS[3]])
                    if fwd:
                        O.CP('dve', o_sb.t[:], PS[3].t[:, 0:384], [PS[3]], [o_sb])
                        P.dma('sp', Of.t.ap()[ch], o_sb.t[:], reads=[o_sb], writes=[Of])
                    else:
                        O.TT('dve', o_sb.t[:], PS[3].t[:, 0:384], of_sb.t[:], ALU.add, [PS[3], of_sb], [o_sb])
                    idf = cst.t[:, C_IDENT:C_IDENT + 128]
                    for p in range(3):
                        if rw:
                            O.TR(PS[6].t[:, p * 128:(p + 1) * 128], Bgp[p].t[:, cs], idf, [Bgp[p], cst], [PS[6]])
                        O.TR(PS[0].t[:, p * 128:(p + 1) * 128], Kgp[p].t[:, cs], idf, [Kgp[p], cst], [PS[0]])
                    if rw:
                        O.CP('dve', BKT.t[:, 0:3, :], PS[6].t[:, 0:384].rearrange("p (a t) -> p a t", a=3), [PS[6]], [BKT])
                    O.CP('act', BKT.t[:, 3:6, :], PS[0].t[:, 0:384].rearrange("p (a t) -> p a t", a=3), [PS[0]], [BKT])
                    first = True
                    for p in range(3):
                        if rw:
                            O.MM(PS[7].t[:, p * 128:(p + 1) * 128], BKT.t[:, p, :], Ub.t[:, p * 128:(p + 1) * 128], first, False, [BKT, Ub], [PS[7]])
                            first = False
                        O.MM(PS[7].t[:, p * 128:(p + 1) * 128], BKT.t[:, 3 + p, :], V.t[:, c, p * 128:(p + 1) * 128], first, p == 2, [BKT, V], [PS[7]])
                        first = False
                    O.TT('dve', ttmp.t[:], PS[7].t[:, 0:384].rearrange("p (a t) -> p a t", a=3), bdm[:, None, :].to_broadcast([128, 3, 128]), ALU.mult,
                         [PS[7], cst], [ttmp])
                    for p in range(3):
                        O.STT(Tw.t[:, p, :], Tw.t[:, p, :], GC.t[:, p, c:c + 1], ttmp.t[:, p, :], ALU.mult, ALU.add, [Tw, GC, ttmp], [Tw])
                    O.CP('act', Twb.t[:], Tw.t[:], [Tw], [Twb])
                    if Prog.limit is not None:
                        print("MS chunk", d, bi, c, P.n_ops)
                    if not fwd:
                        o3 = o_sb.t[:].rearrange("p (h v) -> p h v", h=6)
                        if rw:
                            O.P.op('dve', lambda e, o3=o3: e.tensor_reduce(stt.t[:, 0, 0:6], o3, AX.X, ALU.add), [o_sb], [stt])
                        O.TT('pool', sq_sb.t[:], o_sb.t[:], o_sb.t[:], ALU.mult, [o_sb], [sq_sb])
                        O.P.op('dve', lambda e: e.tensor_reduce(stt.t[:, 1, 0:6], sq_sb.t[:].rearrange("p (h v) -> p h v", h=6), AX.X, ALU.add), [sq_sb], [stt])
                        if rw:
                            O.TS('dve', stt.t[:, 2, 0:6], stt.t[:, 0, 0:6], 1.0 / 64, None, ALU.mult, None, [stt], [stt])
                            O.TT('dve', stt.t[:, 3, 0:6], stt.t[:, 2, 0:6], stt.t[:, 2, 0:6], ALU.mult, [stt], [stt])
                            O.STT(stt.t[:, 4, 0:6], stt.t[:, 1, 0:6], 1.0 / 64, stt.t[:, 3, 0:6], ALU.mult, ALU.subtract, [stt], [stt])
                            O.TS('dve', stt.t[:, 4, 0:6], stt.t[:, 4, 0:6], RW_GN_EPS, None, ALU.add, None, [stt], [stt])
                        else:
                            O.TS('dve', stt.t[:, 4, 0:6], stt.t[:, 1, 0:6], 1.0 / 64, LN_EPS, ALU.mult, ALU.add, [stt], [stt])
                        O.ACT(stt.t[:, 5, 0:6], stt.t[:, 4, 0:6], AF.Sqrt, [stt], [stt])
                        O.P.op('dve', lambda e: e.reciprocal(stt.t[:, 6, 0:6], stt.t[:, 5, 0:6]), [stt], [stt])
                        y3 = y_sb.t[:].rearrange("p (h v) -> p h v", h=6)
                        rstd_b = stt.t[:, 6, 0:6][:, :, None].to_broadcast([128, 6, 64])
                        psg = PS[0]
                        if rw:
                            O.MM(psg.t[:, 0:384], sgd.t[:, cs], wg2.t[:], True, True, [sgd, wg2], [psg])
                            mean_b = stt.t[:, 2, 0:6][:, :, None].to_broadcast([128, 6, 64])
                            O.TT('dve', y3, o3, mean_b, ALU.subtract, [o_sb, stt], [y_sb])
                            O.TT('dve', y3, y3, rstd_b, ALU.mult, [y_sb, stt], [y_sb])
                            O.TT('pool', y_sb.t[:], y_sb.t[:], gnw, ALU.mult, [y_sb, prw], [y_sb])
                            O.TT('pool', y_sb.t[:], y_sb.t[:], gnb, ALU.add, [y_sb, prw], [y_sb])
                            coef_b = COEF.t[:, ch, 0:6][:, :, None].to_broadcast([128, 6, 64])
                            O.TT('dve', sq_sb.t[:].rearrange("p (h v) -> p h v", h=6), V.t[:, c, :].rearrange("p (h v) -> p h v", h=6), coef_b, ALU.mult,
                                 [V, COEF], [sq_sb])
                            O.TT('pool', y_sb.t[:], y_sb.t[:], sq_sb.t[:], ALU.add, [y_sb, sq_sb], [y_sb])
                            O.TT('dve', yb.t[:], psg.t[:, 0:384], y_sb.t[:], ALU.mult, [y_sb, psg], [yb])
                        else:
                            for k in range(8):
                                O.MM(psg.t[:, 0:384], lhs_tok(k, c), Wgt.t[:, k, :], k == 0, k == 7, [xT, Wgt], [psg])
                            O.ACT(gt_sb.t[:], psg.t[:, 0:384], AF.Silu, [psg], [gt_sb])
                            O.TT('dve', y3, o3, rstd_b, ALU.mult, [o_sb, stt], [y_sb])
                            O.TT('pool', y_sb.t[:], y_sb.t[:], nrm, ALU.mult, [y_sb, prw], [y_sb])
                            O.TT('pool', yb.t[:], y_sb.t[:], gt_sb.t[:], ALU.mult, [y_sb, gt_sb], [yb])
                        for p in range(3):
                            O.TR(PS[1].t[:, p * 128:(p + 1) * 128], yb.t[:, p * 128:(p + 1) * 128], cst.t[:, C_IDENT:C_IDENT + 128], [yb, cst], [PS[1]])
                        O.CP('act', yTo.t[:, :, cs], PS[1].t[:, 0:384].rearrange("p (a t) -> p a t", a=3), [PS[1]], [yTo])
                if not fwd:
                    for p in range(3):
                        P.dma('sp', self.YT.t.ap()[feat0 + p * 128:feat0 + (p + 1) * 128, t0:t0 + CB], yTo.t[:, p, :], reads=[yTo], writes=[self.YT])
        P.barrier()
        P.stack = old


def _scan_gates(self, O, f, scm, fwd, cend, GC, p, rate):
    sg, css = f["sg"], f["css"]
    if fwd:
        O.P.op('dve', lambda e: e.tensor_tensor_scan(css.t[:], scm, sg.t[:], 0.0, ALU.mult, ALU.add), [sg, self.cst_sb], [css])
    else:
        O.P.op('dve', lambda e: e.tensor_tensor_scan(css.t[:, ::-1], scm[:, ::-1], sg.t[:, ::-1], 0.0, ALU.mult, ALU.add), [sg, self.cst_sb], [css])
    O.ACT(f["g"].t[:], css.t[:], AF.Exp, [css], [f["g"]], scale=-rate)
    O.ACT(f["gi"].t[:], css.t[:], AF.Exp, [css], [f["gi"]], scale=rate)
    if "gex" in f:
        O.TT('pool', f["t2"].t[:], css.t[:], sg.t[:], ALU.subtract, [css, sg], [f["t2"]])
        O.ACT(f["gex"].t[:], f["t2"].t[:], AF.Exp, [f["t2"]], [f["gex"]], scale=-rate)
    c3 = css.t[:].rearrange("p (c t) -> p c t", c=2)
    cC = c3[:, :, cend:cend + 1]
    O.TT('dve', f["t2"].t[:].rearrange("p (c t) -> p c t", c=2), cC.to_broadcast([128, 2, 128]), c3, ALU.subtract, [css], [f["t2"]])
    O.ACT(f["gp"].t[:], f["t2"].t[:], AF.Exp, [f["t2"]], [f["gp"]], scale=-rate)
    O.ACT(GC.t[:, p, :], css.t[:, cend:256:128], AF.Exp, [css], [GC], scale=-rate)


Kern.ph_scan = _ph_scan
Kern._scan_gates = _scan_gates


def build_full(nb=NB):
    K = Kern(nb=nb)
    P = K.P
    for l in range(DEPTH):
        K.set_layer(l)
        K.ph_mod(l)
    for b in range(nb):
        for l in range(DEPTH):
            K.set_layer(l)
            P.barrier()
            with contextlib.ExitStack() as st:
                P.stack, old = st, P.stack
                xT, dxT = K.ph_x(b, l, need_dx=True)
                K.ph_nat(b, l, xT)
                K.ph_scan(b, l, xT, dxT, 'gla')
                K.ph_scan(b, l, xT, dxT, 'rw')
                P.barrier()
                P.stack = old
            K.ph_wout_ln1(b, l)
            K.ph_ffn_ln2(b, l)
    P.finish()
    return K


_CACHE = {}


def kernel(**inputs):
    n_cores = 8
    if "K" not in _CACHE:
        _CACHE["K"] = build_full(NB)
    K = _CACHE["K"]
    in_maps = [host_inputs(inputs, c, NB) for c in range(n_cores)]
    res = run_bass_kernel_spmd(K.nc, in_maps, core_ids=list(range(n_cores)))
    out = np.concatenate([np.asarray(r["y"]) for r in res.results], axis=0)
    return out.astype(np.float32)
```

```python
import contextlib
import numpy as np
import concourse.bass as bass
import concourse.mybir as mybir
from concourse.bass_utils import run_bass_kernel_spmd

F32 = mybir.dt.float32
BF16 = mybir.dt.bfloat16
AF = mybir.ActivationFunctionType
ALU = mybir.AluOpType
AX = mybir.AxisListType

ENGS = ('pe', 'dve', 'act', 'pool', 'sp')


class Buf:
    def __init__(self, t, name):
        self.t = t
        self.name = name
        self.last_write = None
        self.reads = {}


class Prog:
    SAME_ENGINE_SYNC = True

    def __init__(self, nc, n_dma_sems=12):
        self.nc = nc
        self.stack = contextlib.ExitStack()
        self.ops = {e: [] for e in ENGS}
        self.sem = {}
        for e in ('pe', 'dve', 'act', 'pool'):
            self.sem[e] = self.stack.enter_context(nc.semaphore("s_" + e))
        self.cnt = {e: 0 for e in ('pe', 'dve', 'act', 'pool')}
        self.seen = {e: {} for e in ENGS}
        self.dsems = {}
        self.dcur = {}
        for q in ('sp', 'pool', 'act'):
            self.dsems[q] = []
            for i in range(n_dma_sems):
                key = "d_%s_%d" % (q, i)
                self.sem[key] = self.stack.enter_context(nc.semaphore(key))
                self.dsems[q].append([key, 0])
            self.dcur[q] = 0
        self.n_ops = 0

    _uid = 0

    def sbuf(self, name, shape, dtype):
        Prog._uid += 1
        name = "%s_u%d" % (name, Prog._uid)
        t = self.stack.enter_context(self.nc.sbuf_tensor(name, list(shape), dtype))
        return Buf(t, name)

    def psum(self, name, shape, dtype):
        t = self.stack.enter_context(self.nc.psum_tensor(name, list(shape), dtype))
        return Buf(t, name)

    def view(self, buf, name=None):
        return Buf(buf.t, name or buf.name)

    def _needs(self, reads, writes):
        need = {}

        def add(tok):
            if tok is None:
                return
            k, v = tok
            if need.get(k, 0) < v:
                need[k] = v
        for b in reads:
            add(b.last_write)
        for b in writes:
            add(b.last_write)
            for k, v in b.reads.items():
                add((k, v))
        return need

    def _waits(self, eng, need):
        waits = []
        for k, v in need.items():
            if k == eng and not (self.SAME_ENGINE_SYNC and eng != 'pe'):
                continue
            if self.seen[eng].get(k, 0) >= v:
                continue
            self.seen[eng][k] = v
            waits.append((k, v))
        return waits

    limit = None
    trace_range = None

    def op(self, eng, fn, reads=(), writes=()):
        if self.limit is not None and self.n_ops >= self.limit:
            return
        if self.trace_range and self.trace_range[0] <= self.n_ops < self.trace_range[1]:
            import inspect
            fr = inspect.stack()
            print("OP", self.n_ops, eng, [f.lineno for f in fr[1:4]])
        need = self._needs(reads, writes)
        waits = self._waits(eng, need)
        self.cnt[eng] += 1
        c = self.cnt[eng]
        self.ops[eng].append((waits, fn, (eng, 1)))
        tok = (eng, c)
        for b in reads:
            b.reads[eng] = c
        for b in writes:
            b.last_write = tok
            b.reads = {}
        self.n_ops += 1

    def dma(self, q, out_ap, in_ap, reads=(), writes=(), **kw):
        if self.limit is not None and self.n_ops >= self.limit and not kw.pop("force", False):
            return
        kw.pop("force", None)
        need = self._needs(reads, writes)
        slot = self.dsems[q][self.dcur[q]]
        self.dcur[q] = (self.dcur[q] + 1) % len(self.dsems[q])
        key, val = slot
        if val > 0:
            if need.get(key, 0) < val:
                need[key] = val
        waits = self._waits(q, need)
        slot[1] = val + 16
        tok = (key, val + 16)
        self.ops[q].append((waits, lambda e: e.dma_start(out=out_ap, in_=in_ap, **kw), (key, 16)))
        for b in reads:
            b.reads[key] = val + 16
        for b in writes:
            b.last_write = tok
            b.reads = {}
        self.n_ops += 1

    def barrier(self):
        need = {e: c for e, c in self.cnt.items() if c > 0}
        for q in self.dsems:
            for key, val in self.dsems[q]:
                if val > 0:
                    need[key] = val
        for eng in ENGS:
            waits = self._waits(eng, dict(need))
            if waits:
                self.ops[eng].append((waits, None, None))

    def finish(self):
        self.barrier()
        nc = self.nc
        sem = self.sem
        ops = self.ops

        def replay(eng_name):
            def run(e):
                for waits, fn, inc in ops[eng_name]:
                    for k, v in waits:
                        e.wait_ge(sem[k], v)
                    if fn is not None:
                        ins = fn(e)
                        ins.then_inc(sem[inc[0]], inc[1])
            return run

        with nc.Block() as block:
            block.tensor(replay('pe'))
            block.vector(replay('dve'))
            block.scalar(replay('act'))
            block.gpsimd(replay('pool'))
            block.sync(replay('sp'))
        self.stack.close()


D = 1024
NCTX = 256
LAT = 2048
S = NCTX + LAT
NT = S // 128
NB = 4
DEPTH = 2
HID = 2816
NHC = HID // 128
IN_COLS = 3488
GRID_W = 64
ALPHA = (2.0 * DEPTH) ** 0.25
LN_EPS = 1e-5
RW_GN_EPS = 64e-5
DECAY = float(np.exp(-0.5))
CB = 256
NBLK = S // CB
GQ, GK, GV, GG, GDN = 0, 192, 384, 768, 1152
NQ, NK, NV = 1184, 1440, 1696
RR, RK, RV, RDD, RAD, RGD = 1952, 2336, 2720, 3104, 3232, 3360
NEG = -30000.0

PC_W0 = 0
PC_A0 = 6
PC_KK = 12
PC_KA = 15
PC_RK = 18
PC_GB = 21
PC_BMOD = 27
NPC = 75
PR_LN1W, PR_LN1B, PR_LN2W, PR_LN2B = 0, 1024, 2048, 3072
PR_GNORM = 4096
PR_GNW = 4480
PR_GNB = 4864
PR_MU = 5248
NPR = 6784
C_IDENT = 0
C_MASKF = 128
C_MASKB = 384
C_LMF = 640
C_LMB = 768
C_SCF = 896
C_SCB = 1152
C_BONES = 1408
C_BD = 1536
C_HSEL = 1664
C_ONE = 1666
C_E12 = 1667
NCST = 1668


def _host_constants():
    c = np.zeros((128, NCST), np.float32)
    i = np.arange(128)
    c[:, C_IDENT:C_IDENT + 128] = np.eye(128, dtype=np.float32)
    j, t = i[:, None], i[None, :]
    c[:, C_MASKF:C_MASKF + 128] = (t > j)
    c[:, C_MASKF + 128:C_MASKF + 256] = (t >= j)
    c[:, C_MASKB:C_MASKB + 128] = (t < j)
    c[:, C_MASKB + 128:C_MASKB + 256] = (t <= j)
    c[:, C_LMF:C_LMF + 128] = (i[None, :] < i[:, None])
    c[:, C_LMB:C_LMB + 128] = (i[None, :] > i[:, None])
    sc = np.ones((256,), np.float32); sc[0] = 0; sc[128] = 0
    c[:, C_SCF:C_SCF + 256] = sc[None]
    sc = np.ones((256,), np.float32); sc[127] = 0; sc[255] = 0
    c[:, C_SCB:C_SCB + 256] = sc[None]
    blk = (i[:, None] // 64 == i[None, :] // 64).astype(np.float32)
    c[:, C_BONES:C_BONES + 128] = blk
    c[:, C_BD:C_BD + 128] = blk
    c[:, C_HSEL] = (i < 64)
    c[:, C_HSEL + 1] = (i >= 64)
    c[:, C_ONE] = 1.0
    c[:, C_E12] = 1e-12
    return c


def _rope_tables():
    tt = np.arange(LAT)
    row = (tt // GRID_W).astype(np.float32)
    col = (tt % GRID_W).astype(np.float32)
    n_freq = 8
    inv = (10000.0 ** (-np.arange(n_freq, dtype=np.float32) / n_freq)).astype(np.float32)
    ang = np.concatenate([row[:, None] * inv, col[:, None] * inv], axis=-1).astype(np.float32)
    cos, sin = np.cos(ang).astype(np.float32), np.sin(ang).astype(np.float32)
    C = np.zeros((128, LAT), np.float32)
    Sn = np.zeros((128, LAT), np.float32)
    for e in range(2):
        for kp in range(32):
            C[e * 64 + kp] = cos[:, kp // 2]
            Sn[e * 64 + kp] = sin[:, kp // 2]
    return C, Sn


def _nat_tables(rpb):
    cq = np.arange(GRID_W)
    col0 = np.clip(cq - 8, 0, GRID_W - 16)
    in_win = (cq[None, :] >= col0[:, None]) & (cq[None, :] < col0[:, None] + 16)
    dc = np.clip(cq[None, :] - cq[:, None], -15, 15) + 15
    out = np.full((DEPTH, 4, 128, 14, 64), NEG, np.float32)
    for e in range(2):
        for idx in range(14):
            g = rpb[:, :, idx + e, :][:, :, dc]
            g = np.where(in_win[None, None], g, np.float32(NEG))
            out[:, :, e * 64:(e + 1) * 64, idx, :] = np.transpose(g, (0, 1, 3, 2))
    return out.reshape(DEPTH, 4, 128, 14 * 64)


def _vec3(v):
    return np.ascontiguousarray(v.reshape(3, 128).T)


def _pad_gla(v):
    o = np.zeros((3, 2, 64), np.float32)
    o[:, :, :32] = v.reshape(3, 2, 32)
    return np.ascontiguousarray(o.reshape(3, 128).T)


class Kern:
    def __init__(self, nb=NB, depth=DEPTH, test=None):
        self.nb = nb
        self.depth = depth
        self.test = test
        nc = self.nc = bass.Bass("TRN2", target_bir_lowering=False)
        P = self.P = Prog(nc)

        def din(name, shape, dt=F32):
            return nc.dram_tensor(name, list(shape), dt, kind="ExternalInput").ap()
        self.x = din("x", [nb, LAT, D])
        self.ctx = din("ctx", [nb, NCTX, D])
        self.cc = din("cc", [NB + 1, D])
        self.w_mod = din("w_mod", [DEPTH, D, 6 * D])
        self.w_in = din("w_in", [DEPTH, D, IN_COLS])
        self.w_out = din("w_out", [DEPTH, D, D])
        self.w13 = din("ffn_w13", [DEPTH, D, 2 * HID])
        self.w2 = din("ffn_w2", [DEPTH, HID, D])
        self.pcol = din("pcol", [DEPTH, 128, NPC])
        self.prow = din("prow", [DEPTH, 1, NPR])
        self.cst = din("cst", [128, NCST])
        self.ropeC = din("ropeC", [128, LAT])
        self.ropeS = din("ropeS", [128, LAT])
        self.nattab = din("nattab", [DEPTH, 4, 128, 14 * 64])
        self.gup = din("gup", [DEPTH, 2, 16, 384])
        self.wd2 = din("wd2", [DEPTH, 128, 384])
        self.wa2 = din("wa2", [DEPTH, 128, 384])
        self.wg2 = din("wg2", [DEPTH, 128, 384])
        self.y = nc.dram_tensor("y", [nb, LAT, D], F32, kind="ExternalOutput").ap()
        self.XA = Buf(nc.dram_tensor("XA", [S, D], F32), "XA")
        self.XB = Buf(nc.dram_tensor("XB", [S, D], F32), "XB")
        if test == 'dense':
            self.YT = Buf(nc.dram_tensor("YT", [D, S], BF16, kind="ExternalInput"), "YT")
        else:
            self.YT = Buf(nc.dram_tensor("YT", [D, S], BF16), "YT")
        self.OF = Buf(nc.dram_tensor("OFs", [NT, 128, 384], F32), "OF")
        self.dbg = {}
        self.cst_sb = P.sbuf("cst_sb", [128, NCST], F32)
        P.dma('sp', self.cst_sb.t[:], self.cst, writes=[self.cst_sb])
        self.identb = P.sbuf("identb", [128, 128], BF16)
        P.op('dve', lambda e: e.tensor_copy(self.identb.t[:], self.cst_sb.t[:, C_IDENT:C_IDENT + 128]),
             reads=[self.cst_sb], writes=[self.identb])
        self.mTs = [P.sbuf("mT%d" % l, [128, 48, 8], F32) for l in range(DEPTH)]
        self.pcols = [P.sbuf("pcol_sb%d" % l, [128, NPC + 12], F32) for l in range(DEPTH)]
        self.set_layer(0)
        self.PS = [P.psum("psb%d" % i, [128, 512], F32) for i in range(8)]

    def set_layer(self, l):
        self.mT = self.mTs[l]
        self.pcol_sb = self.pcols[l]

    def ident(self):
        return self.cst_sb.t[:, C_IDENT:C_IDENT + 128]

    def dbg_out(self, name, shape, dt=F32):
        ap = self.nc.dram_tensor(name, list(shape), dt, kind="ExternalOutput").ap()
        self.dbg[name] = ap
        return ap

    def ph_mod(self, l):
        P, nc = self.P, self.nc
        P.barrier()
        pcol_sb, mT = self.pcol_sb, self.mT
        with contextlib.ExitStack() as st:
            P.stack, old = st, P.stack
            ccT = P.sbuf("ccT", [128, 8, 8], F32)
            sc = P.sbuf("scT", [128, 8, 8], F32)
            P.dma('sp', pcol_sb.t[:, 0:NPC], self.pcol[l], writes=[pcol_sb])
            P.op('dve', lambda e: e.memset(ccT.t[:], 0.0), writes=[ccT])
            for j in range(NB + 1):
                P.dma('sp', ccT.t[:, :, j:j + 1], self.cc[j].rearrange("(k p o) -> p k o", p=128, o=1), writes=[ccT],
                      allow_slow_non_contiguous=True)
            P.op('act', lambda e: e.activation(sc.t[:], ccT.t[:], AF.Silu), reads=[ccT], writes=[sc])
            P.op('dve', lambda e: e.tensor_scalar(pcol_sb.t[:, NPC:NPC + 3], pcol_sb.t[:, PC_KA:PC_KA + 3], -1.0, 1.0,
                                                  ALU.mult, ALU.add), reads=[pcol_sb], writes=[pcol_sb])
            P.op('dve', lambda e: e.tensor_scalar(pcol_sb.t[:, NPC + 3:NPC + 9], pcol_sb.t[:, PC_GB:PC_GB + 6], -1.0, None,
                                                  ALU.mult), reads=[pcol_sb], writes=[pcol_sb])
            wm = [P.sbuf("wm%d" % i, [128, 8, 512], F32) for i in range(2)]
            ps = self.PS[0]
            for cg in range(12):
                w = wm[cg % 2]
                P.dma('sp', w.t[:], self.w_mod[l][:, cg * 512:(cg + 1) * 512].rearrange("(k p) c -> p k c", p=128), writes=[w])
                first = True
                for c in range(4):
                    for k in range(8):
                        P.op('pe', lambda e, w=w, c=c, k=k, first=first: e.matmul(
                            ps.t[:, c * 8:c * 8 + 8], w.t[:, k, c * 128:(c + 1) * 128], sc.t[:, k, :], start=first, stop=(k == 7),
                            skip_group_check=True), reads=[w, sc], writes=[ps])
                        first = False
                for c in range(4):
                    ch = cg * 4 + c
                    isscale = (8 <= ch < 16) or (32 <= ch < 40)
                    P.op('dve', lambda e, c=c, ch=ch, isscale=isscale: e.tensor_scalar(
                        mT.t[:, ch, :], ps.t[:, c * 8:c * 8 + 8], pcol_sb.t[:, PC_BMOD + ch:PC_BMOD + ch + 1],
                        1.0 if isscale else 0.0, ALU.add, ALU.add), reads=[ps, pcol_sb], writes=[mT])
            P.barrier()
            P.stack = old

    def build_xT(self, src_rows, xT, ntok, shift_ch, scale_ch, jb, segs, xt_bufs):
        P = self.P
        gi = 0
        for (tok0, n, jcol) in segs:
            nt = n // 128
            xt = xt_bufs[gi % 2]
            gi += 1
            P.dma('sp', xt.t[:, 0:nt, :], src_rows(tok0, n).rearrange("(j p) d -> p j d", p=128), reads=self._src_reads, writes=[xt])
            for dc in range(8):
                ps = self.PS[dc % 2]
                for j in range(nt):
                    P.op('pe', lambda e, ps=ps, xt=xt, j=j, dc=dc: e.transpose(
                        ps.t[:, j * 128:(j + 1) * 128], xt.t[:, j, dc * 128:(dc + 1) * 128], self.ident()),
                        reads=[xt, self.cst_sb], writes=[ps])
                eng = 'dve' if dc % 2 == 0 else 'act'
                o = xT.t[:, dc, tok0:tok0 + n]
                sc_ap = self.mT.t[:, scale_ch + dc, jcol:jcol + 1]
                sh_ap = self.mT.t[:, shift_ch + dc, jcol:jcol + 1]
                if eng == 'dve':
                    P.op('dve', lambda e, o=o, ps=ps, n=n, sc_ap=sc_ap, sh_ap=sh_ap: e.tensor_scalar(
                        o, ps.t[:, 0:n], sc_ap, sh_ap, ALU.mult, ALU.add), reads=[ps, self.mT], writes=[xT])
                else:
                    P.op('act', lambda e, o=o, ps=ps, n=n, sc_ap=sc_ap, sh_ap=sh_ap: e.activation(
                        o, ps.t[:, 0:n], AF.Identity, bias=sh_ap, scale=sc_ap), reads=[ps, self.mT], writes=[xT])

    def seq_segs(self, b, with_ctx=True):
        segs = []
        if with_ctx:
            segs.append((0, NCTX, NB))
        for g in range(4):
            segs.append((NCTX + g * 512, 512, b))
        return segs

    def src_rows_fn(self, b, l):
        if l == 0:
            def f(tok0, n):
                if tok0 < NCTX:
                    return self.ctx[b][tok0:tok0 + n, :]
                return self.x[b][tok0 - NCTX:tok0 - NCTX + n, :]
            return f, []
        XB = self.XB

        def f2(tok0, n):
            return XB.t.ap()[tok0:tok0 + n, :]
        return f2, [XB]

    def ln_tail(self, pre, rows_w, rows_b, out_tile, tmp):
        P = self.P
        st = self.ln_st
        mv = self.ln_mv
        P.op('dve', lambda e: e.tensor_reduce(mv.t[:, 5:6], pre.t[:], AX.X, ALU.add), reads=[pre], writes=[mv])
        P.op('act', lambda e: e.activation(tmp.t[:], pre.t[:], AF.Square, accum_out=st.t[:, 0:1]), reads=[pre], writes=[tmp, st])
        P.op('dve', lambda e: e.tensor_scalar(mv.t[:, 0:1], mv.t[:, 5:6], 1.0 / D, None, ALU.mult), reads=[mv], writes=[mv])
        P.op('dve', lambda e: e.tensor_tensor(mv.t[:, 1:2], mv.t[:, 0:1], mv.t[:, 0:1], op=ALU.mult), reads=[mv], writes=[mv])
        P.op('dve', lambda e: e.scalar_tensor_tensor(mv.t[:, 2:3], st.t[:, 0:1], 1.0 / D, mv.t[:, 1:2], ALU.mult, ALU.subtract),
             reads=[mv, st], writes=[mv])
        P.op('dve', lambda e: e.tensor_scalar(mv.t[:, 2:3], mv.t[:, 2:3], LN_EPS, None, ALU.add), reads=[mv], writes=[mv])
        P.op('act', lambda e: e.activation(mv.t[:, 3:4], mv.t[:, 2:3], AF.Sqrt), reads=[mv], writes=[mv])
        P.op('dve', lambda e: e.reciprocal(mv.t[:, 4:5], mv.t[:, 3:4]), reads=[mv], writes=[mv])
        P.op('dve', lambda e: e.tensor_scalar(tmp.t[:], pre.t[:], mv.t[:, 0:1], mv.t[:, 4:5], ALU.subtract, ALU.mult),
             reads=[pre, mv], writes=[tmp])
        P.op('pool', lambda e: e.tensor_tensor(tmp.t[:], tmp.t[:], rows_w, op=ALU.mult), reads=[tmp, self.prow_sb], writes=[tmp])
        P.op('pool', lambda e: e.tensor_tensor(out_tile.t[:], tmp.t[:], rows_b, op=ALU.add), reads=[tmp, self.prow_sb], writes=[out_tile])

    def gate_bcast(self, gate_ch, jcol, gb):
        P = self.P
        dg = self.diag_tmp
        mT = self.mT
        ones_f = self.ones_f
        for c in range(8):
            ps = self.PS[2 + (c // 4)]
            P.op('dve', lambda e, c=c: e.tensor_scalar(dg.t[:], self.ident(), mT.t[:, gate_ch + c, jcol:jcol + 1], None, ALU.mult),
                 reads=[mT, self.cst_sb], writes=[dg])
            P.op('pe', lambda e, c=c, ps=ps: e.matmul(ps.t[:, (c % 4) * 128:(c % 4 + 1) * 128], ones_f.t[:], dg.t[:],
                                                      start=True, stop=True), reads=[dg, ones_f], writes=[ps])
            if c % 4 == 3:
                h = c // 4
                P.op('act', lambda e, ps=ps, h=h: e.activation(gb.t[:, h * 512:(h + 1) * 512], ps.t[:], AF.Identity),
                     reads=[ps], writes=[gb])

    def ph_wout_ln1(self, b, l, ntiles_from=0):
        P, nc = self.P, self.nc
        P.barrier()
        last = (l == self.depth - 1)
        src, src_reads = self.src_rows_fn(b, l)
        with contextlib.ExitStack() as st:
            P.stack, old = st, P.stack
            wo = P.sbuf("wo", [128, 8, D], BF16)
            for k in range(8):
                P.dma('pool', wo.t[:, k, :], self.w_out[l][k * 128:(k + 1) * 128, :], writes=[wo])
            self.prow_sb = P.sbuf("prow_sb", [128, 2048], F32)
            P.dma('sp', self.prow_sb.t[:], self.prow[l][:, PR_LN1W:PR_LN1W + 2048].partition_broadcast(128), writes=[self.prow_sb])
            self.ln_st = P.sbuf("ln_st", [128, 12], F32)
            self.ln_mv = P.sbuf("ln_mv", [128, 8], F32)
            self.diag_tmp = P.sbuf("diag_tmp", [128, 128], F32)
            self.ones_f = ones_f_ = P.sbuf("ones_f", [128, 128], F32)
            P.op('pool', lambda e: e.memset(ones_f_.t[:], 1.0), writes=[ones_f_])
            gbl = P.sbuf("gbl", [128, D], F32)
            gbc = P.sbuf("gbc", [128, D], F32)
            self.gate_bcast(16, b, gbl)
            self.gate_bcast(16, NB, gbc)
            yt = [P.sbuf("yt%d" % i, [128, 8, 128], BF16) for i in range(2)]
            xr = [P.sbuf("xr%d" % i, [128, D], F32) for i in range(2)]
            pre = [P.sbuf("pre%d" % i, [128, D], F32) for i in range(2)]
            tmp = P.sbuf("lntmp", [128, D], F32)
            ot = [P.sbuf("ot%d" % i, [128, D], F32) for i in range(2)]
            t0 = 2 if last else 0
            for ti in range(t0, NT):
                i2 = ti % 2
                gb = gbc if ti < 2 else gbl
                P.dma('sp', yt[i2].t[:], self.YT.t.ap()[:, ti * 128:(ti + 1) * 128].rearrange("(k p) t -> p k t", p=128),
                      reads=[self.YT], writes=[yt[i2]])
                P.dma('sp', xr[i2].t[:], src(ti * 128, 128), reads=src_reads, writes=[xr[i2]])
                for h in range(2):
                    ps = self.PS[4 + h]
                    for k in range(8):
                        P.op('pe', lambda e, ps=ps, k=k, h=h, i2=i2: e.matmul(ps.t[:], yt[i2].t[:, k, :], wo.t[:, k, h * 512:(h + 1) * 512],
                                                                           start=(k == 0), stop=(k == 7)), reads=[yt[i2], wo], writes=[ps])
                    P.op('dve', lambda e, ps=ps, h=h, i2=i2, gb=gb: e.tensor_tensor(pre[i2].t[:, h * 512:(h + 1) * 512], ps.t[:],
                                                                             gb.t[:, h * 512:(h + 1) * 512], op=ALU.mult),
                         reads=[ps, gb], writes=[pre[i2]])
                P.op('dve', lambda e, i2=i2: e.scalar_tensor_tensor(pre[i2].t[:], xr[i2].t[:], ALPHA, pre[i2].t[:], ALU.mult, ALU.add),
                     reads=[xr[i2], pre[i2]], writes=[pre[i2]])
                self.ln_tail(pre[i2], self.prow_sb.t[:, 0:1024], self.prow_sb.t[:, 1024:2048], ot[i2], tmp)
                P.dma('pool', self.XA.t.ap()[ti * 128:(ti + 1) * 128, :], ot[i2].t[:], reads=[ot[i2]], writes=[self.XA])
            P.barrier()
            P.stack = old

    def ph_ffn_ln2(self, b, l):
        P, nc = self.P, self.nc
        P.barrier()
        last = (l == self.depth - 1)
        tok_lo = NCTX if last else 0
        XA = self.XA
        with contextlib.ExitStack() as st:
            P.stack, old = st, P.stack
            hT = P.sbuf("hT", [128, NHC, S], BF16)
            with contextlib.ExitStack() as st2:
                P.stack = st2
                xT = P.sbuf("x2T", [128, 8, S], BF16)
                xtb = [P.sbuf("xtb%d" % i, [128, 4, D], F32) for i in range(2)]
                self._src_reads = [XA]
                segs = self.seq_segs(b, with_ctx=not last)
                self.build_xT(lambda tok0, n: XA.t.ap()[tok0:tok0 + n, :], xT, S, 24, 32, b, segs, xtb)
                wg = [P.sbuf("wg%d" % i, [128, 8, 128], BF16) for i in range(2)]
                wu = [P.sbuf("wu%d" % i, [128, 8, 128], BF16) for i in range(2)]
                sg = [P.sbuf("sg%d" % i, [128, 512], F32) for i in range(2)]
                blocks = [(t, min(512, S - t)) for t in range(tok_lo, S, 512)]
                for hc in range(NHC):
                    i2 = hc % 2
                    P.dma('pool', wg[i2].t[:], self.w13[l][:, hc * 128:(hc + 1) * 128].rearrange("(k p) c -> p k c", p=128), writes=[wg[i2]])
                    P.dma('pool', wu[i2].t[:], self.w13[l][:, HID + hc * 128:HID + (hc + 1) * 128].rearrange("(k p) c -> p k c", p=128),
                          writes=[wu[i2]])
                    for bi, (t0, n) in enumerate(blocks):
                        pg = self.PS[(bi % 2) * 2]
                        pu = self.PS[(bi % 2) * 2 + 1]
                        for k in range(8):
                            P.op('pe', lambda e, pg=pg, k=k, i2=i2, t0=t0, n=n: e.matmul(pg.t[:, 0:n], wg[i2].t[:, k, :], xT.t[:, k, t0:t0 + n],
                                                                                     start=(k == 0), stop=(k == 7)), reads=[wg[i2], xT], writes=[pg])
                        for k in range(8):
                            P.op('pe', lambda e, pu=pu, k=k, i2=i2, t0=t0, n=n: e.matmul(pu.t[:, 0:n], wu[i2].t[:, k, :], xT.t[:, k, t0:t0 + n],
                                                                                     start=(k == 0), stop=(k == 7)), reads=[wu[i2], xT], writes=[pu])
                        s = sg[bi % 2]
                        P.op('act', lambda e, s=s, pg=pg, n=n: e.activation(s.t[:, 0:n], pg.t[:, 0:n], AF.Silu), reads=[pg], writes=[s])
                        P.op('dve', lambda e, s=s, pu=pu, n=n, hc=hc, t0=t0: e.tensor_tensor(hT.t[:, hc, t0:t0 + n], s.t[:, 0:n], pu.t[:, 0:n],
                                                                                       op=ALU.mult), reads=[s, pu], writes=[hT])
                P.barrier()
            P.stack = st
            w2 = P.sbuf("w2", [128, NHC, D], BF16)
            for hc in range(NHC):
                P.dma('pool', w2.t[:, hc, :], self.w2[l][hc * 128:(hc + 1) * 128, :], writes=[w2])
            self.prow_sb = P.sbuf("prow_sb2", [128, 2048], F32)
            P.dma('sp', self.prow_sb.t[:], self.prow[l][:, PR_LN2W:PR_LN2W + 2048].partition_broadcast(128), writes=[self.prow_sb])
            self.ln_st = P.sbuf("ln_st2", [128, 12], F32)
            self.ln_mv = P.sbuf("ln_mv2", [128, 8], F32)
            self.diag_tmp = P.sbuf("diag_tmp2", [128, 128], F32)
            self.ones_f = ones_f_ = P.sbuf("ones_f2", [128, 128], F32)
            P.op('pool', lambda e: e.memset(ones_f_.t[:], 1.0), writes=[ones_f_])
            gbl = P.sbuf("gbl2", [128, D], F32)
            gbc = P.sbuf("gbc2", [128, D], F32)
            self.gate_bcast(40, b, gbl)
            self.gate_bcast(40, NB, gbc)
            xr = [P.sbuf("xr2%d" % i, [128, D], F32) for i in range(2)]
            pre = [P.sbuf("pre2%d" % i, [128, D], F32) for i in range(2)]
            tmp = P.sbuf("lntmp2", [128, D], F32)
            ot = [P.sbuf("ot2%d" % i, [128, D], F32) for i in range(2)]
            for ti in range(2 if last else 0, NT):
                i2 = ti % 2
                gb = gbc if ti < 2 else gbl
                P.dma('sp', xr[i2].t[:], XA.t.ap()[ti * 128:(ti + 1) * 128, :], reads=[XA], writes=[xr[i2]])
                for h in range(2):
                    ps = self.PS[4 + h]
                    for hc in range(NHC):
                        P.op('pe', lambda e, ps=ps, hc=hc, h=h, ti=ti: e.matmul(ps.t[:], hT.t[:, hc, ti * 128:(ti + 1) * 128],
                                                                             w2.t[:, hc, h * 512:(h + 1) * 512], start=(hc == 0), stop=(hc == NHC - 1)),
                             reads=[hT, w2], writes=[ps])
                    P.op('dve', lambda e, ps=ps, h=h, i2=i2, gb=gb: e.tensor_tensor(pre[i2].t[:, h * 512:(h + 1) * 512], ps.t[:],
                                                                             gb.t[:, h * 512:(h + 1) * 512], op=ALU.mult),
                         reads=[ps, gb], writes=[pre[i2]])
                P.op('dve', lambda e, i2=i2: e.scalar_tensor_tensor(pre[i2].t[:], xr[i2].t[:], ALPHA, pre[i2].t[:], ALU.mult, ALU.add),
                     reads=[xr[i2], pre[i2]], writes=[pre[i2]])
                self.ln_tail(pre[i2], self.prow_sb.t[:, 0:1024], self.prow_sb.t[:, 1024:2048], ot[i2], tmp)
                if last:
                    P.dma('pool', self.y[b][(ti - 2) * 128:(ti - 1) * 128, :], ot[i2].t[:], reads=[ot[i2]])
                else:
                    P.dma('pool', self.XB.t.ap()[ti * 128:(ti + 1) * 128, :], ot[i2].t[:], reads=[ot[i2]], writes=[self.XB])
            P.barrier()
            P.stack = old

    def dump(self, buf, name, shape, dt=F32):
        ap = self.dbg_out(name, shape, dt)
        self.P.dma('sp', ap, buf.t.ap(), reads=[buf])


def host_inputs(inputs, core, nb=NB):
    f = lambda a: np.ascontiguousarray(np.asarray(a, dtype=np.float32))
    b0 = core * nb
    m = {}
    m["x"] = f(inputs["x"][b0:b0 + nb])
    m["ctx"] = f(inputs["ctx"][b0:b0 + nb])
    cc = np.zeros((NB + 1, D), np.float32)
    cc[:nb] = inputs["c"][b0:b0 + nb]
    cc[NB] = inputs["c_ctx"]
    m["cc"] = cc
    for k in ("w_mod", "w_in", "w_out", "ffn_w13", "ffn_w2"):
        m[k] = f(inputs[k])
    pcol = np.zeros((DEPTH, 128, NPC), np.float32)
    prow = np.zeros((DEPTH, 1, NPR), np.float32)
    gup = np.zeros((DEPTH, 2, 16, 3, 2, 64), np.float32)
    for l in range(DEPTH):
        for d in range(2):
            pcol[l, :, PC_W0 + d * 3:PC_W0 + d * 3 + 3] = _vec3(inputs["rw_w0"][l, d])
            pcol[l, :, PC_A0 + d * 3:PC_A0 + d * 3 + 3] = _vec3(inputs["rw_a0"][l, d])
            pcol[l, :, PC_GB + d * 3:PC_GB + d * 3 + 3] = _pad_gla(inputs["gla_gate_b"][l, d])
            gup[l, d, :, :, :, :32] = inputs["gla_gate_up"][l, d].reshape(16, 3, 2, 32)
        pcol[l, :, PC_KK:PC_KK + 3] = _vec3(inputs["rw_k_k"][l])
        pcol[l, :, PC_KA:PC_KA + 3] = _vec3(inputs["rw_k_a"][l])
        pcol[l, :, PC_RK:PC_RK + 3] = _vec3(inputs["rw_r_k"][l])
        pcol[l, :, PC_BMOD:PC_BMOD + 48] = inputs["b_mod"][l].reshape(48, 128).T
        prow[l, 0, PR_LN1W:PR_LN1W + 1024] = inputs["ln1_w"][l]
        prow[l, 0, PR_LN1B:PR_LN1B + 1024] = inputs["ln1_b"][l]
        prow[l, 0, PR_LN2W:PR_LN2W + 1024] = inputs["ln2_w"][l]
        prow[l, 0, PR_LN2B:PR_LN2B + 1024] = inputs["ln2_b"][l]
        prow[l, 0, PR_GNORM:PR_GNORM + 384] = np.tile(inputs["gla_norm_w"][l], 6)
        prow[l, 0, PR_GNW:PR_GNW + 384] = inputs["rw_gn_w"][l]
        prow[l, 0, PR_GNB:PR_GNB + 384] = inputs["rw_gn_b"][l]
        prow[l, 0, PR_MU:PR_MU + 1536] = inputs["rw_mu"][l]
    m["pcol"] = pcol
    m["prow"] = prow
    m["gup"] = np.ascontiguousarray(gup.reshape(DEPTH, 2, 16, 384))
    m["cst"] = _host_constants()
    C, Sn = _rope_tables()
    m["ropeC"], m["ropeS"] = C, Sn
    m["nattab"] = _nat_tables(np.asarray(inputs["nat_rpb"], np.float32))
    m["wd2"] = f(inputs["rw_wd2"]).reshape(DEPTH, 128, 384)
    m["wa2"] = f(inputs["rw_wa2"]).reshape(DEPTH, 128, 384)
    m["wg2"] = f(inputs["rw_wg2"])
    return m


def _ph_x(self, b, l, need_dx=True):
    P = self.P
    xT = P.sbuf("xmodT", [128, 8, S], BF16)
    dxT = P.sbuf("dxT", [128, 8, S], BF16) if need_dx else None
    outer = P.stack
    with contextlib.ExitStack() as st:
        P.stack = st
        xtb = [P.sbuf("xtb%d" % i, [128, 4, D], F32) for i in range(2)]
        src, src_reads = self.src_rows_fn(b, l)
        self._src_reads = src_reads
        self.build_xT(src, xT, S, 0, 8, b, self.seq_segs(b), xtb)
        if need_dx:
            tmp = P.sbuf("dxtmp", [128, 2, LAT], F32)
            for (t0, n) in ((0, NCTX), (NCTX, LAT)):
                for c in range(0, 8, 2):
                    P.op('dve', lambda e, t0=t0, n=n, c=c: e.tensor_tensor(tmp.t[:, :, 0:n - 2], xT.t[:, c:c + 2, t0:t0 + n - 2],
                                                                        xT.t[:, c:c + 2, t0 + 2:t0 + n], op=ALU.add), reads=[xT], writes=[tmp])
                    P.op('dve', lambda e, t0=t0, n=n, c=c: e.scalar_tensor_tensor(dxT.t[:, c:c + 2, t0 + 1:t0 + n - 1], tmp.t[:, :, 0:n - 2], 0.5,
                                                                               xT.t[:, c:c + 2, t0 + 1:t0 + n - 1], ALU.mult, ALU.subtract),
                         reads=[tmp, xT], writes=[dxT])
                P.op('dve', lambda e, t0=t0: e.scalar_tensor_tensor(dxT.t[:, :, t0:t0 + 1], xT.t[:, :, t0 + 1:t0 + 2], 0.5, xT.t[:, :, t0:t0 + 1],
                                                                 ALU.mult, ALU.subtract), reads=[xT], writes=[dxT])
                P.op('dve', lambda e, t0=t0, n=n: e.scalar_tensor_tensor(dxT.t[:, :, t0 + n - 1:t0 + n], xT.t[:, :, t0 + n - 2:t0 + n - 1], 0.5,
                                                                      xT.t[:, :, t0 + n - 1:t0 + n], ALU.mult, ALU.subtract), reads=[xT], writes=[dxT])
        P.barrier()
    P.stack = outer
    return xT, dxT


Kern.ph_x = _ph_x


def _ph_nat(self, b, l, xT):
    P = self.P
    last = (l == self.depth - 1)
    with contextlib.ExitStack() as st:
        P.stack, old = st, P.stack
        qT = P.sbuf("n_qT", [128, 2, S], BF16)
        kT = P.sbuf("n_kT", [128, 2, S], BF16)
        V = P.sbuf("n_V", [128, NT, 256], BF16)
        Vs = P.sbuf("n_Vs", [128, 15, 256], BF16)
        yTn = P.sbuf("n_yT", [128, 2, S], BF16)
        tab = P.sbuf("n_tab", [128, 4, 14 * 64], F32)
        onesb = P.sbuf("n_ones", [128, 128], BF16)
        P.op('pool', lambda e: e.memset(onesb.t[:], 1.0), writes=[onesb])
        for h in range(4):
            P.dma('sp', tab.t[:, h, :], self.nattab[l, h], writes=[tab])
        wq = P.sbuf("n_wq", [128, 8, 256], BF16)
        wk = P.sbuf("n_wk", [128, 8, 256], BF16)
        wv = P.sbuf("n_wv", [128, 8, 256], BF16)
        for (w, c0) in ((wq, NQ), (wk, NK), (wv, NV)):
            P.dma('pool', w.t[:], self.w_in[l][:, c0:c0 + 256].rearrange("(k p) c -> p k c", p=128), writes=[w])
        blocks = [(t, min(512, S - t)) for t in range(0, S, 512)]
        n = 0
        for (w, dst) in ((wq, qT), (wk, kT)):
            for tl in range(2):
                for (t0, nn) in blocks:
                    ps = self.PS[n % 2]
                    for k in range(8):
                        P.op('pe', lambda e, ps=ps, w=w, tl=tl, k=k, t0=t0, nn=nn: e.matmul(
                            ps.t[:, 0:nn], w.t[:, k, tl * 128:(tl + 1) * 128], xT.t[:, k, t0:t0 + nn], start=(k == 0), stop=(k == 7)),
                            reads=[w, xT], writes=[ps])
                    if n % 2 == 0:
                        P.op('dve', lambda e, ps=ps, dst=dst, tl=tl, t0=t0, nn=nn: e.tensor_copy(dst.t[:, tl, t0:t0 + nn], ps.t[:, 0:nn]),
                             reads=[ps], writes=[dst])
                    else:
                        P.op('act', lambda e, ps=ps, dst=dst, tl=tl, t0=t0, nn=nn: e.activation(dst.t[:, tl, t0:t0 + nn], ps.t[:, 0:nn], AF.Identity),
                             reads=[ps], writes=[dst])
                    n += 1
        for (dst, nt, base) in ((V, NT, 0), (Vs, 15, NCTX + 64)):
            for ti in range(nt):
                ps = self.PS[2 + ti % 2]
                t0 = base + ti * 128
                for k in range(8):
                    P.op('pe', lambda e, ps=ps, k=k, t0=t0: e.matmul(ps.t[:, 0:256], xT.t[:, k, t0:t0 + 128], wv.t[:, k, :], start=(k == 0), stop=(k == 7)),
                         reads=[wv, xT], writes=[ps])
                if ti % 2 == 0:
                    P.op('dve', lambda e, ps=ps, dst=dst, ti=ti: e.tensor_copy(dst.t[:, ti, :], ps.t[:, 0:256]), reads=[ps], writes=[dst])
                else:
                    P.op('act', lambda e, ps=ps, dst=dst, ti=ti: e.activation(dst.t[:, ti, :], ps.t[:, 0:256], AF.Identity), reads=[ps], writes=[dst])
        stt = [P.sbuf("n_stt%d" % i, [128, 4, 64], F32) for i in range(2)]
        E = [P.sbuf("n_E%d" % i, [128, 6, 64], BF16) for i in range(2)]
        rB = P.sbuf("n_rB", [128, 2, 64], F32)
        it = 0
        for r in range(32):
            r0 = min(max(r - 4, 0), 24)
            off = r0 - r + 7
            q0 = NCTX + r * 64
            for hp in range(2):
                pso = self.PS[4 + (it % 2)]
                for e2 in range(2):
                    h = hp * 2 + e2
                    pb = e2 * 64
                    pss = self.PS[(it % 2) * 2 + e2]
                    Eh = E[e2]
                    for j in range(6):
                        k0 = (NCTX + r0 * 64 + j * 128) if j < 4 else (j - 4) * 128
                        P.op('pe', lambda e, pss=pss, j=j, pb=pb, hp=hp, k0=k0, q0=q0: e.matmul(
                            pss.t[:, j * 64:(j + 1) * 64], kT.t[pb:pb + 64, hp, k0:k0 + 128], qT.t[pb:pb + 64, hp, q0:q0 + 64], start=True, stop=True),
                            reads=[kT, qT], writes=[pss])
                    P.op('dve', lambda e, pss=pss, e2=e2, h=h, off=off: e.scalar_tensor_tensor(
                        stt[e2].t[:], pss.t[:, 0:256].rearrange("p (a b) -> p a b", a=4), 0.125,
                        tab.t[:, h, :].rearrange("p (a b) -> p a b", a=14)[:, off:off + 7:2, :], ALU.mult, ALU.add),
                        reads=[pss, tab], writes=[stt[e2]])
                    P.op('act', lambda e, Eh=Eh, e2=e2: e.activation(Eh.t[:, 0:4, :], stt[e2].t[:], AF.Exp), reads=[stt[e2]], writes=[Eh])
                    P.op('act', lambda e, Eh=Eh, pss=pss: e.activation(Eh.t[:, 4:6, :], pss.t[:, 256:384].rearrange("p (a b) -> p a b", a=2), AF.Exp, scale=0.125),
                         reads=[pss], writes=[Eh])
                first = True
                for e2 in range(2):
                    Eh = E[e2]
                    for j in range(6):
                        if j < 4:
                            if r0 % 2 == 0:
                                vt = V.t[:, 2 + r0 // 2 + j, hp * 128:(hp + 1) * 128]
                            else:
                                vt = Vs.t[:, (r0 - 1) // 2 + j, hp * 128:(hp + 1) * 128]
                        else:
                            vt = V.t[:, j - 4, hp * 128:(hp + 1) * 128]
                        P.op('pe', lambda e, pso=pso, vt=vt, Eh=Eh, j=j, e2=e2, first=first: e.matmul(
                            pso.t[:, e2 * 64:(e2 + 1) * 64], vt, Eh.t[:, j, :], start=first, stop=(j == 5), skip_group_check=True),
                            reads=[V, Vs, Eh], writes=[pso])
                        first = False
                        P.op('pe', lambda e, pso=pso, Eh=Eh, j=j, e2=e2: e.matmul(
                            pso.t[:, 128 + e2 * 64:128 + (e2 + 1) * 64], onesb.t[:], Eh.t[:, j, :], start=False, stop=(j == 5), skip_group_check=True),
                            reads=[onesb, Eh], writes=[pso])
                P.op('dve', lambda e, pso=pso: e.reciprocal(rB.t[:], pso.t[:, 128:256].rearrange("p (a b) -> p a b", a=2)), reads=[pso], writes=[rB])
                for e2 in range(2):
                    pb = e2 * 64
                    P.op('dve', lambda e, pso=pso, e2=e2, pb=pb, hp=hp, q0=q0: e.tensor_tensor(
                        yTn.t[pb:pb + 64, hp, q0:q0 + 64], pso.t[pb:pb + 64, e2 * 64:(e2 + 1) * 64], rB.t[pb:pb + 64, e2, :], op=ALU.mult),
                        reads=[pso, rB], writes=[yTn])
                it += 1
        if not last:
            Ec = [P.sbuf("n_Ec%d" % i, [128, 2, 256], BF16) for i in range(2)]
            rBc = P.sbuf("n_rBc", [128, 256], F32)
            for hp in range(2):
                for e2 in range(2):
                    pb = e2 * 64
                    pss = self.PS[e2]
                    pso = self.PS[2 + e2]
                    for j in range(2):
                        P.op('pe', lambda e, pss=pss, j=j, pb=pb, hp=hp: e.matmul(
                            pss.t[:, j * 256:(j + 1) * 256], kT.t[pb:pb + 64, hp, j * 128:(j + 1) * 128], qT.t[pb:pb + 64, hp, 0:256], start=True, stop=True),
                            reads=[kT, qT], writes=[pss])
                    P.op('act', lambda e, pss=pss, e2=e2: e.activation(Ec[e2].t[:], pss.t[:].rearrange("p (a b) -> p a b", a=2), AF.Exp, scale=0.125),
                         reads=[pss], writes=[Ec[e2]])
                    for j in range(2):
                        P.op('pe', lambda e, pso=pso, j=j, hp=hp, e2=e2: e.matmul(
                            pso.t[:, 0:256], V.t[:, j, hp * 128:(hp + 1) * 128], Ec[e2].t[:, j, :], start=(j == 0), stop=(j == 1), skip_group_check=True),
                            reads=[V, Ec[e2]], writes=[pso])
                        P.op('pe', lambda e, pso=pso, j=j, e2=e2: e.matmul(
                            pso.t[:, 256:512], onesb.t[:], Ec[e2].t[:, j, :], start=False, stop=(j == 1), skip_group_check=True),
                            reads=[onesb, Ec[e2]], writes=[pso])
                    P.op('dve', lambda e, pso=pso: e.reciprocal(rBc.t[:], pso.t[:, 256:512]), reads=[pso], writes=[rBc])
                    P.op('dve', lambda e, pso=pso, pb=pb, hp=hp: e.tensor_tensor(
                        yTn.t[pb:pb + 64, hp, 0:256], pso.t[pb:pb + 64, 0:256], rBc.t[pb:pb + 64, :], op=ALU.mult),
                        reads=[pso, rBc], writes=[yTn])
        t_lo = NCTX if last else 0
        for tl in range(2):
            P.dma('sp', self.YT.t.ap()[384 + tl * 128:384 + (tl + 1) * 128, t_lo:S], yTn.t[:, tl, t_lo:S], reads=[yTn], writes=[self.YT])
        P.barrier()
        P.stack = old


Kern.ph_nat = _ph_nat


class _Ops:
    def __init__(self, P):
        self.P = P

    def TT(self, eng, out, a, b, op, reads, writes):
        self.P.op(eng, lambda e: e.tensor_tensor(out, a, b, op=op), reads, writes)

    def TS(self, eng, out, a, s1, s2, op0, op1, reads, writes):
        if op1 is None:
            self.P.op(eng, lambda e: e.tensor_scalar(out, a, s1, None, op0), reads, writes)
        else:
            self.P.op(eng, lambda e: e.tensor_scalar(out, a, s1, s2, op0, op1), reads, writes)

    def STT(self, out, a, s, b, op0, op1, reads, writes):
        self.P.op('dve', lambda e: e.scalar_tensor_tensor(out, a, s, b, op0, op1), reads, writes)

    def ACT(self, out, in_, func, reads, writes, **kw):
        self.P.op('act', lambda e: e.activation(out, in_, func, **kw), reads, writes)

    def CP(self, eng, out, in_, reads, writes):
        if eng == 'act':
            self.P.op('act', lambda e: e.activation(out, in_, AF.Identity), reads, writes)
        else:
            self.P.op(eng, lambda e: e.tensor_copy(out, in_), reads, writes)

    def MM(self, out, lhsT, rhs, start, stop, reads, writes):
        self.P.op('pe', lambda e: e.matmul(out, lhsT, rhs, start=start, stop=stop, skip_group_check=True), reads, writes)

    def TR(self, out, in_, ident, reads, writes):
        self.P.op('pe', lambda e: e.transpose(out, in_, ident), reads, writes)


def _ph_scan(self, b, l, xT, dxT, kind):
    P = self.P
    O = _Ops(P)
    rw = (kind == 'rw')
    NK = 16 if rw else 8
    PS = self.PS
    cst = self.cst_sb
    pc = self.pcol_sb
    with contextlib.ExitStack() as st:
        P.stack, old = st, P.stack
        Of = self.OF
        of_sbs = [P.sbuf("s_of%d" % i, [128, 384], F32) for i in range(2)]
        COEF = P.sbuf("s_coef", [128, NT, 8], F32) if rw else None
        mskF = P.sbuf("s_mskF", [128, 256], BF16)
        mskB = P.sbuf("s_mskB", [128, 256], BF16)
        lmF = P.sbuf("s_lmF", [128, 128], BF16)
        lmB = P.sbuf("s_lmB", [128, 128], BF16)
        hselb = P.sbuf("s_hsel", [128, 2], BF16)
        O.CP('dve', mskF.t[:], cst.t[:, C_MASKF:C_MASKF + 256], [cst], [mskF])
        O.CP('dve', mskB.t[:], cst.t[:, C_MASKB:C_MASKB + 256], [cst], [mskB])
        O.CP('dve', lmF.t[:], cst.t[:, C_LMF:C_LMF + 128], [cst], [lmF])
        O.CP('dve', lmB.t[:], cst.t[:, C_LMB:C_LMB + 128], [cst], [lmB])
        O.CP('dve', hselb.t[:], cst.t[:, C_HSEL:C_HSEL + 2], [cst], [hselb])
        prw = P.sbuf("s_prow", [128, 3 * 384], F32)
        P.dma('sp', prw.t[:], self.prow[l][:, PR_GNORM:PR_GNORM + 3 * 384].partition_broadcast(128), writes=[prw])
        if rw:
            ncolt = 9
            W = P.sbuf("s_W", [128, ncolt, 16, 128], BF16)
            Wv = P.sbuf("s_Wv", [128, 16, 384], BF16)
            with contextlib.ExitStack() as st2:
                P.stack = st2
                mub = P.sbuf("s_mub", [128, 1536], F32)
                P.dma('sp', mub.t[:], self.prow[l][:, PR_MU:PR_MU + 1536].partition_broadcast(128), writes=[mub])
                colt = [RR, RR + 128, RR + 256, RK, RK + 128, RK + 256, RDD, RAD, RGD]
                for i, c0 in enumerate(colt):
                    P.dma('pool', W.t[:, i, 0:8, :], self.w_in[l][:, c0:c0 + 128].rearrange("(k p) c -> p k c", p=128), writes=[W])
                    m0 = c0 - RR
                    O.TT('dve', W.t[:, i, 8:16, :], W.t[:, i, 0:8, :], mub.t[:, m0:m0 + 128][:, None, :].to_broadcast([128, 8, 128]), ALU.mult,
                         [W, mub], [W])
                P.dma('pool', Wv.t[:, 0:8, :], self.w_in[l][:, RV:RV + 384].rearrange("(k p) c -> p k c", p=128), writes=[Wv])
                O.TT('dve', Wv.t[:, 8:16, :], Wv.t[:, 0:8, :], mub.t[:, RV - RR:RV - RR + 384][:, None, :].to_broadcast([128, 8, 384]), ALU.mult,
                     [Wv, mub], [Wv])
                P.barrier()
            P.stack = st
            wd2 = P.sbuf("s_wd2", [128, 384], BF16)
            wa2 = P.sbuf("s_wa2", [128, 384], BF16)
            wg2 = P.sbuf("s_wg2", [128, 384], BF16)
            P.dma('pool', wd2.t[:], self.wd2[l], writes=[wd2])
            P.dma('pool', wa2.t[:], self.wa2[l], writes=[wa2])
            P.dma('pool', wg2.t[:], self.wg2[l], writes=[wg2])
            bones = cst.t[:, C_BONES:C_BONES + 128]
        else:
            W = P.sbuf("s_Wg", [128, 8, 4, 384], BF16)
            Wdn = P.sbuf("s_Wdn", [128, 8, 48], BF16)
            Wv = P.sbuf("s_Wv", [128, 8, 384], BF16)
            Wgt = P.sbuf("s_Wgt", [128, 8, 384], BF16)
            gup = P.sbuf("s_gup", [48, 384], BF16)
            P.op('pool', lambda e: e.memset(W.t[:], 0.0), writes=[W])
            P.op('pool', lambda e: e.memset(Wdn.t[:], 0.0), writes=[Wdn])
            P.op('pool', lambda e: e.memset(gup.t[:], 0.0), writes=[gup])
            with contextlib.ExitStack() as st2:
                P.stack = st2
                wqk = P.sbuf("s_wqk", [128, 8, 384], BF16)
                P.dma('pool', wqk.t[:], self.w_in[l][:, GQ:GQ + 384].rearrange("(k p) c -> p k c", p=128), writes=[wqk])
                for qi in range(2):
                    src = wqk.t[:, :, qi * 192:(qi + 1) * 192].rearrange("p k (h c) -> p k h c", h=6)
                    dst = W.t[:, :, 2 * qi, :].rearrange("p k (h c) -> p k h c", h=6)[:, :, :, 0:32]
                    dstp = W.t[:, :, 2 * qi + 1, :].rearrange("p k (h c) -> p k h c", h=6)
                    for kk in range(8):
                        O.CP('dve', dst[:, kk], src[:, kk], [wqk], [W])
                        O.TS('dve', dstp[:, kk, :, 0:32:2], src[:, kk, :, 1:32:2], -1.0, None, ALU.mult, None, [wqk], [W])
                        O.CP('dve', dstp[:, kk, :, 1:32:2], src[:, kk, :, 0:32:2], [wqk], [W])
                P.barrier()
            P.stack = st
            P.dma('pool', Wdn.t[:, :, 0:16], self.w_in[l][:, GDN:GDN + 16].rearrange("(k p) c -> p k c", p=128), writes=[Wdn])
            P.dma('pool', Wdn.t[:, :, 32:48], self.w_in[l][:, GDN + 16:GDN + 32].rearrange("(k p) c -> p k c", p=128), writes=[Wdn])
            P.dma('pool', Wv.t[:], self.w_in[l][:, GV:GV + 384].rearrange("(k p) c -> p k c", p=128), writes=[Wv])
            P.dma('pool', Wgt.t[:], self.w_in[l][:, GG:GG + 384].rearrange("(k p) c -> p k c", p=128), writes=[Wgt])
            P.dma('pool', gup.t[0:16, :], self.gup[l, 0], writes=[gup])
            P.dma('pool', gup.t[32:48, :], self.gup[l, 1], writes=[gup])
            ropeC = P.sbuf("s_ropeC", [128, 256], F32)
            ropeS = P.sbuf("s_ropeS", [128, 256], F32)
        if Prog.limit is not None:
            print("MS weights", P.n_ops)
        KR = [P.sbuf("s_KR%d" % p, [128, 2, 2, 128], BF16) for p in range(3)]
        Kg = [P.sbuf("s_Kg%d" % p, [128, 256], BF16) for p in range(3)]
        Kgp = [P.sbuf("s_Kgp%d" % p, [128, 256], F32) for p in range(3)]
        if rw:
            Bg = [P.sbuf("s_Bg%d" % p, [128, 256], BF16) for p in range(3)]
            Bgp = [P.sbuf("s_Bgp%d" % p, [128, 256], F32) for p in range(3)]
        else:
            for p in range(3):
                P.op('pool', lambda e, p=p: e.memset(KR[p].t[:], 0.0), writes=[KR[p]])
        GC = P.sbuf("s_GC", [128, 3, 2], F32)
        V = P.sbuf("s_V", [128, 2, 384], BF16)
        Vf = P.sbuf("s_Vf", [128, 2, 384], F32)
        ft = {n: P.sbuf("s_f_" + n, [128, 256], F32) for n in
              (["rf", "kf", "sg", "css", "g", "gi", "gex", "gp", "a", "kk", "t1", "kmod", "kka", "t2"] if rw else
               ["sg", "css", "g", "gi", "gp", "t1", "t2", "qr", "kr"])}
        if rw:
            tdd = P.sbuf("s_tdd", [128, 256], BF16)
            adb = P.sbuf("s_adb", [128, 256], BF16)
            sgd = P.sbuf("s_sgd", [128, 256], BF16)
            prodb = P.sbuf("s_prodb", [128, 256], BF16)
        else:
            dnb = P.sbuf("s_dnb", [48, 256], BF16)
        LMC = P.sbuf("s_LMC", [128, 6, 2, 256], BF16)
        Lm = P.sbuf("s_L", [128, 6, 128], BF16)
        PTs = [P.sbuf("s_PT%d" % i, [128, 6, 128], BF16) for i in range(2)]
        Psq = [P.sbuf("s_Pq%d" % i, [128, 6, 128], BF16) for i in range(2)]
        Ub = P.sbuf("s_Ub", [128, 384], BF16)
        BKT = P.sbuf("s_BKT", [128, 6, 128], BF16)
        Tw = P.sbuf("s_Tw", [128, 3, 128], F32)
        Twb = P.sbuf("s_Twb", [128, 3, 128], BF16)
        ttmp = P.sbuf("s_ttmp", [128, 3, 128], F32)
        o_sbs = [P.sbuf("s_o%d" % i, [128, 384], F32) for i in range(2)]
        y_sb = P.sbuf("s_y", [128, 384], F32)
        sq_sb = P.sbuf("s_sq", [128, 384], F32)
        gt_sb = P.sbuf("s_gt", [128, 384], F32)
        yb = P.sbuf("s_yb", [128, 384], F32)
        stt = P.sbuf("s_stat", [128, 8, 8], F32)
        yTo = P.sbuf("s_yTo", [128, 3, 256], BF16)
        nrm = prw.t[:, 0:384]
        gnw = prw.t[:, 384:768]
        gnb = prw.t[:, 768:1152]
        bdm = cst.t[:, C_BD:C_BD + 128]
        ident_b = self.identb
        feat0 = 5 * 128 if rw else 0

        for d in range(2):
            fwd = (d == 0)
            msk = mskF if fwd else mskB
            lm = lmF if fwd else lmB
            scm = cst.t[:, C_SCF:C_SCF + 256] if fwd else cst.t[:, C_SCB:C_SCB + 256]
            cend = 127 if fwd else 0
            O.P.op('pool', lambda e: e.memset(Tw.t[:], 0.0), writes=[Tw])
            O.P.op('pool', lambda e: e.memset(Twb.t[:], 0.0), writes=[Twb])
            border = list(range(NBLK)) if fwd else [0] + list(range(NBLK - 1, 0, -1))
            for bi in border:
                t0 = bi * CB
                lat = bi > 0
                skip_o = (l == self.depth - 1) and bi == 0

                def rhs(k):
                    return xT.t[:, k, t0:t0 + CB] if k < 8 else dxT.t[:, k - 8, t0:t0 + CB]

                def lhs_tok(k, c):
                    return xT.t[:, k, t0 + c * 128:t0 + (c + 1) * 128] if k < 8 else dxT.t[:, k - 8, t0 + c * 128:t0 + (c + 1) * 128]
                xr = [xT, dxT] if rw else [xT]
                for c in range(2):
                    ps = PS[c]
                    for k in range(NK):
                        O.MM(ps.t[:, 0:384], lhs_tok(k, c), Wv.t[:, k, :], k == 0, k == NK - 1, xr + [Wv], [ps])
                    O.CP('act', V.t[:, c, :], ps.t[:, 0:384], [ps], [V])
                if rw:
                    def projF(i, ps, half):
                        o = ps.t[:, half * 256:(half + 1) * 256]
                        for k in range(16):
                            O.MM(o, W.t[:, i, k, :], rhs(k), k == 0, k == 15, xr + [W], [ps])
                        return o
                    o_dd = projF(6, PS[2], 0)
                    O.ACT(tdd.t[:], o_dd, AF.Tanh, [PS[2]], [tdd])
                    o_ad = projF(7, PS[2], 1)
                    O.CP('dve', adb.t[:], o_ad, [PS[2]], [adb])
                    if not fwd:
                        o_gd = projF(8, PS[3], 0)
                        O.ACT(sgd.t[:], o_gd, AF.Sigmoid, [PS[3]], [sgd])
                    for p in range(3):
                        f = ft
                        o_r = projF(p, PS[4], 0)
                        O.CP('act', f["rf"].t[:], o_r, [PS[4]], [f["rf"]])
                        o_k = projF(3 + p, PS[4], 1)
                        O.CP('dve', f["kf"].t[:], o_k, [PS[4]], [f["kf"]])
                        o_d = PS[5].t[:, 0:256]
                        O.MM(o_d, wd2.t[d * 64:(d + 1) * 64, p * 128:(p + 1) * 128], tdd.t[d * 64:(d + 1) * 64, :], True, True, [wd2, tdd], [PS[5]])
                        O.ACT(f["sg"].t[:], o_d, AF.Sigmoid, [PS[5], pc], [f["sg"]], bias=pc.t[:, PC_W0 + d * 3 + p:PC_W0 + d * 3 + p + 1])
                        o_a = PS[5].t[:, 256:512]
                        O.MM(o_a, wa2.t[d * 64:(d + 1) * 64, p * 128:(p + 1) * 128], adb.t[d * 64:(d + 1) * 64, :], True, True, [wa2, adb], [PS[5]])
                        O.ACT(f["a"].t[:], o_a, AF.Sigmoid, [PS[5], pc], [f["a"]], bias=pc.t[:, PC_A0 + d * 3 + p:PC_A0 + d * 3 + p + 1])
                        self._scan_gates(O, f, scm, fwd, cend, GC, p, DECAY)
                        O.TS('pool', f["kk"].t[:], f["kf"].t[:], pc.t[:, PC_KK + p:PC_KK + p + 1], None, ALU.mult, None, [f["kf"], pc], [f["kk"]])
                        O.TT('pool', f["t1"].t[:], f["kk"].t[:], f["kk"].t[:], ALU.mult, [f["kk"]], [f["t1"]])
                        o_ss = PS[6].t[:, 0:256]
                        O.MM(o_ss, bones, f["t1"].t[:], True, True, [cst, f["t1"]], [PS[6]])
                        O.ACT(f["t2"].t[:], o_ss, AF.Sqrt, [PS[6], cst], [f["t2"]], bias=cst.t[:, C_E12:C_E12 + 1])
                        O.P.op('dve', lambda e: e.reciprocal(f["t1"].t[:], f["t2"].t[:]), [f["t2"]], [f["t1"]])
                        O.TT('pool', f["kk"].t[:], f["kk"].t[:], f["t1"].t[:], ALU.mult, [f["kk"], f["t1"]], [f["kk"]])
                        O.TS('dve', f["t1"].t[:], f["a"].t[:], pc.t[:, PC_KA + p:PC_KA + p + 1], pc.t[:, NPC + p:NPC + p + 1], ALU.mult, ALU.add,
                             [f["a"], pc], [f["t1"]])
                        O.TT('pool', f["kmod"].t[:], f["kf"].t[:], f["t1"].t[:], ALU.mult, [f["kf"], f["t1"]], [f["kmod"]])
                        O.TT('pool', f["kka"].t[:], f["kk"].t[:], f["a"].t[:], ALU.mult, [f["kk"], f["a"]], [f["kka"]])
                        O.TT('dve', KR[p].t[:, :, 0, :], f["kk"].t[:].rearrange("p (c t) -> p c t", c=2), f["gex"].t[:].rearrange("p (c t) -> p c t", c=2),
                             ALU.mult, [f["kk"], f["gex"]], [KR[p]])
                        O.TT('pool', KR[p].t[:, :, 1, :], f["rf"].t[:].rearrange("p (c t) -> p c t", c=2), f["g"].t[:].rearrange("p (c t) -> p c t", c=2),
                             ALU.mult, [f["rf"], f["g"]], [KR[p]])
                        O.TT('dve', Kg[p].t[:], f["kmod"].t[:], f["gi"].t[:], ALU.mult, [f["kmod"], f["gi"]], [Kg[p]])
                        O.TT('pool', Kgp[p].t[:], f["kmod"].t[:], f["gp"].t[:], ALU.mult, [f["kmod"], f["gp"]], [Kgp[p]])
                        O.STT(Bg[p].t[:], f["kka"].t[:], -1.0, f["gi"].t[:], ALU.mult, ALU.mult, [f["kka"], f["gi"]], [Bg[p]])
                        O.STT(Bgp[p].t[:], f["kka"].t[:], -1.0, f["gp"].t[:], ALU.mult, ALU.mult, [f["kka"], f["gp"]], [Bgp[p]])
                        O.TT('pool', f["t1"].t[:], f["rf"].t[:], f["kmod"].t[:], ALU.mult, [f["rf"], f["kmod"]], [f["t1"]])
                        O.TS('dve', prodb.t[:], f["t1"].t[:], pc.t[:, PC_RK + p:PC_RK + p + 1], None, ALU.mult, None, [f["t1"], pc], [prodb])
                        for c in range(2):
                            O.MM(PS[7].t[:, c * 8 + p * 2:c * 8 + p * 2 + 2], prodb.t[:, c * 128:(c + 1) * 128], hselb.t[:], True, True,
                                 [prodb, hselb], [PS[7]])
                    for c in range(2):
                        ch = bi * 2 + c
                        if fwd:
                            O.CP('dve', COEF.t[:, ch, 0:6], PS[7].t[:, c * 8:c * 8 + 6], [PS[7]], [COEF])
                        else:
                            O.TT('dve', COEF.t[:, ch, 0:6], PS[7].t[:, c * 8:c * 8 + 6], COEF.t[:, ch, 0:6], ALU.add, [PS[7], COEF], [COEF])
                else:
                    f = ft
                    o_dn = PS[2].t[0:48, 0:256]
                    for k in range(8):
                        O.MM(o_dn, Wdn.t[:, k, :], rhs(k), k == 0, k == 7, [xT, Wdn], [PS[2]])
                    O.CP('dve', dnb.t[:], o_dn, [PS[2]], [dnb])
                    if lat:
                        lt0 = t0 - NCTX
                        P.dma('sp', ropeC.t[:], self.ropeC[:, lt0:lt0 + CB], writes=[ropeC])
                        P.dma('sp', ropeS.t[:], self.ropeS[:, lt0:lt0 + CB], writes=[ropeS])
                    for p in range(3):
                        o_z = PS[3].t[:, 0:256]
                        O.MM(o_z, gup.t[d * 32:d * 32 + 16, p * 128:(p + 1) * 128], dnb.t[d * 32:d * 32 + 16, :], True, True, [gup, dnb], [PS[3]])
                        O.ACT(f["t1"].t[:], o_z, AF.Exp, [PS[3], pc], [f["t1"]], scale=-1.0, bias=pc.t[:, NPC + 3 + d * 3 + p:NPC + 3 + d * 3 + p + 1])
                        O.ACT(f["sg"].t[:], f["t1"].t[:], AF.Ln, [f["t1"], cst], [f["sg"]], bias=cst.t[:, C_ONE:C_ONE + 1])
                        self._scan_gates(O, f, scm, fwd, cend, GC, p, 1.0 / 16.0)
                        for qi, (dst, nm) in enumerate(((None, "qr"), (None, "kr"))):
                            def projq(j, half):
                                o = PS[4 + (half // 2)].t[:, (half % 2) * 256:(half % 2 + 1) * 256]
                                for k in range(8):
                                    O.MM(o, W.t[:, k, j, p * 128:(p + 1) * 128], rhs(k), k == 0, k == 7, [xT, W], [PS[4 + (half // 2)]])
                                return o, PS[4 + (half // 2)]
                            o1, b1 = projq(2 * qi, 2 * qi)
                            if lat:
                                o2, b2 = projq(2 * qi + 1, 2 * qi + 1)
                                O.TT('dve', f["t1"].t[:], o1, ropeC.t[:], ALU.mult, [b1, ropeC], [f["t1"]])
                                O.TT('dve', f["t2"].t[:], o2, ropeS.t[:], ALU.mult, [b2, ropeS], [f["t2"]])
                                O.TT('pool', f[nm].t[:], f["t1"].t[:], f["t2"].t[:], ALU.add, [f["t1"], f["t2"]], [f[nm]])
                            else:
                                O.CP('dve', f[nm].t[:], o1, [b1], [f[nm]])
                        O.STT(KR[p].t[:, :, 1, :], f["qr"].t[:].rearrange("p (c t) -> p c t", c=2), 32.0 ** -0.5,
                              f["g"].t[:].rearrange("p (c t) -> p c t", c=2), ALU.mult, ALU.mult, [f["qr"], f["g"]], [KR[p]])
                        O.TT('dve', Kg[p].t[:], f["kr"].t[:], f["gi"].t[:], ALU.mult, [f["kr"], f["gi"]], [Kg[p]])
                        O.TT('pool', Kgp[p].t[:], f["kr"].t[:], f["gp"].t[:], ALU.mult, [f["kr"], f["gp"]], [Kgp[p]])
                if Prog.limit is not None:
                    print("MS prep", d, bi, P.n_ops)
                for c in ([0, 1] if fwd else [1, 0]):
                    ch = bi * 2 + c
                    cs = slice(c * 128, (c + 1) * 128)
                    o_sb = o_sbs[ch % 2]
                    of_sb = of_sbs[ch % 2]
                    if not fwd and not skip_o:
                        P.dma('sp', of_sb.t[:], Of.t.ap()[ch], reads=[Of], writes=[of_sb])
                    mb = msk.t[:, None, :].to_broadcast([128, 2, 256])
                    for p in range(3):
                        for e2 in range(2):
                            h = 2 * p + e2
                            pb = e2 * 64
                            ps = PS[e2]
                            krr = KR[p].t[pb:pb + 64, c, :, :].rearrange("p a t -> p (a t)")
                            if rw:
                                O.MM(ps.t[:, 0:256], Bg[p].t[pb:pb + 64, cs], krr, True, True, [Bg[p], KR[p]], [ps])
                            O.MM(ps.t[:, 256:512], Kg[p].t[pb:pb + 64, cs], krr, True, True, [Kg[p], KR[p]], [ps])
                            if rw:
                                O.TT('dve', LMC.t[:, h, :, :], ps.t[:].rearrange("p (a t) -> p a t", a=2), mb, ALU.mult, [ps, msk], [LMC])
                            else:
                                O.TT('dve', LMC.t[:, h, 1, :], ps.t[:, 256:512], msk.t[:], ALU.mult, [ps, msk], [LMC])
                    if rw:
                        for e2 in range(2):
                            pb = e2 * 64
                            ps = PS[e2]
                            for pp in range(3):
                                O.MM(ps.t[:, pp * 128:(pp + 1) * 128], KR[pp].t[pb:pb + 64, c, 0, :], Bg[pp].t[pb:pb + 64, cs], True, True,
                                     [KR[pp], Bg[pp]], [ps])
                            O.TT('dve', Lm.t[:, e2:6:2, :], ps.t[:, 0:384].rearrange("p (a t) -> p a t", a=3),
                                 lm.t[:, None, :].to_broadcast([128, 3, 128]), ALU.mult, [ps, lm], [Lm])
                        first = True
                        for p in range(3):
                            O.MM(PS[2].t[:, p * 128:(p + 1) * 128], KR[p].t[:, c, 0, :], Twb.t[:, p, :], first, False, [KR[p], Twb], [PS[2]])
                            first = False
                        for h in range(6):
                            O.MM(PS[2].t[:, h * 64:(h + 1) * 64], LMC.t[:, h, 1, 0:128], V.t[:, c, h * 64:(h + 1) * 64], False, False, [LMC, V], [PS[2]])
                        O.CP('act', Ub.t[:], PS[2].t[:, 0:384], [PS[2]], [Ub])
                        for lv in range(7):
                            def PT(h):
                                return LMC.t[:, h, 0, 0:128] if lv == 0 else PTs[lv % 2].t[:, h, :]

                            def PP(h):
                                return Lm.t[:, h, :] if lv == 0 else Psq[lv % 2].t[:, h, :]
                            ptb = LMC if lv == 0 else PTs[lv % 2]
                            ppb = Lm if lv == 0 else Psq[lv % 2]
                            if lv < 6:
                                for hg in range(2):
                                    ps = PS[4 + hg]
                                    for hh in range(3):
                                        h = hg * 3 + hh
                                        O.MM(ps.t[:, hh * 128:(hh + 1) * 128], PP(h), PT(h), True, True, [ptb, ppb], [ps])
                                    O.CP('dve', PTs[(lv + 1) % 2].t[:, hg * 3:hg * 3 + 3, :], ps.t[:, 0:384].rearrange("p (a t) -> p a t", a=3),
                                         [ps], [PTs[(lv + 1) % 2]])
                                if lv < 5:
                                    for hg in range(2):
                                        ps = PS[6 + hg]
                                        for hh in range(3):
                                            h = hg * 3 + hh
                                            O.MM(ps.t[:, hh * 128:(hh + 1) * 128], PT(h), PP(h), True, True, [ptb, ppb], [ps])
                                        O.CP('dve' if hg == 0 else 'act', Psq[(lv + 1) % 2].t[:, hg * 3:hg * 3 + 3, :],
                                             ps.t[:, 0:384].rearrange("p (a t) -> p a t", a=3), [ps], [Psq[(lv + 1) % 2]])
                            for h in range(6):
                                O.MM(PS[2].t[:, h * 64:(h + 1) * 64], PT(h), Ub.t[:, h * 64:(h + 1) * 64], False, lv == 6, [ptb, Ub], [PS[2]])
                            O.CP('act', Ub.t[:], PS[2].t[:, 0:384], [PS[2]], [Ub])
                    if not skip_o:
                        first = True
                        for p in range(3):
                            O.MM(PS[3].t[:, p * 128:(p + 1) * 128], KR[p].t[:, c, 1, :], Twb.t[:, p, :], first, False, [KR[p], Twb], [PS[3]])
                            first = False
                        for h in range(6):
                            if rw:
                                O.MM(PS[3].t[:, h * 64:(h + 1) * 64], LMC.t[:, h, 0, 128:256], Ub.t[:, h * 64:(h + 1) * 64], False, False, [LMC, Ub], [PS[3]])
                            O.MM(PS[3].t[:, h * 64:(h + 1) * 64], LMC.t[:, h, 1, 128:256], V.t[:, c, h * 64:(h + 1) * 64], False, h == 5, [LMC, V], [PS[3]])
                        if fwd:
                            O.CP('dve', o_sb.t[:], PS[3].t[:, 0:384], [PS[3]], [o_sb])
                            P.dma('sp', Of.t.ap()[ch], o_sb.t[:], reads=[o_sb], writes=[Of])
                        else:
                            O.TT('dve', o_sb.t[:], PS[3].t[:, 0:384], of_sb.t[:], ALU.add, [PS[3], of_sb], [o_sb])
                    idf = cst.t[:, C_IDENT:C_IDENT + 128]
                    for p in range(3):
                        if rw:
                            O.TR(PS[6].t[:, p * 128:(p + 1) * 128], Bgp[p].t[:, cs], idf, [Bgp[p], cst], [PS[6]])
                        O.TR(PS[0].t[:, p * 128:(p + 1) * 128], Kgp[p].t[:, cs], idf, [Kgp[p], cst], [PS[0]])
                    if rw:
                        O.CP('dve', BKT.t[:, 0:3, :], PS[6].t[:, 0:384].rearrange("p (a t) -> p a t", a=3), [PS[6]], [BKT])
                    O.CP('act', BKT.t[:, 3:6, :], PS[0].t[:, 0:384].rearrange("p (a t) -> p a t", a=3), [PS[0]], [BKT])
                    first = True
                    for p in range(3):
                        if rw:
                            O.MM(PS[7].t[:, p * 128:(p + 1) * 128], BKT.t[:, p, :], Ub.t[:, p * 128:(p + 1) * 128], first, False, [BKT, Ub], [PS[7]])
                            first = False
                        O.MM(PS[7].t[:, p * 128:(p + 1) * 128], BKT.t[:, 3 + p, :], V.t[:, c, p * 128:(p + 1) * 128], first, p == 2, [BKT, V], [PS[7]])
                        first = False
                    O.TT('dve', ttmp.t[:], PS[7].t[:, 0:384].rearrange("p (a t) -> p a t", a=3), bdm[:, None, :].to_broadcast([128, 3, 128]), ALU.mult,
                         [PS[7], cst], [ttmp])
                    for p in range(3):
                        O.STT(Tw.t[:, p, :], Tw.t[:, p, :], GC.t[:, p, c:c + 1], ttmp.t[:, p, :], ALU.mult, ALU.add, [Tw, GC, ttmp], [Tw])
                    O.CP('act', Twb.t[:], Tw.t[:], [Tw], [Twb])
                    if Prog.limit is not None:
                        print("MS chunk", d, bi, c, P.n_ops)
                    if not fwd and not skip_o:
                        o3 = o_sb.t[:].rearrange("p (h v) -> p h v", h=6)
                        if rw:
                            O.P.op('dve', lambda e, o3=o3: e.tensor_reduce(stt.t[:, 0, 0:6], o3, AX.X, ALU.add), [o_sb], [stt])
                        O.TT('pool', sq_sb.t[:], o_sb.t[:], o_sb.t[:], ALU.mult, [o_sb], [sq_sb])
                        O.P.op('dve', lambda e: e.tensor_reduce(stt.t[:, 1, 0:6], sq_sb.t[:].rearrange("p (h v) -> p h v", h=6), AX.X, ALU.add), [sq_sb], [stt])
                        if rw:
                            O.TS('dve', stt.t[:, 2, 0:6], stt.t[:, 0, 0:6], 1.0 / 64, None, ALU.mult, None, [stt], [stt])
                            O.TT('dve', stt.t[:, 3, 0:6], stt.t[:, 2, 0:6], stt.t[:, 2, 0:6], ALU.mult, [stt], [stt])
                            O.STT(stt.t[:, 4, 0:6], stt.t[:, 1, 0:6], 1.0 / 64, stt.t[:, 3, 0:6], ALU.mult, ALU.subtract, [stt], [stt])
                            O.TS('dve', stt.t[:, 4, 0:6], stt.t[:, 4, 0:6], RW_GN_EPS, None, ALU.add, None, [stt], [stt])
                        else:
                            O.TS('dve', stt.t[:, 4, 0:6], stt.t[:, 1, 0:6], 1.0 / 64, LN_EPS, ALU.mult, ALU.add, [stt], [stt])
                        O.ACT(stt.t[:, 5, 0:6], stt.t[:, 4, 0:6], AF.Sqrt, [stt], [stt])
                        O.P.op('dve', lambda e: e.reciprocal(stt.t[:, 6, 0:6], stt.t[:, 5, 0:6]), [stt], [stt])
                        y3 = y_sb.t[:].rearrange("p (h v) -> p h v", h=6)
                        rstd_b = stt.t[:, 6, 0:6][:, :, None].to_broadcast([128, 6, 64])
                        psg = PS[0]
                        if rw:
                            O.MM(psg.t[:, 0:384], sgd.t[:, cs], wg2.t[:], True, True, [sgd, wg2], [psg])
                            mean_b = stt.t[:, 2, 0:6][:, :, None].to_broadcast([128, 6, 64])
                            O.TT('dve', y3, o3, mean_b, ALU.subtract, [o_sb, stt], [y_sb])
                            O.TT('dve', y3, y3, rstd_b, ALU.mult, [y_sb, stt], [y_sb])
                            O.TT('pool', y_sb.t[:], y_sb.t[:], gnw, ALU.mult, [y_sb, prw], [y_sb])
                            O.TT('pool', y_sb.t[:], y_sb.t[:], gnb, ALU.add, [y_sb, prw], [y_sb])
                            coef_b = COEF.t[:, ch, 0:6][:, :, None].to_broadcast([128, 6, 64])
                            O.TT('dve', sq_sb.t[:].rearrange("p (h v) -> p h v", h=6), V.t[:, c, :].rearrange("p (h v) -> p h v", h=6), coef_b, ALU.mult,
                                 [V, COEF], [sq_sb])
                            O.TT('pool', y_sb.t[:], y_sb.t[:], sq_sb.t[:], ALU.add, [y_sb, sq_sb], [y_sb])
                            O.TT('dve', yb.t[:], psg.t[:, 0:384], y_sb.t[:], ALU.mult, [y_sb, psg], [yb])
                        else:
                            for k in range(8):
                                O.MM(psg.t[:, 0:384], lhs_tok(k, c), Wgt.t[:, k, :], k == 0, k == 7, [xT, Wgt], [psg])
                            O.ACT(gt_sb.t[:], psg.t[:, 0:384], AF.Silu, [psg], [gt_sb])
                            O.TT('dve', y3, o3, rstd_b, ALU.mult, [o_sb, stt], [y_sb])
                            O.TT('pool', y_sb.t[:], y_sb.t[:], nrm, ALU.mult, [y_sb, prw], [y_sb])
                            O.TT('pool', yb.t[:], y_sb.t[:], gt_sb.t[:], ALU.mult, [y_sb, gt_sb], [yb])
                        for p in range(3):
                            O.TR(PS[1].t[:, p * 128:(p + 1) * 128], yb.t[:, p * 128:(p + 1) * 128], cst.t[:, C_IDENT:C_IDENT + 128], [yb, cst], [PS[1]])
                        O.CP('act', yTo.t[:, :, cs], PS[1].t[:, 0:384].rearrange("p (a t) -> p a t", a=3), [PS[1]], [yTo])
                if not fwd and not skip_o:
                    for p in range(3):
                        P.dma('sp', self.YT.t.ap()[feat0 + p * 128:feat0 + (p + 1) * 128, t0:t0 + CB], yTo.t[:, p, :], reads=[yTo], writes=[self.YT])
        P.barrier()
        P.stack = old


def _scan_gates(self, O, f, scm, fwd, cend, GC, p, rate):
    sg, css = f["sg"], f["css"]
    if fwd:
        O.P.op('dve', lambda e: e.tensor_tensor_scan(css.t[:], scm, sg.t[:], 0.0, ALU.mult, ALU.add), [sg, self.cst_sb], [css])
    else:
        O.P.op('dve', lambda e: e.tensor_tensor_scan(css.t[:, ::-1], scm[:, ::-1], sg.t[:, ::-1], 0.0, ALU.mult, ALU.add), [sg, self.cst_sb], [css])
    O.ACT(f["g"].t[:], css.t[:], AF.Exp, [css], [f["g"]], scale=-rate)
    O.ACT(f["gi"].t[:], css.t[:], AF.Exp, [css], [f["gi"]], scale=rate)
    if "gex" in f:
        O.TT('pool', f["t2"].t[:], css.t[:], sg.t[:], ALU.subtract, [css, sg], [f["t2"]])
        O.ACT(f["gex"].t[:], f["t2"].t[:], AF.Exp, [f["t2"]], [f["gex"]], scale=-rate)
    c3 = css.t[:].rearrange("p (c t) -> p c t", c=2)
    cC = c3[:, :, cend:cend + 1]
    O.TT('dve', f["t2"].t[:].rearrange("p (c t) -> p c t", c=2), cC.to_broadcast([128, 2, 128]), c3, ALU.subtract, [css], [f["t2"]])
    O.ACT(f["gp"].t[:], f["t2"].t[:], AF.Exp, [f["t2"]], [f["gp"]], scale=-rate)
    O.ACT(GC.t[:, p, :], css.t[:, cend:256:128], AF.Exp, [css], [GC], scale=-rate)


Kern.ph_scan = _ph_scan
Kern._scan_gates = _scan_gates


def build_full(nb=NB):
    K = Kern(nb=nb)
    P = K.P
    for l in range(DEPTH):
        K.set_layer(l)
        K.ph_mod(l)
    for b in range(nb):
        for l in range(DEPTH):
            K.set_layer(l)
            P.barrier()
            with contextlib.ExitStack() as st:
                P.stack, old = st, P.stack
                xT, dxT = K.ph_x(b, l, need_dx=True)
                K.ph_nat(b, l, xT)
                K.ph_scan(b, l, xT, dxT, 'gla')
                K.ph_scan(b, l, xT, dxT, 'rw')
                P.barrier()
                P.stack = old
            K.ph_wout_ln1(b, l)
            K.ph_ffn_ln2(b, l)
    P.finish()
    return K


_CACHE = {}


def kernel(**inputs):
    n_cores = 8
    if "K" not in _CACHE:
        _CACHE["K"] = build_full(NB)
    K = _CACHE["K"]
    in_maps = [host_inputs(inputs, c, NB) for c in range(n_cores)]
    res = run_bass_kernel_spmd(K.nc, in_maps, core_ids=list(range(n_cores)))
    out = np.concatenate([np.asarray(r["y"]) for r in res.results], axis=0)
    return out.astype(np.float32)
```
